# Optimizing a Trainium2 kernel written in Bass

```python
import math, functools
import jax, jax.numpy as jnp
from jax import lax
import numpy as np

D_MODEL = 2048
BATCH = 4
SEQ = 2048
DEPTH = 1
DEC_BATCH = 32
DEC_SEQ = 1
PAST_LEN = 8192
PAGE_SIZE = 128

HEAD_DIM = 64
C_MIX = D_MODEL
N_ATT_HEADS = C_MIX // (2 * HEAD_DIM)
N_RW_HEADS = C_MIX // (2 * HEAD_DIM)
C_ATT = N_ATT_HEADS * HEAD_DIM
C_RW = N_RW_HEADS * HEAD_DIM
DIL_WINDOWS = (128, 512, 2048)
DIL_RATES = (1, 4, 16)
MAX_WINDOW = max(DIL_WINDOWS)
Q_BLOCK = 128
ATT_SCALE = HEAD_DIM ** -0.5
W_LORA = 64
A_LORA = 64
G_LORA = 160
C_SHIFT = 3 * C_RW + W_LORA + A_LORA + G_LORA
C_IN = 3 * C_ATT + C_SHIFT
D_FF = 256 * ((8 * D_MODEL // 3 + 255) // 256)
CONV_W = 3
NORM_EPS = 1e-6
LNX_EPS = HEAD_DIM * 1e-5

kernel_name = "hybrid_dilated_attn_rwkv7_convffn_step"


def rms_norm(x, g, eps=NORM_EPS):
    xf = x.astype(jnp.float32)
    y = xf * lax.rsqrt(jnp.mean(xf * xf, axis=-1, keepdims=True) + eps)
    return (y * g.astype(jnp.float32)).astype(x.dtype)


def _softmax_av(s, vals, spec):
    m = jnp.max(s, axis=-1, keepdims=True)
    e = jnp.exp(s - m)
    den = jnp.sum(e, axis=-1, keepdims=True)
    o = jnp.einsum(spec, e / den, vals.astype(jnp.float32))
    return o, (m + jnp.log(den))[..., 0]


def _mix_branches(outs, lses):
    wts = jax.nn.softmax(jnp.stack(lses, 0), axis=0)
    return jnp.einsum('nbth,nbthe->bthe', wts, jnp.stack(outs, 0))


def _dilated_branch_prompt(q, k, v, rate, n_back):
    B, S, H, E = q.shape
    L = S // rate
    qb = math.gcd(L, Q_BLOCK)
    nblk = L // qb
    qr = q.reshape(B, nblk, qb, rate, H, E)
    pad = ((0, 0), (n_back, 0), (0, 0), (0, 0), (0, 0))
    kp = jnp.pad(k.reshape(B, L, rate, H, E), pad)
    vp = jnp.pad(v.reshape(B, L, rate, H, E), pad)
    idx = jnp.arange(nblk)[:, None] * qb + jnp.arange(qb + n_back)[None, :]
    kb = kp[:, idx]
    vb = vp[:, idx]
    s = jnp.einsum('bnqrhe,bnkrhe->bnrhqk', qr, kb, preferred_element_type=jnp.float32) * ATT_SCALE
    i = jnp.arange(qb)[:, None]
    c = jnp.arange(qb + n_back)[None, :]
    blk = jnp.arange(nblk)[:, None, None]
    mask = (c >= i) & (c <= i + n_back) & (blk * qb + c >= n_back)
    s = jnp.where(mask[None, :, None, None], s, -jnp.inf)
    o, lse = _softmax_av(s, vb, 'bnrhqk,bnkrhe->bnqrhe')
    o = o.reshape(B, S, H, E)
    lse = lse.transpose(0, 1, 4, 2, 3).reshape(B, S, H)
    return o, lse


def dilated_attention_prompt(q, k, v):
    outs, lses = [], []
    for win, rate in zip(DIL_WINDOWS, DIL_RATES):
        o, lse = _dilated_branch_prompt(q, k, v, rate, win // rate)
        outs.append(o)
        lses.append(lse)
    return _mix_branches(outs, lses).astype(q.dtype)


def dilated_attention_sample(q, k, v, k_cache, v_cache):
    T = q.shape[1]
    W = k_cache.shape[1]
    k_all = jnp.concatenate([k_cache.astype(k.dtype), k], axis=1)
    v_all = jnp.concatenate([v_cache.astype(v.dtype), v], axis=1)
    outs, lses = [], []
    for win, rate in zip(DIL_WINDOWS, DIL_RATES):
        n_back = win // rate
        idx = W + jnp.arange(T)[:, None] - rate * jnp.arange(n_back + 1)[None, :]
        valid = idx >= 0
        idx = jnp.maximum(idx, 0)
        kg = k_all[:, idx]
        vg = v_all[:, idx]
        s = jnp.einsum('bthe,btkhe->bhtk', q, kg, preferred_element_type=jnp.float32) * ATT_SCALE
        s = jnp.where(valid[None, None], s, -jnp.inf)
        o, lse = _softmax_av(s, vg, 'bhtk,btkhe->bthe')
        outs.append(o)
        lses.append(lse.transpose(0, 2, 1))
    return _mix_branches(outs, lses).astype(q.dtype)


def rwkv7_group(cols, prev_cols, wkv0, rw_mu, rw_w0, rw_w_up, rw_a0, rw_a_up, rw_g_up,
                rw_k_k, rw_k_a, rw_r_k, rw_lnx_w, rw_lnx_b):
    f32 = jnp.float32
    B, T, _ = cols.shape
    H, E = N_RW_HEADS, HEAD_DIM
    shifted = jnp.concatenate([prev_cols.astype(cols.dtype), cols[:, :-1]], axis=1)
    xs = cols + rw_mu * (shifted - cols)
    r, k, v, wd, ad, gd = jnp.split(
        xs, [C_RW, 2 * C_RW, 3 * C_RW, 3 * C_RW + W_LORA, 3 * C_RW + W_LORA + A_LORA], axis=-1)
    w = -jax.nn.softplus(-(rw_w0 + jnp.tanh(wd) @ rw_w_up).astype(f32)) - 0.5
    decay = jnp.exp(-jnp.exp(w)).reshape(B, T, H, E)
    a = jax.nn.sigmoid((rw_a0 + ad @ rw_a_up).astype(f32)).reshape(B, T, H, E)
    g = jax.nn.sigmoid(gd) @ rw_g_up
    hs = lambda t: t.astype(f32).reshape(B, T, H, E)
    kh, rh, vh = hs(k), hs(r), hs(v)
    kk = kh * rw_k_k.astype(f32).reshape(H, E)
    kk = kk / jnp.maximum(jnp.linalg.norm(kk, axis=-1, keepdims=True), 1e-12)
    k_eff = kh * (1.0 + (a - 1.0) * rw_k_a.astype(f32).reshape(H, E))
    a_vec = -kk
    b_vec = kk * a

    def step(S, inp):
        rt, wt, kt, vt, at, bt = inp
        sa = jnp.einsum('bhij,bhj->bhi', S, at)
        S = S * wt[:, :, None, :] + sa[..., None] * bt[:, :, None, :] + vt[..., None] * kt[:, :, None, :]
        return S, jnp.einsum('bhij,bhj->bhi', S, rt)

    tm = lambda t: jnp.moveaxis(t, 1, 0)
    S_T, ys = lax.scan(step, wkv0.astype(f32), (tm(rh), tm(decay), tm(k_eff), tm(vh), tm(a_vec), tm(b_vec)))
    y = jnp.moveaxis(ys, 0, 1)
    mean = jnp.mean(y, axis=-1, keepdims=True)
    var = jnp.mean(jnp.square(y - mean), axis=-1, keepdims=True)
    yn = (y - mean) * lax.rsqrt(var + LNX_EPS) * rw_lnx_w.astype(f32).reshape(H, E) + rw_lnx_b.astype(f32).reshape(H, E)
    bonus = jnp.sum(rh * k_eff * rw_r_k.astype(f32).reshape(H, E), axis=-1, keepdims=True) * vh
    out = (yn + bonus).reshape(B, T, C_RW).astype(cols.dtype) * g
    return out, cols[:, -1:], S_T


def decoder_layer(x, attend, rw_prev, wkv0, ffn_prev, norm_mix_g, w_in, att_out_g, rw_mu, rw_w0, rw_w_up,
                  rw_a0, rw_a_up, rw_g_up, rw_k_k, rw_k_a, rw_r_k, rw_lnx_w, rw_lnx_b, w_o,
                  norm_ffn_g, ffn_w_up, ffn_conv_w, ffn_conv_b, ffn_w_down):
    B, T, _ = x.shape
    H, E = N_ATT_HEADS, HEAD_DIM
    h = rms_norm(x, norm_mix_g)
    p = h @ w_in
    q = p[..., :C_ATT].reshape(B, T, H, E)
    k = p[..., C_ATT:2 * C_ATT].reshape(B, T, H, E)
    v = p[..., 2 * C_ATT:3 * C_ATT].reshape(B, T, H, E)
    o_att = attend(q, k, v)
    o_att = rms_norm(o_att, att_out_g.reshape(H, E)).reshape(B, T, C_ATT)
    o_rw, rw_last, wkv_T = rwkv7_group(p[..., 3 * C_ATT:], rw_prev, wkv0, rw_mu, rw_w0, rw_w_up, rw_a0,
                                       rw_a_up, rw_g_up, rw_k_k, rw_k_a, rw_r_k, rw_lnx_w, rw_lnx_b)
    x = x + jnp.concatenate([o_att, o_rw], axis=-1) @ w_o
    h2 = rms_norm(x, norm_ffn_g)
    u = h2 @ ffn_w_up
    up = jnp.concatenate([ffn_prev.astype(u.dtype), u], axis=1)
    c = ffn_conv_b + sum(ffn_conv_w[j] * up[:, j:j + T] for j in range(CONV_W))
    gate, val = jnp.split(c, 2, axis=-1)
    x = x + (jax.nn.silu(gate) * val) @ ffn_w_down
    return x, k, v, rw_last, wkv_T, up[:, -(CONV_W - 1):]


def setup_inputs(seed: int = 0) -> dict:
    key = jax.random.key(seed)
    ks = jax.random.split(key, 28)
    win_buf = min(MAX_WINDOW, PAST_LEN)
    nrm = lambda kk, shape, scale=1.0: scale * jax.random.normal(kk, shape, jnp.float32)
    L = DEPTH
    return {
        "x_prompt": nrm(ks[0], (BATCH, SEQ, D_MODEL)),
        "x_sample": nrm(ks[1], (DEC_BATCH, DEC_SEQ, D_MODEL)),
        "cache_att_k": nrm(ks[2], (L, DEC_BATCH, win_buf, N_ATT_HEADS, HEAD_DIM)),
        "cache_att_v": nrm(ks[3], (L, DEC_BATCH, win_buf, N_ATT_HEADS, HEAD_DIM)),
        "state_rwkv_shift": nrm(ks[4], (L, DEC_BATCH, 1, C_SHIFT)),
        "state_rwkv_wkv": nrm(ks[5], (L, DEC_BATCH, N_RW_HEADS, HEAD_DIM, HEAD_DIM), 0.3),
        "state_ffn_conv": nrm(ks[6], (L, DEC_BATCH, CONV_W - 1, 2 * D_FF)),
        "norm_mix_g": 1.0 + nrm(ks[7], (L, D_MODEL), 0.02),
        "w_in": nrm(ks[8], (L, D_MODEL, C_IN), D_MODEL ** -0.5),
        "att_out_g": 1.0 + nrm(ks[9], (L, C_ATT), 0.02),
        "rw_mu": jax.random.uniform(ks[10], (L, C_SHIFT), jnp.float32),
        "rw_w0": jax.random.uniform(ks[11], (L, C_RW), jnp.float32, minval=-5.0, maxval=0.0),
        "rw_w_up": nrm(ks[12], (L, W_LORA, C_RW), 0.5 * W_LORA ** -0.5),
        "rw_a0": nrm(ks[13], (L, C_RW), 0.5),
        "rw_a_up": nrm(ks[14], (L, A_LORA, C_RW), A_LORA ** -0.5),
        "rw_g_up": nrm(ks[15], (L, G_LORA, C_RW), G_LORA ** -0.5),
        "rw_k_k": 0.85 + nrm(ks[16], (L, C_RW), 0.05),
        "rw_k_a": 1.0 + nrm(ks[17], (L, C_RW), 0.05),
        "rw_r_k": nrm(ks[18], (L, C_RW), 0.1),
        "rw_lnx_w": 1.0 + nrm(ks[19], (L, C_RW), 0.02),
        "rw_lnx_b": nrm(ks[20], (L, C_RW), 0.02),
        "w_o": nrm(ks[21], (L, C_MIX, D_MODEL), C_MIX ** -0.5),
        "norm_ffn_g": 1.0 + nrm(ks[22], (L, D_MODEL), 0.02),
        "ffn_w_up": nrm(ks[23], (L, D_MODEL, 2 * D_FF), D_MODEL ** -0.5),
        "ffn_conv_w": nrm(ks[24], (L, CONV_W, 2 * D_FF), CONV_W ** -0.5),
        "ffn_conv_b": nrm(ks[25], (L, 2 * D_FF), 0.02),
        "ffn_w_down": nrm(ks[26], (L, D_FF, D_MODEL), D_FF ** -0.5),
        "norm_final_g": 1.0 + nrm(ks[27], (D_MODEL,), 0.02),
    }


def reference(x_prompt, x_sample, cache_att_k, cache_att_v, state_rwkv_shift, state_rwkv_wkv, state_ffn_conv,
              norm_mix_g, w_in, att_out_g, rw_mu, rw_w0, rw_w_up, rw_a0, rw_a_up, rw_g_up, rw_k_k, rw_k_a,
              rw_r_k, rw_lnx_w, rw_lnx_b, w_o, norm_ffn_g, ffn_w_up, ffn_conv_w, ffn_conv_b, ffn_w_down,
              norm_final_g):
    B, S, _ = x_prompt.shape
    win_p = min(MAX_WINDOW, S)
    rw_prev_p = jnp.zeros((B, 1, C_SHIFT), x_prompt.dtype)
    wkv_p0 = jnp.zeros((B, N_RW_HEADS, HEAD_DIM, HEAD_DIM), jnp.float32)
    ffn_prev_p = jnp.zeros((B, CONV_W - 1, 2 * D_FF), x_prompt.dtype)
    xp, xs = x_prompt, x_sample
    pk, pv, prw, pwkv, pffn = [], [], [], [], []
    sk, sv, srw, swkv, sffn = [], [], [], [], []
    for l in range(DEPTH):
        lp = dict(norm_mix_g=norm_mix_g[l], w_in=w_in[l], att_out_g=att_out_g[l], rw_mu=rw_mu[l],
                  rw_w0=rw_w0[l], rw_w_up=rw_w_up[l], rw_a0=rw_a0[l], rw_a_up=rw_a_up[l], rw_g_up=rw_g_up[l],
                  rw_k_k=rw_k_k[l], rw_k_a=rw_k_a[l], rw_r_k=rw_r_k[l], rw_lnx_w=rw_lnx_w[l],
                  rw_lnx_b=rw_lnx_b[l], w_o=w_o[l], norm_ffn_g=norm_ffn_g[l], ffn_w_up=ffn_w_up[l],
                  ffn_conv_w=ffn_conv_w[l], ffn_conv_b=ffn_conv_b[l], ffn_w_down=ffn_w_down[l])
        xp, kp_, vp_, rwp, wkvp, ffp = decoder_layer(xp, dilated_attention_prompt, rw_prev_p, wkv_p0,
                                                     ffn_prev_p, **lp)
        attend_s = functools.partial(dilated_attention_sample, k_cache=cache_att_k[l], v_cache=cache_att_v[l])
        xs, ks_, vs_, rws, wkvs, ffs = decoder_layer(xs, attend_s, state_rwkv_shift[l], state_rwkv_wkv[l],
                                                     state_ffn_conv[l], **lp)
        pk.append(kp_[:, -win_p:]); pv.append(vp_[:, -win_p:]); prw.append(rwp); pwkv.append(wkvp); pffn.append(ffp)
        sk.append(ks_); sv.append(vs_); srw.append(rws); swkv.append(wkvs); sffn.append(ffs)
    y_prompt = rms_norm(xp, norm_final_g)
    y_sample = rms_norm(xs, norm_final_g)
    return (y_prompt, y_sample,
            jnp.stack(pk), jnp.stack(pv), jnp.stack(prw), jnp.stack(pwkv), jnp.stack(pffn),
            jnp.stack(sk), jnp.stack(sv), jnp.stack(srw), jnp.stack(swkv), jnp.stack(sffn))
```

```python
import contextlib
import os
import numpy as np
import ml_dtypes
import concourse.bass as bass
import concourse.mybir as mybir
from concourse.bass_utils import run_bass_kernel_spmd

F32 = mybir.dt.float32
BF16 = mybir.dt.bfloat16
I32 = mybir.dt.int32
ALU = mybir.AluOpType
AF = mybir.ActivationFunctionType
AX = mybir.AxisListType

ENGS = ["pe", "act", "dve", "pool", "sp"]
EPOCH = 30000
N_DMA_SEMS = 32


class Prog:
    def __init__(self, nc):
        self.nc = nc
        self.streams = {e: [] for e in ENGS}
        self.cnt = {e: 0 for e in ENGS}
        self.seen = {e: {} for e in ENGS}
        self.lastw = {}
        self.readers = {}
        self.dma_sems = ["dma%d" % i for i in range(N_DMA_SEMS)]
        self.sdma_sems = ["sdma%d" % i for i in range(16)]
        self.dma_cnt = {s: 0 for s in self.dma_sems + self.sdma_sems}
        self.dma_rr = 0
        self.sdma_rr = 0
        self.semnames = set()

    def _need(self, reads, writes):
        need = {}
        for k in reads:
            lw = self.lastw.get(k)
            if lw is not None:
                need[lw[0]] = max(need.get(lw[0], 0), lw[1])
        for k in writes:
            lw = self.lastw.get(k)
            if lw is not None:
                need[lw[0]] = max(need.get(lw[0], 0), lw[1])
            for s, v in self.readers.get(k, {}).items():
                need[s] = max(need.get(s, 0), v)
        return need

    def _waits(self, eng, need):
        waits = []
        for s, v in need.items():
            if eng == "pe" and s.startswith("pe"):
                continue
            if self.seen[eng].get(s, 0) >= v:
                continue
            self.seen[eng][s] = v
            waits.append((s, v))
        return waits

    def _mark(self, reads, writes, sem, val):
        for k in reads:
            d = self.readers.setdefault(k, {})
            d[sem] = max(d.get(sem, 0), val)
        for k in writes:
            self.lastw[k] = (sem, val)
            self.readers[k] = {}

    def op(self, eng, reads, writes, meth, *a, **kw):
        fn = (meth, a, kw)
        waits = self._waits(eng, self._need(reads, writes))
        c = self.cnt[eng]
        sem = "%s_e%d" % (eng, c // EPOCH)
        val = c % EPOCH + 1
        self.cnt[eng] = c + 1
        self.semnames.add(sem)
        self.streams[eng].append((fn, waits, (sem, 1)))
        self._mark(reads, writes, sem, val)

    def dma(self, q, reads, writes, **kw):
        fn = ("dma_start", (), kw)
        if q == "pool":
            s = self.sdma_sems[self.sdma_rr % 16]
            self.sdma_rr += 1
        else:
            s = self.dma_sems[self.dma_rr % N_DMA_SEMS]
            self.dma_rr += 1
        need = self._need(reads, writes)
        prev = self.dma_cnt[s]
        if prev > 0:
            need[s] = max(need.get(s, 0), prev)
        waits = self._waits(q, need)
        self.dma_cnt[s] = prev + 16
        self.semnames.add(s)
        self.streams[q].append((fn, waits, (s, 16)))
        self._mark(reads, writes, s, prev + 16)

    def barrier(self):
        need = {}
        for sname in self.dma_sems + self.sdma_sems:
            if self.dma_cnt[sname] > 0:
                need[sname] = self.dma_cnt[sname]
        for e in ENGS:
            c = self.cnt[e]
            if c > 0:
                need["%s_e%d" % (e, (c - 1) // EPOCH)] = (c - 1) % EPOCH + 1
        for e in ENGS:
            waits = []
            for sn, v in need.items():
                if self.seen[e].get(sn, 0) >= v:
                    continue
                self.seen[e][sn] = v
                waits.append((sn, v))
            if waits:
                self.streams[e].append((None, waits, None))

    def emit(self):
        nc = self.nc
        names = sorted(self.semnames)
        final = []
        for s in names:
            if s.startswith("dma") or s.startswith("sdma"):
                final.append((s, self.dma_cnt[s]))
        for e in ENGS:
            c = self.cnt[e]
            if c > 0:
                final.append(("%s_e%d" % (e, (c - 1) // EPOCH), (c - 1) % EPOCH + 1))
        with contextlib.ExitStack() as st:
            sems = {n: st.enter_context(nc.semaphore(n)) for n in names}
            block = st.enter_context(nc.Block())
            streams = self.streams

            def run(engobj, lst, fin=None):
                for fn, waits, inc in lst:
                    for s, v in waits:
                        engobj.wait_ge(sems[s], v)
                    if fn is None:
                        continue
                    ins = getattr(engobj, fn[0])(*fn[1], **fn[2])
                    ins.then_inc(sems[inc[0]], inc[1])
                if fin:
                    for s, v in fin:
                        engobj.wait_ge(sems[s], v)

            @block.tensor
            def _(e):
                run(e, streams["pe"])

            @block.scalar
            def _(e):
                run(e, streams["act"])

            @block.vector
            def _(e):
                run(e, streams["dve"])

            @block.gpsimd
            def _(e):
                run(e, streams["pool"])

            @block.sync
            def _(e):
                run(e, streams["sp"], final)


D = 2048
NSLOT = 2048
NTOK = 2176
NTILE = 17
C_IN = 6432
C_SH = 3360
DFF = 5632
EPS = 1.0e-6
STAGE = 99
HAVE_SAMPLE = True


class Ctx:
    pass


def build(stage=STAGE):
    nc = bass.Bass("TRN2", target_bir_lowering=False)
    P = Prog(nc)

    def OP(eng, meth, reads, writes, *a, **kw):
        if eng in ("act", "dve"):
            extra = [k for k in reads if isinstance(k, tuple) and k[0] == "pmm"]
            if extra:
                writes = list(writes) + extra
        P.op(eng, reads, writes, meth, *a, **kw)

    def DMA(q, out, in_, reads, writes, **kw):
        P.dma(q, reads, writes, out=out, in_=in_, **kw)

    def din(name, shape, dt=F32):
        return nc.dram_tensor(name, list(shape), dt, kind="ExternalInput").ap()

    def dout(name, shape, dt=F32):
        return nc.dram_tensor(name, list(shape), dt, kind="ExternalOutput").ap()

    def dscr(name, shape, dt=F32):
        return nc.dram_tensor(name, list(shape), dt).ap()

    xin = din("xin", [NTOK, D])
    ident_d = din("ident", [128, 128], BF16)
    masks_d = din("masks", [128, 9, 512], BF16)
    flag_d = din("flag", [128, 1])
    w_in = din("w_in", [D, C_IN])
    g_mix = din("norm_mix_g", [D])
    g_att = din("att_out_g", [1, 1024])
    rw_mu = din("rw_mu", [C_SH])
    sshiftT = din("sshiftT", [C_SH, 4])
    trim_d = din("trim", [64, 320], BF16)
    rw_w0 = din("rw_w0", [1024])
    rw_a0 = din("rw_a0", [1024])
    rw_k_k = din("rw_k_k", [1024])
    rw_k_a = din("rw_k_a", [1024])
    rw_r_k = din("rw_r_k", [1024])
    rw_w_up = din("rw_w_up", [64, 1024])
    rw_a_up = din("rw_a_up", [64, 1024])
    rw_g_up = din("rw_g_up", [160, 1024])
    rw_lnx_w = din("rw_lnx_w", [1, 1024])
    rw_lnx_b = din("rw_lnx_b", [1, 1024])
    w_o = din("w_o", [D, D])
    g_ffn = din("norm_ffn_g", [D])
    w_up = din("ffn_w_up", [D, 2 * DFF])
    conv_w = din("ffn_conv_w", [3, 2 * DFF])
    conv_b = din("ffn_conv_b", [2 * DFF])
    w_down = din("ffn_w_down", [DFF, D])
    g_fin = din("norm_final_g", [1, D])
    sffnT = din("sffnT", [2 * DFF, 2, 4])
    sffn_in1 = din("sffn_in1", [4, 2 * DFF])
    cache_k = din("cache_k", [4, 2048, 1024])
    cache_v = din("cache_v", [4, 2048, 1024])
    swkv_in = din("swkv_in", [4, 16, 64, 64])
    rw_lnx_w_c = din("rw_lnx_w_c", [1024])
    rw_lnx_b_c = din("rw_lnx_b_c", [1024])
    k_out = dout("k_out", [1024, 1024])
    v_out = dout("v_out", [1024, 1024])
    sk_out = dout("sk_out", [4, 1024])
    sv_out = dout("sv_out", [4, 1024])
    pshift_out = dout("pshift_out", [C_SH, 1])
    sshift_outT = dout("sshift_outT", [C_SH, 4])
    pwkv_out = dout("pwkv_out", [16, 64, 64])
    y_out = dout("y_out", [1024, D])
    ys_out = dout("ys_out", [4, D])
    pffn_out = dout("pffn_out", [2, 2 * DFF])
    sffn_outT = dout("sffn_outT", [2 * DFF, 4])
    sffn_row0 = dout("sffn_row0", [4, 2 * DFF])
    swkv_out = dout("swkv_out", [4, 16, 64, 64])
    QT = dscr("QT", [1024, NTOK], BF16)
    KT = dscr("KT", [1024, NTOK], BF16)
    XS = dscr("XS", [C_SH, NTOK], F32)
    O_scr = dscr("O_scr", [1280, 2048], BF16)
    QS = dscr("QS", [4, 1024], F32)
    KS = dscr("KS", [4, 1024], F32)
    VS = dscr("VS", [4, 1024], F32)
    XM = dscr("XM", [1280, D], F32)
    XO = dscr("XO", [1152, D], F32)

    with contextlib.ExitStack() as st:
        def sb(name, shape, dt=F32):
            return st.enter_context(nc.sbuf_tensor(name, list(shape), dt))

        def ps(name, shape, dt=F32):
            return st.enter_context(nc.psum_tensor(name, list(shape), dt))

        ident = sb("ident_sb", [128, 128], BF16)
        DMA("sp", ident[:], ident_d[:, :], [], ["ident"])
        epst = sb("epst", [128, 1])
        OP("pool", "memset", [], ["epst"], epst[:], EPS)
        epsl = sb("epsl", [128, 1])
        OP("pool", "memset", [], ["epsl"], epsl[:], 64 * 1e-5)
        flag = sb("flag_sb", [128, 1])
        DMA("sp", flag[:], flag_d[:, :], [], ["flag"])
        gcol = sb("gcol", [128, 16])
        DMA("sp", gcol[:], g_mix.rearrange("(k p) -> p k", p=128), [], ["gcol"], allow_slow_non_contiguous=True)
        mucol = sb("mucol", [128, 27])
        DMA("sp", mucol[:, 0:26], rw_mu[0:3328].rearrange("(k p) -> p k", p=128), [], ["mucol"],
            allow_slow_non_contiguous=True)
        DMA("sp", mucol[0:32, 26:27], rw_mu[3328:3360].rearrange("(p o) -> p o", o=1), ["mucol"], ["mucol"],
            allow_slow_non_contiguous=True)

        ss = [sb("ss%d" % i, [128, 1]) for i in range(2)]
        rstd = [sb("rstd%d" % i, [128, 1]) for i in range(2)]
        pmm = [ps("pmm%d" % i, [128, 512]) for i in range(8)]
        pT = [pmm[6 + i][:, :].bitcast(BF16).rearrange("p (a b) -> p a b", b=128)[:, 0:4, :] for i in range(2)]
        stAB = contextlib.ExitStack()
        def sbAB(name, shape, dt=F32):
            return stAB.enter_context(nc.sbuf_tensor(name, list(shape), dt))
        VA = sbAB("VA", [128, 17, 16, 65], BF16)
        OP("pool", "memset", [], ["VAones"], VA[:, :, :, 64:65], 1.0)
        OP("pool", "tensor_scalar", ["flag", "VAones"], ["VAones"], VA[:, 0:8, :, 64:65], VA[:, 0:8, :, 64:65],
           flag[:, 0:1], None, ALU.mult)

        MARK = {}
        def mark(name):
            MARK[name] = dict(P.cnt)
        _NC_CACHE["MARK"] = MARK
        stA = contextlib.ExitStack()
        def sbA(name, shape, dt=F32):
            return stA.enter_context(nc.sbuf_tensor(name, list(shape), dt))
        hT = sbA("hT", [128, 16, NTOK], BF16)
        xt = [sbA("xt%d" % i, [128, D]) for i in range(2)]
        xn = [sbA("xn%d" % i, [128, D], BF16) for i in range(2)]

        def norm_transpose(src_ap, gc, gkey, dst, dst_key, it, src_keys=()):
            b = it % 2
            DMA("sp", xt[b][:], src_ap, list(src_keys), ["xt%d" % b])
            OP("act", "activation", ["xt%d" % b], ["xn%d" % b, "ss%d" % b], out=xn[b][:], in_=xt[b][:], func=AF.Square,
               accum_out=ss[b][:])
            OP("act", "activation", ["ss%d" % b, "epst"], ["rstd%d" % b], out=rstd[b][:], in_=ss[b][:], func=AF.Ln,
               scale=1.0 / D, bias=epst[:, 0:1])
            OP("act", "activation", ["rstd%d" % b], ["rstd%d" % b], out=rstd[b][:], in_=rstd[b][:], func=AF.Exp,
               scale=-0.5)
            OP("dve", "tensor_scalar", ["xt%d" % b, "rstd%d" % b], ["xn%d" % b], xn[b][:], xt[b][:],
               rstd[b][:, 0:1], None, ALU.mult)
            for g4 in range(4):
                pb = g4 % 2
                for j in range(4):
                    kc = g4 * 4 + j
                    OP("pe", "transpose", ["xn%d" % b, "ident"], ["pT%d" % pb], out=pT[pb][:, j, :],
                       in_=xn[b][:, kc * 128:(kc + 1) * 128], identity=ident[:])
                OP("dve", "tensor_tensor", ["pT%d" % pb, gkey], [dst_key], out=dst[:, g4 * 4:(g4 + 1) * 4, :],
                   in0=pT[pb], in1=gc[:, g4 * 4:(g4 + 1) * 4].unsqueeze(2).broadcast_to([128, 4, 128]),
                   op=ALU.mult)

        for t in range(NTILE):
            norm_transpose(xin[t * 128:(t + 1) * 128, :], gcol, "gcol", hT[:, :, t * 128:(t + 1) * 128], ("hT", t), t)

        mark("A_norm_done")
        SL = 256
        wbf = [sbA("wbf%d" % i, [128, 16, SL], BF16) for i in range(3)]
        ev32 = [sbA("ev32_%d" % i, [128, 512]) for i in range(4)]
        ev16 = [sbA("ev16_%d" % i, [128, 512], BF16) for i in range(4)]
        cb = [sbA("cb%d" % i, [128, 513]) for i in range(2)]
        dsh = sbA("dsh", [128, 512])
        xsb = [sbA("xsb%d" % i, [128, 512]) for i in range(2)]
        sprev = sbA("sprev", [128, 27, 4])
        DMA("sp", sprev[:, 0:26, :], sshiftT[0:3328, :].rearrange("(k p) f -> p k f", p=128), [], ["sprev"])
        DMA("sp", sprev[0:32, 26, :], sshiftT[3328:3360, :], ["sprev"], ["sprev"])
        state = {"pm": 0, "evq": 0, "rwb": 0}
        hT_all = [("hT", t) for t in range(NTILE)]

        def evac(dst_sb, src_ps, rkeys, wkeys):
            state["evq"] += 1
            if state["evq"] % 2 == 0:
                OP("act", "activation", rkeys, wkeys, out=dst_sb, in_=src_ps, func=AF.Copy)
            else:
                OP("dve", "tensor_copy", rkeys, wkeys, out=dst_sb, in_=src_ps)

        nslab = (C_IN + SL - 1) // SL
        for s in range(nslab):
            c0 = s * SL
            cw = min(SL, C_IN - c0)
            b = s % 3
            for half in range(2):
                DMA("pool", wbf[b][:, half * 8:(half + 1) * 8, 0:cw],
                    w_in[half * 1024:(half + 1) * 1024, c0:c0 + cw].rearrange("(k p) c -> p k c", p=128),
                    [], [("wbf", b, half)])
            wkeys = [("wbf", b, 0), ("wbf", b, 1)]
            kind = "q" if c0 < 1024 else "k" if c0 < 2048 else "v" if c0 < 3072 else "rw"
            if kind in ("q", "k", "rw"):
                for m in range((cw + 127) // 128):
                    mw = min(128, cw - m * 128)
                    cc = c0 + m * 128
                    if kind == "q":
                        blocks = [(896, 128), (1024, 512), (1536, 512), (2048, 128)]
                    else:
                        blocks = [(0, 512), (512, 512), (1024, 512), (1536, 512), (2048, 128)]
                    for (t0, n) in blocks:
                        pi = state["pm"] % 4
                        state["pm"] += 1
                        for kc in range(16):
                            OP("pe", "matmul", wkeys + hT_all, [("pmm", pi)], pmm[pi][0:mw, 0:n],
                               lhsT=wbf[b][:, kc, m * 128:m * 128 + mw], rhs=hT[:, kc, t0:t0 + n],
                               start=(kc == 0), stop=(kc == 15))
                        if kind in ("q", "k"):
                            dstT = QT if kind == "q" else KT
                            r0 = cc - (0 if kind == "q" else 1024)
                            evac(ev16[pi][0:mw, 0:n], pmm[pi][0:mw, 0:n], [("pmm", pi)], [("ev16", pi)])
                            DMA("sp", dstT[r0:r0 + mw, t0:t0 + n], ev16[pi][0:mw, 0:n], [("ev16", pi)],
                                [("scr", kind, r0 // 64, t0), ("scr", kind, r0 // 64 + 1, t0)])
                        else:
                            ch0 = cc - 3072
                            ci = ch0 // 128
                            rb = state["rwb"] % 2
                            state["rwb"] += 1
                            nb = cb[rb]
                            if t0 == 0:
                                OP("dve", "memset", [], [("cb", rb)], nb[0:mw, 0:1], 0.0)
                            OP("act", "activation", [("pmm", pi)], [("cb", rb)], out=nb[0:mw, 1:1 + n],
                               in_=pmm[pi][0:mw, 0:n], func=AF.Copy)
                            if t0 < 1536:
                                OP("dve", "tensor_copy", [("cb", rb)], [("cb", 1 - rb)], out=cb[1 - rb][0:mw, 0:1],
                                   in_=nb[0:mw, n:n + 1])
                            ne = n if t0 < 2048 else 4
                            prev_ap = nb[0:mw, 0:ne] if t0 < 2048 else sprev[0:mw, ci, :]
                            OP("dve", "tensor_tensor", [("cb", rb), "sprev"], ["dsh"], out=dsh[0:mw, 0:ne],
                               in0=prev_ap, in1=nb[0:mw, 1:1 + ne], op=ALU.subtract)
                            xb = state["rwb"] % 2
                            OP("dve", "scalar_tensor_tensor", ["dsh", ("cb", rb), "mucol"], [("xsb", xb)],
                               out=xsb[xb][0:mw, 0:ne], in0=dsh[0:mw, 0:ne], scalar=mucol[0:mw, ci:ci + 1],
                               in1=nb[0:mw, 1:1 + ne], op0=ALU.mult, op1=ALU.add)
                            DMA("sp", XS[ch0:ch0 + mw, t0:t0 + ne], xsb[xb][0:mw, 0:ne], [("xsb", xb)],
                                [("XS", ci, t0)])
                            if t0 == 1536:
                                DMA("sp", pshift_out[ch0:ch0 + mw, :], nb[0:mw, 512:513], [("cb", rb)],
                                    [("pshift", ci)])
                            if t0 == 2048:
                                DMA("sp", sshift_outT[ch0:ch0 + mw, :], nb[0:mw, 1:5], [("cb", rb)],
                                    [("sshift", ci)])
            if kind in ("q", "k", "v"):
                tiles = [16] if kind == "q" else list(range(8, 17)) if kind == "k" else list(range(17))
                for t in tiles:
                    pi = state["pm"] % 4
                    state["pm"] += 1
                    for kc in range(16):
                        OP("pe", "matmul", wkeys + [("hT", t)], [("pmm", pi)], pmm[pi][:, 0:cw],
                           lhsT=hT[:, kc, t * 128:(t + 1) * 128], rhs=wbf[b][:, kc, 0:cw],
                           start=(kc == 0), stop=(kc == 15))
                    evac(ev32[pi][:, 0:cw], pmm[pi][:, 0:cw], [("pmm", pi)], [("ev32", pi)])
                    col0 = c0 - (0 if kind == "q" else 1024 if kind == "k" else 2048)
                    if kind == "q":
                        DMA("sp", QS[0:4, col0:col0 + cw], ev32[pi][0:4, 0:cw], [("ev32", pi)], [("QS", col0)])
                        continue
                    if t == 16:
                        scr = KS if kind == "k" else VS
                        DMA("sp", scr[0:4, col0:col0 + cw], ev32[pi][0:4, 0:cw], [("ev32", pi)],
                            [("KVS", kind, col0)])
                    if 8 <= t < 16:
                        o = k_out if kind == "k" else v_out
                        DMA("sp", o[(t - 8) * 128:(t - 7) * 128, col0:col0 + cw], ev32[pi][:, 0:cw],
                            [("ev32", pi)], [("kvout", kind, t, col0)])
                    if t == 16:
                        o = sk_out if kind == "k" else sv_out
                        DMA("sp", o[0:4, col0:col0 + cw], ev32[pi][0:4, 0:cw], [("ev32", pi)],
                            [("skvout", kind, col0)])
                    if kind == "v":
                        h0 = col0 // 64
                        src = ev32[pi][:, 0:cw].rearrange("p (h e) -> p h e", e=64)
                        if t < 8:
                            OP("dve", "tensor_scalar", [("ev32", pi), "flag"], [("VA", t)],
                               VA[:, t, h0:h0 + 4, 0:64], src, flag[:, 0:1], None, ALU.mult)
                        else:
                            OP("dve", "tensor_copy", [("ev32", pi)], [("VA", t)], out=VA[:, t, h0:h0 + 4, 0:64],
                               in_=src)
        P.barrier()
        stA.close()
        mark("A_done")
        stB = contextlib.ExitStack()
        def sbB(name, shape, dt=F32):
            return stB.enter_context(nc.sbuf_tensor(name, list(shape), dt))
        Oatt = sbB("Oatt", [128, 9, 1024], BF16)
        msk = sbB("msk", [128, 9, 512], BF16)
        gatt = sbB("gatt", [128, 1024])
        DMA("sp", msk[:], masks_d[:, :, :], [], ["msk"])
        DMA("pool", gatt[:], g_att[0:1, :].partition_broadcast(128), [], ["gatt"])
        qh = [sbB("qh%d" % i, [64, 1152], BF16) for i in range(2)]
        kh = [sbB("kh%d" % i, [64, 2048], BF16) for i in range(2)]
        pTs = [sbB("pTs%d" % i, [128, 512], BF16) for i in range(4)]
        SBK = [0, 1, 6, 7]
        junk64 = sbB("junk64", [128, 64])
        fin = [[sbB("fin%d_%d" % (i, j), [128, 1]) for j in range(5)] for i in range(2)]
        stt = {"s": 0, "mq": 0, "f": 0}
        scr_q = lambda hh: [("scr", "q", hh, t0) for t0 in (896, 1024, 1536, 2048)]
        scr_k = lambda hh: [("scr", "k", hh, t0) for t0 in (0, 512, 1024, 1536, 2048)]
        items = []
        for h in range(16):
            b = h % 2
            first_of_head = [True]
            for (q0, n) in [(896, 128), (1024, 512), (1536, 512)]:
                nsub = n // 128
                qt0 = q0 // 128
                nkc = qt0 + nsub
                for kc in range(0, nkc):
                    def s1(h=h, b=b, q0=q0, n=n, kc=kc, qt0=qt0, ld=first_of_head[0]):
                        if ld:
                            DMA("sp", qh[b][:, :], QT[h * 64:(h + 1) * 64, 896:2048], scr_q(h), [("qh", b)])
                            DMA("sp", kh[b][:, :], KT[h * 64:(h + 1) * 64, 0:2048], scr_k(h), [("kh", b)])
                        s0 = kc * 128
                        d0 = q0 - s0
                        qi_min = max(0, kc - qt0)
                        c_lo = qi_min * 128
                        mi = (d0 + 384) // 128 if d0 <= 512 else 8
                        sj = stt["s"] % 4
                        si = SBK[sj]
                        stt["s"] += 1
                        OP("pe", "matmul", [("kh", b), ("qh", b)], [("pmm", si)], pmm[si][:, c_lo:n],
                           lhsT=kh[b][:, s0:s0 + 128], rhs=qh[b][:, q0 - 896 + c_lo:q0 - 896 + n], start=True, stop=True)
                        OP("act", "activation", [("pmm", si)], [("pTs", sj)], out=pTs[sj][:, c_lo:n],
                           in_=pmm[si][:, c_lo:n], func=AF.Exp, scale=0.125)
                        OP("dve", "tensor_tensor", [("pTs", sj), "msk"], [("pTs", sj)],
                           out=pTs[sj][:, c_lo:n], in0=pTs[sj][:, c_lo:n], in1=msk[:, mi, c_lo:n], op=ALU.mult)
                        return sj, qi_min
                    def s2(sj, qi_min, h=h, q0=q0, n=n, kc=kc, qt0=qt0, nsub=nsub, last=(kc == nkc - 1)):
                        for qi in range(qi_min, nsub):
                            OP("pe", "matmul", [("pTs", sj), ("VA", kc), "VAones"], [("pmm", 2 + qi)],
                               pmm[2 + qi][:, 0:65], lhsT=pTs[sj][:, qi * 128:(qi + 1) * 128], rhs=VA[:, kc, h, :],
                               start=(kc == 0), stop=(kc == qt0 + qi))
                        if not last:
                            return
                        for qi in range(nsub):
                            tile_i = qt0 + qi - 7
                            acc = pmm[2 + qi]
                            f = fin[stt["f"] % 2]
                            fk = ("fin", stt["f"] % 2)
                            stt["f"] += 1
                            OP("act", "activation", [("pmm", 2 + qi)], ["junk64", fk], out=junk64[:], in_=acc[:, 0:64],
                               func=AF.Square, accum_out=f[0][:])
                            OP("act", "activation", [("pmm", 2 + qi)], [fk], out=f[1][:], in_=acc[:, 64:65], func=AF.Copy)
                            OP("dve", "scalar_tensor_tensor", [fk], [fk], out=f[2][:], in0=f[1][:], scalar=EPS, in1=f[1][:],
                               op0=ALU.mult, op1=ALU.mult)
                            OP("dve", "scalar_tensor_tensor", [fk], [fk], out=f[3][:], in0=f[0][:], scalar=1.0 / 64,
                               in1=f[2][:], op0=ALU.mult, op1=ALU.add)
                            OP("dve", "tensor_scalar_max", [fk], [fk], out=f[3][:], in0=f[3][:], scalar1=1e-30)
                            OP("act", "activation", [fk], [fk], out=f[4][:], in_=f[3][:], func=AF.Ln)
                            OP("act", "activation", [fk], [fk], out=f[4][:], in_=f[4][:], func=AF.Exp, scale=-0.5)
                            OP("dve", "scalar_tensor_tensor", [("pmm", 2 + qi), fk, "gatt"], [("Oatt", tile_i)],
                               out=Oatt[:, tile_i, h * 64:(h + 1) * 64], in0=acc[:, 0:64], scalar=f[4][:, 0:1],
                               in1=gatt[:, h * 64:(h + 1) * 64], op0=ALU.mult, op1=ALU.mult)
                    items.append((s1, s2))
                    first_of_head[0] = False
        LA = 2
        pend = {}
        for k in range(len(items) + LA):
            if k < len(items):
                pend[k] = items[k][0]()
            if k - LA >= 0:
                items[k - LA][1](*pend.pop(k - LA))
        mark("B_prompt_done")
        qb = sbB("qb", [128, 1024])
        ktl = [sbB("ktl%d" % i, [128, 1024]) for i in range(2)]
        vtl = [sbB("vtl%d" % i, [128, 1024]) for i in range(2)]
        prod = sbB("prod", [128, 1024])
        pvb = sbB("pvb", [128, 1024], BF16)
        sc = sbB("sc", [128, 16])
        pb16 = sbB("pb16", [128, 16], BF16)
        onesb = sbB("onesb", [128, 1], BF16)
        OP("pool", "memset", [], ["onesb"], onesb[:], 1.0)
        srow = {nm: sbB("srow_" + nm, [1, 1024]) for nm in ("o", "sq", "o2")}
        srb = sbB("srow_b", [1, 1024], BF16)
        s16 = {nm: sbB("s16_" + nm, [1, 16]) for nm in ("den", "rden", "ms", "r")}
        qs_keys = [("QS", c) for c in (0, 256, 512, 768)]
        ks_keys = [("KVS", "k", c) for c in (0, 256, 512, 768)]
        vs_keys = [("KVS", "v", c) for c in (0, 256, 512, 768)]
        pnum = [pmm[0][0:1, :], pmm[1][0:1, :]]
        pden = pmm[2][0:1, 0:16]
        gi = {"i": 0}
        for bi in range(4):
            DMA("sp", qb[:], QS[bi:bi + 1, :].partition_broadcast(128), qs_keys, ["qb"])
            groups = [(2048 - 128 * r, r, 128) for r in (1, 4, 16)] + [(None, 0, 1)]
            for gidx, (start, rate, rows) in enumerate(groups):
                tb_ = gi["i"] % 2
                gi["i"] += 1
                kt_, vt_ = ktl[tb_], vtl[tb_]
                if start is not None:
                    DMA("sp", kt_[:], cache_k[bi, start:2048:rate, :], [], [("ktl", tb_)])
                    DMA("pool", vt_[:], cache_v[bi, start:2048:rate, :], [], [("vtl", tb_)])
                else:
                    DMA("sp", kt_[0:1, :], KS[bi:bi + 1, :], ks_keys, [("ktl", tb_)])
                    DMA("pool", vt_[0:1, :], VS[bi:bi + 1, :], vs_keys, [("vtl", tb_)])
                R_ = slice(0, rows)
                OP("dve", "tensor_tensor", [("ktl", tb_), "qb"], ["prod"], out=prod[R_, :], in0=kt_[R_, :], in1=qb[R_, :],
                   op=ALU.mult)
                OP("dve", "tensor_reduce", ["prod"], ["sc"], out=sc[R_, :],
                   in_=prod[R_, :].rearrange("p (h e) -> p h e", e=64), axis=AX.X, op=ALU.add)
                OP("act", "activation", ["sc"], ["sc"], out=sc[R_, :], in_=sc[R_, :], func=AF.Exp, scale=0.125)
                if start is None:
                    OP("dve", "tensor_scalar", ["sc"], ["sc"], sc[R_, :], sc[R_, :], 3.0, None, ALU.mult)
                OP("dve", "tensor_copy", ["sc"], ["pb16"], out=pb16[R_, :], in_=sc[R_, :])
                OP("dve", "tensor_tensor", [("vtl", tb_), "sc"], ["pvb"],
                   out=pvb[R_, :].rearrange("p (h e) -> p h e", e=64),
                   in0=vt_[R_, :].rearrange("p (h e) -> p h e", e=64),
                   in1=sc[R_, :].unsqueeze(2).broadcast_to([rows, 16, 64]), op=ALU.mult)
                first, last = (gidx == 0), (gidx == 3)
                for hf in range(2):
                    OP("pe", "matmul", ["pvb", "onesb"], [("pmm", hf)], pnum[hf], lhsT=onesb[R_, :],
                       rhs=pvb[R_, hf * 512:(hf + 1) * 512], start=first, stop=last)
                OP("pe", "matmul", ["pb16", "onesb"], [("pmm", 2)], pden, lhsT=onesb[R_, :], rhs=pb16[R_, :], start=first,
                   stop=last)
            OP("act", "activation", [("pmm", 2)], ["s16"], out=s16["den"][:], in_=pden, func=AF.Copy)
            OP("dve", "reciprocal", ["s16"], ["s16"], out=s16["rden"][:], in_=s16["den"][:])
            for hf in range(2):
                OP("dve", "tensor_tensor", [("pmm", hf), "s16"], ["srow_o"],
                   out=srow["o"][:, hf * 512:(hf + 1) * 512].rearrange("p (h e) -> p h e", e=64),
                   in0=pnum[hf].rearrange("p (h e) -> p h e", e=64),
                   in1=s16["rden"][:, hf * 8:(hf + 1) * 8].unsqueeze(2).broadcast_to([1, 8, 64]), op=ALU.mult)
            OP("dve", "tensor_tensor", ["srow_o"], ["srow_sq"], out=srow["sq"][:], in0=srow["o"][:], in1=srow["o"][:],
               op=ALU.mult)
            OP("dve", "tensor_reduce", ["srow_sq"], ["s16"], out=s16["ms"][:],
               in_=srow["sq"][:].rearrange("p (h e) -> p h e", e=64), axis=AX.X, op=ALU.add)
            OP("act", "activation", ["s16", "epst"], ["s16"], out=s16["r"][:], in_=s16["ms"][:], func=AF.Ln, scale=1.0 / 64,
               bias=epst[0:1, 0:1])
            OP("act", "activation", ["s16"], ["s16"], out=s16["r"][:], in_=s16["r"][:], func=AF.Exp, scale=-0.5)
            OP("dve", "tensor_tensor", ["srow_o", "s16"], ["srow_o2"],
               out=srow["o2"][:].rearrange("p (h e) -> p h e", e=64),
               in0=srow["o"][:].rearrange("p (h e) -> p h e", e=64),
               in1=s16["r"][:].unsqueeze(2).broadcast_to([1, 16, 64]), op=ALU.mult)
            OP("dve", "tensor_tensor", ["srow_o2", "gatt"], ["srow_b"], out=srb[:], in0=srow["o2"][:], in1=gatt[0:1, :],
               op=ALU.mult)
            DMA("sp", O_scr[1152 + bi:1153 + bi, 0:1024], srb[:], ["srow_b"], ["Oscr_samp"])
        DMA("sp", O_scr[0:1152, 0:1024].rearrange("(c p) f -> p c f", p=128), Oatt[:], [("Oatt", i) for i in range(9)],
            ["Oscr_att"])
        P.barrier()
        stB.close()
        stAB.close()
        mark("B_done")
        stC = contextlib.ExitStack()
        def sbC(name, shape, dt=F32):
            return stC.enter_context(nc.sbuf_tensor(name, list(shape), dt))
        i64 = ident[0:64, 0:64]
        ones64 = sbC("ones64", [64, 64], BF16)
        OP("pool", "memset", [], ["ones64"], ones64[:], 1.0)
        identf = sbC("identf", [64, 64])
        OP("dve", "tensor_copy", ["ident"], ["identf"], out=identf[:], in_=ident[0:64, 0:64])
        lstage = sbC("lstage", [128, 2, 1024])
        DMA("sp", lstage[0:64, 0, :], rw_w_up[:, :], [], ["lstage"])
        DMA("sp", lstage[0:64, 1, :], rw_a_up[:, :], ["lstage"], ["lstage"])
        lora_w = sbC("lora_w", [64, 1024], BF16)
        lora_a = sbC("lora_a", [64, 1024], BF16)
        OP("dve", "tensor_copy", ["lstage"], ["lora_w"], out=lora_w[:], in_=lstage[0:64, 0, :])
        OP("dve", "tensor_copy", ["lstage"], ["lora_a"], out=lora_a[:], in_=lstage[0:64, 1, :])
        gupb = sbC("gupb", [128, 2, 1024], BF16)
        OP("pool", "memset", [], ["gupb"], gupb[:, 1, :], 0.0)
        DMA("sp", lstage[:, 0, :], rw_g_up[0:128, :], ["lstage"], ["lstage"])
        DMA("sp", lstage[0:32, 1, :], rw_g_up[128:160, :], ["lstage"], ["lstage"])
        OP("dve", "tensor_copy", ["lstage", "gupb"], ["gupb"], out=gupb[:, 0, :], in_=lstage[:, 0, :])
        OP("dve", "tensor_copy", ["lstage", "gupb"], ["gupb"], out=gupb[0:32, 1, :], in_=lstage[0:32, 1, :])
        colv = {}
        for nm, apd in (("w0", rw_w0), ("a0", rw_a0), ("kk", rw_k_k), ("ka", rw_k_a), ("rk", rw_r_k)):
            tcol = sbC("col_" + nm, [64, 16])
            DMA("sp", tcol[:], apd.rearrange("(h p) -> p h", p=64), [], ["col_" + nm], allow_slow_non_contiguous=True)
            colv[nm] = tcol
        lnw2 = sbC("lnw2", [128, 4, 128])
        lnb2 = sbC("lnb2", [128, 4, 128])
        for t_ in range(2):
            DMA("pool", lnw2[t_ * 64:(t_ + 1) * 64, :, :],
                rw_lnx_w[0:1, :].rearrange("o (q t f) -> o t q f", t=2, f=128)[:, t_].partition_broadcast(64), [], ["lnw2"])
            DMA("pool", lnb2[t_ * 64:(t_ + 1) * 64, :, :],
                rw_lnx_b[0:1, :].rearrange("o (q t f) -> o t q f", t=2, f=128)[:, t_].partition_broadcast(64), [], ["lnb2"])
        colv2 = {}
        for nm, apd in (("w0", rw_w0), ("a0", rw_a0), ("kk", rw_k_k), ("ka", rw_k_a), ("rk", rw_r_k)):
            tcol = sbC("col2_" + nm, [128, 4, 2])
            for t_ in range(2):
                for hi_ in range(2):
                    DMA("sp", tcol[t_ * 64:(t_ + 1) * 64, :, hi_],
                        apd.rearrange("(q t hi p) -> t hi p q", t=2, hi=2, p=64)[t_, hi_], ["col2_" + nm], ["col2_" + nm],
                        allow_slow_non_contiguous=True)
            colv2[nm] = tcol
        lora_w2 = sbC("lora_w2", [64, 4, 2, 2, 64], BF16)
        lora_a2 = sbC("lora_a2", [64, 4, 2, 2, 64], BF16)
        for q_ in range(4):
            for t_ in range(2):
                DMA("pool", lora_w2[:, q_, :, t_, :],
                    rw_w_up.rearrange("l (q t hi c) -> l q t hi c", t=2, hi=2, c=64)[:, q_, t_], [], ["lora_w2"])
                DMA("pool", lora_a2[:, q_, :, t_, :],
                    rw_a_up.rearrange("l (q t hi c) -> l q t hi c", t=2, hi=2, c=64)[:, q_, t_], [], ["lora_a2"])
        bd = sbC("bd", [128, 128], BF16)
        OP("pool", "memset", [], ["bd"], bd[:], 0.0)
        OP("pool", "memset", ["bd"], ["bd"], bd[0:64, 0:64], 1.0)
        OP("pool", "memset", ["bd"], ["bd"], bd[64:128, 64:128], 1.0)
        identf2 = sbC("identf2", [128, 64])
        OP("dve", "tensor_copy", ["ident"], ["identf2"], out=identf2[0:64, :], in_=ident[0:64, 0:64])
        OP("dve", "tensor_copy", ["ident", "identf2"], ["identf2"], out=identf2[64:128, :], in_=ident[64:128, 64:128])
        trim2 = sbC("trim2", [128, 320], BF16)
        DMA("sp", trim2[0:64, :], trim_d[:, :], [], ["trim2"])
        DMA("sp", trim2[64:128, :], trim_d[:, :], ["trim2"], ["trim2"])
        rm2 = sbC("rm2", [128, 512])
        OP("pool", "memset", [], ["rm2"], rm2[:], 1.0)
        OP("pool", "memset", ["rm2"], ["rm2"], rm2[:].rearrange("p (n l) -> p n l", l=64)[:, :, 0:1], 0.0)
        NTC = 2052
        TW = sbC("TW", [64, NTC], BF16)
        AD = sbC("AD", [64, NTC], BF16)
        SG0 = sbC("SG0", [128, NTC], BF16)
        SG1 = sbC("SG1", [128, NTC], BF16)
        OP("pool", "memset", [], ["SG1"], SG1[:], 0.0)
        la = sbC("la", [64, 512])
        lb = sbC("lb", [64, 512])
        lg0 = sbC("lg0", [128, 512])
        lg1 = sbC("lg1", [32, 512])
        for (t0, n) in [(0, 512), (512, 512), (1024, 512), (1536, 512), (2048, 4)]:
            DMA("sp", la[:, 0:n], XS[3072:3136, t0:t0 + n], [("XS", 24, t0)], ["la"])
            DMA("sp", lb[:, 0:n], XS[3136:3200, t0:t0 + n], [("XS", 24, t0)], ["lb"])
            DMA("sp", lg0[:, 0:n], XS[3200:3328, t0:t0 + n], [("XS", 25, t0)], ["lg0"])
            DMA("sp", lg1[:, 0:n], XS[3328:3360, t0:t0 + n], [("XS", 26, t0)], ["lg1"])
            OP("act", "activation", ["la"], ["TW"], out=TW[:, t0:t0 + n], in_=la[:, 0:n], func=AF.Tanh)
            OP("act", "activation", ["lb"], ["AD"], out=AD[:, t0:t0 + n], in_=lb[:, 0:n], func=AF.Copy)
            OP("act", "activation", ["lg0"], ["SG0"], out=SG0[:, t0:t0 + n], in_=lg0[:, 0:n], func=AF.Sigmoid)
            OP("act", "activation", ["lg1", "SG1"], ["SG1"], out=SG1[0:32, t0:t0 + n], in_=lg1[:, 0:n], func=AF.Sigmoid)
        stC2 = contextlib.ExitStack()
        sbC_outer = sbC
        def sbC(name, shape, dt=F32):
            return stC2.enter_context(nc.sbuf_tensor(name, list(shape), dt))
        W = {nm: sbC("w_" + nm, [128, 512]) for nm in
             ("r32", "k32", "v32", "lw", "asig", "kk", "rn", "t1", "keff", "bvec", "cum", "ep", "em", "ea")}
        kk2 = sbC("w_kk2", [128, 512], BF16)
        wkvo = sbC("wkvo", [128, 2, 64])
        bufs = []
        for par in range(2):
            d = {}
            d["AR"] = sbC("AR%d" % par, [128, 2, 32, 128], BF16)
            d["KT"] = sbC("KTt%d" % par, [128, 2, 32, 64], BF16)
            d["BT"] = sbC("BTt%d" % par, [128, 2, 32, 64], BF16)
            d["VT"] = sbC("VTt%d" % par, [128, 2, 32, 64], BF16)
            d["RK"] = sbC("RK%d" % par, [128, 2, 2048], BF16)
            d["WL"] = sbC("WL%d" % par, [128, 2, 32])
            d["STf"] = sbC("STf%d" % par, [128, 2, 64])
            d["STb"] = sbC("STb%d" % par, [128, 2, 64], BF16)
            d["MBs"] = sbC("MBs%d" % par, [128, 2, 192], BF16)
            d["MKs"] = sbC("MKs%d" % par, [128, 2, 192], BF16)
            d["MNs"] = sbC("MNs%d" % par, [128, 2, 128], BF16)
            d["Uf"] = sbC("Uf%d" % par, [128, 2, 64])
            d["Ub"] = sbC("Ub%d" % par, [128, 2, 64], BF16)
            d["PPs"] = [sbC("PPs%d_%d" % (par, i), [128, 2, 128], BF16) for i in range(2)]
            d["tmpS"] = sbC("tmpS%d" % par, [128, 2, 64])
            d["yn"] = sbC("yn%d" % par, [128, 2, 64])
            d["st"] = sbC("st%d" % par, [128, 12])
            d["jk"] = sbC("jk%d" % par, [128, 64])
            d["Ost"] = sbC("Ost%d" % par, [128, 18, 128], BF16)
            bufs.append(d)

        PZW = (6, 384)
        PZA = (7, 384)
        PSS = (5, 260)

        def prep_block(q4, par, tb):
            B = bufs[par]
            kAR, kKT, kBT, kVT, kRK, kWL = [(x, par, tb) for x in ("AR", "KT", "BT", "VT", "RK", "WL")]
            t0 = tb * 512
            c0 = tb * 8
            for hi in range(2):
                hh = [4 * q4 + hi, 4 * q4 + 2 + hi]
                cs = slice(q4 * 2 + hi, q4 * 2 + hi + 1)
                cv_ = lambda nm: colv2[nm][:].rearrange("p q h -> p (q h)")[:, cs]
                for t_, h in enumerate(hh):
                    ps_ = slice(t_ * 64, (t_ + 1) * 64)
                    DMA("sp", W["r32"][ps_, :], XS[h * 64:(h + 1) * 64, t0:t0 + 512], [("XS", h // 2, t0)], ["r32"])
                    DMA("sp", W["k32"][ps_, :], XS[1024 + h * 64:1024 + (h + 1) * 64, t0:t0 + 512],
                        [("XS", 8 + h // 2, t0)], ["k32"])
                    DMA("sp", W["v32"][ps_, :], XS[2048 + h * 64:2048 + (h + 1) * 64, t0:t0 + 512],
                        [("XS", 16 + h // 2, t0)], ["v32"])
                yield
                for pc in range(4):
                    cc = slice(pc * 128, (pc + 1) * 128)
                    tc_ = slice(t0 + pc * 128, t0 + (pc + 1) * 128)
                    zw = pmm[PZW[0]][:, PZW[1]:PZW[1] + 128]
                    za = pmm[PZA[0]][:, PZA[1]:PZA[1] + 128]
                    OP("pe", "matmul", ["lora_w2", "TW"], [("pmm", PZW[0])], zw,
                       lhsT=lora_w2[:, q4, hi, :, :].rearrange("l t c -> l (t c)"), rhs=TW[:, tc_], start=True, stop=True)
                    OP("act", "activation", [("pmm", PZW[0]), "col2_w0"], ["lw"], out=W["lw"][:, cc], in_=zw,
                       func=AF.Sigmoid, bias=cv_("w0"))
                    OP("pe", "matmul", ["lora_a2", "AD"], [("pmm", PZA[0])], za,
                       lhsT=lora_a2[:, q4, hi, :, :].rearrange("l t c -> l (t c)"), rhs=AD[:, tc_], start=True, stop=True)
                    OP("act", "activation", [("pmm", PZA[0]), "col2_a0"], ["asig"], out=W["asig"][:, cc], in_=za,
                       func=AF.Sigmoid, bias=cv_("a0"))
                    yield
                OP("dve", "tensor_scalar", ["lw"], ["lw"], W["lw"][:], W["lw"][:], -0.6065306597126334, None, ALU.mult)
                OP("dve", "tensor_scalar", ["k32", "col2_kk"], ["kk"], W["kk"][:], W["k32"][:], cv_("kk"), None, ALU.mult)
                OP("dve", "tensor_tensor", ["kk"], ["kk2"], out=kk2[:], in0=W["kk"][:], in1=W["kk"][:], op=ALU.mult)
                yield
                for pc in range(4):
                    cc = slice(pc * 128, (pc + 1) * 128)
                    zs = pmm[PSS[0]][:, PSS[1]:PSS[1] + 128]
                    OP("pe", "matmul", ["bd", "kk2"], [("pmm", PSS[0])], zs, lhsT=bd[:], rhs=kk2[:, cc], start=True,
                       stop=True)
                    OP("dve", "tensor_scalar_max", [("pmm", PSS[0])], ["rn"], out=W["rn"][:, cc], in0=zs, scalar1=1e-24)
                    yield
                OP("act", "activation", ["rn"], ["rn"], out=W["rn"][:], in_=W["rn"][:], func=AF.Ln)
                OP("act", "activation", ["rn"], ["rn"], out=W["rn"][:], in_=W["rn"][:], func=AF.Exp, scale=-0.5)
                OP("dve", "tensor_tensor", ["kk", "rn"], ["kk"], out=W["kk"][:], in0=W["kk"][:], in1=W["rn"][:],
                   op=ALU.mult)
                yield
                OP("dve", "tensor_scalar", ["asig", "col2_ka"], ["t1"], W["t1"][:], W["asig"][:], -1.0, cv_("ka"),
                   ALU.add, ALU.mult)
                OP("dve", "scalar_tensor_tensor", ["t1", "k32"], ["keff"], out=W["keff"][:], in0=W["t1"][:],
                   scalar=1.0, in1=W["k32"][:], op0=ALU.add, op1=ALU.mult)
                OP("dve", "tensor_tensor", ["kk", "asig"], ["bvec"], out=W["bvec"][:], in0=W["kk"][:],
                   in1=W["asig"][:], op=ALU.mult)
                yield
                OP("dve", "tensor_tensor_scan", ["rm2", "lw"], ["cum"], out=W["cum"][:], data0=rm2[:],
                   data1=W["lw"][:], initial=0.0, op0=ALU.mult, op1=ALU.add)
                OP("act", "activation", ["cum"], ["ep"], out=W["ep"][:], in_=W["cum"][:], func=AF.Exp)
                OP("act", "activation", ["cum"], ["em"], out=W["em"][:], in_=W["cum"][:], func=AF.Exp, scale=-1.0)
                OP("dve", "tensor_tensor", ["cum", "lw"], ["ea"], out=W["ea"][:], in0=W["cum"][:], in1=W["lw"][:],
                   op=ALU.subtract)
                OP("act", "activation", ["ea"], ["ea"], out=W["ea"][:], in_=W["ea"][:], func=AF.Exp)
                yield
                v3 = lambda a: a[:].rearrange("p (n l) -> p n l", l=64)
                OP("dve", "tensor_tensor", ["r32", "ep"], [kAR], out=B["AR"][:, hi, c0:c0 + 8, 64:128],
                   in0=v3(W["r32"]), in1=v3(W["ep"]), op=ALU.mult)
                OP("dve", "scalar_tensor_tensor", ["kk", "ea"], [kAR], out=B["AR"][:, hi, c0:c0 + 8, 0:64],
                   in0=v3(W["kk"]), scalar=-1.0, in1=v3(W["ea"]), op0=ALU.mult, op1=ALU.mult)
                OP("pool", "tensor_tensor", ["keff", "em"], [kKT], out=B["KT"][:, hi, c0:c0 + 8, :],
                   in0=v3(W["keff"]), in1=v3(W["em"]), op=ALU.mult)
                yield
                OP("pool", "tensor_tensor", ["bvec", "em"], [kBT], out=B["BT"][:, hi, c0:c0 + 8, :],
                   in0=v3(W["bvec"]), in1=v3(W["em"]), op=ALU.mult)
                OP("act", "activation", ["v32"], [kVT], out=B["VT"][:, hi, c0:c0 + 8, :], in_=v3(W["v32"]),
                   func=AF.Copy)
                OP("dve", "tensor_copy", ["ep"], [kWL], out=B["WL"][:, hi, c0:c0 + 8], in_=v3(W["ep"])[:, :, 63])
                OP("dve", "scalar_tensor_tensor", ["r32", "col2_rk", "keff"], [kRK],
                   out=B["RK"][:, hi, t0:t0 + 512], in0=W["r32"][:], scalar=cv_("rk"), in1=W["keff"][:],
                   op0=ALU.mult, op1=ALU.mult)
                yield

        m6 = trim2[:, 0:192].unsqueeze(1).broadcast_to([128, 2, 192])
        ml = trim2[:, 192:320].unsqueeze(1).broadcast_to([128, 2, 128])
        def _v(bank, c0, ncol, c):
            return pmm[bank][:, c0:c0 + ncol].rearrange("p (h c) -> p h c", c=c)
        PV = [
            (_v(0, 0, 384, 192), _v(1, 0, 384, 192), _v(2, 0, 256, 128), _v(2, 256, 128, 64), _v(3, 0, 128, 64),
             _v(4, 0, 256, 128), _v(5, 0, 128, 64), pmm[5][:, 128:256], pmm[5][:, 256:258],
             ("pmm", 0), ("pmm", 1), ("pmm", 2), ("pmm", 2), ("pmm", 3), ("pmm", 4), ("pmm", 5), ("pmm", 5), ("pmm", 5)),
            (_v(6, 0, 384, 192), _v(7, 0, 384, 192), _v(3, 128, 256, 128), _v(3, 384, 128, 64), _v(0, 384, 128, 64),
             _v(4, 256, 256, 128), _v(1, 384, 128, 64), pmm[2][:, 384:512], pmm[5][:, 258:260],
             ("pmm", 6), ("pmm", 7), ("pmm", 3), ("pmm", 3), ("pmm", 0), ("pmm", 4), ("pmm", 1), ("pmm", 2), ("pmm", 5)),
        ]
        LNX_EPS = 64 * 1e-5
        HALF = [slice(0, 64), slice(64, 128)]
        TH = [(t_, hi) for t_ in range(2) for hi in range(2)]

        def chunk_step(q4, n, par):
            B = bufs[par]
            K = lambda x: (x, par)
            pMB, pMK, pMN, pS, pZ, pPP, pY, pG, pBo, kMB, kMK, kMN, kS, kZ, kPP, kY, kG, kBo = PV[par]
            AR, KT_, BT_, VT_ = B["AR"], B["KT"], B["BT"], B["VT"]
            prepk = [(x, par, n // 8) for x in ("AR", "KT", "BT", "VT")]
            for t_, hi in TH:
                p_ = HALF[t_]
                idh = ident[p_, p_]
                OP("pe", "matmul", prepk, [kMB], pMB[p_, hi, 0:128], lhsT=BT_[p_, hi, n, :], rhs=AR[p_, hi, n, :],
                   start=True, stop=True)
                OP("pe", "matmul", prepk + ["ident"], [kMB], pMB[p_, hi, 128:192], lhsT=BT_[p_, hi, n, :], rhs=idh,
                   start=True, stop=True)
                OP("pe", "matmul", prepk, [kMK], pMK[p_, hi, 0:128], lhsT=KT_[p_, hi, n, :], rhs=AR[p_, hi, n, :],
                   start=True, stop=True)
                OP("pe", "matmul", prepk + ["ident"], [kMK], pMK[p_, hi, 128:192], lhsT=KT_[p_, hi, n, :], rhs=idh,
                   start=True, stop=True)
                OP("pe", "matmul", prepk, [kMN], pMN[p_, hi, 0:64], lhsT=AR[p_, hi, n, 0:64], rhs=BT_[p_, hi, n, :],
                   start=True, stop=True)
                OP("pe", "matmul", prepk + ["ident"], [kMN], pMN[p_, hi, 64:128], lhsT=VT_[p_, hi, n, :], rhs=idh,
                   start=True, stop=True)
            yield
            OP("dve", "tensor_tensor", [kMB, "trim2"], [K("MBs")], out=B["MBs"][:], in0=pMB, in1=m6, op=ALU.mult)
            OP("dve", "tensor_tensor", [kMK, "trim2"], [K("MKs")], out=B["MKs"][:], in0=pMK, in1=m6, op=ALU.mult)
            OP("dve", "tensor_tensor", [kMN, "trim2"], [K("MNs")], out=B["MNs"][:], in0=pMN, in1=ml, op=ALU.mult)
            yield
            for t_, hi in TH:
                p_ = HALF[t_]
                OP("pe", "matmul", prepk + [K("STb")], [kZ], pZ[p_, hi, :], lhsT=AR[p_, hi, n, 0:64],
                   rhs=B["STb"][p_, hi, :], start=True, stop=False)
                OP("pe", "matmul", [K("MKs"), K("MNs")], [kZ], pZ[p_, hi, :], lhsT=B["MKs"][p_, hi, 0:64],
                   rhs=B["MNs"][p_, hi, 64:128], start=False, stop=True)
            OP("act", "activation", [kZ], [K("Uf")], out=B["Uf"][:], in_=pZ, func=AF.Copy)
            OP("act", "activation", [K("Uf")], [K("Ub")], out=B["Ub"][:], in_=B["Uf"][:], func=AF.Copy)
            yield
            PT = lambda p_, hi: B["MBs"][p_, hi, 0:64]
            Pm = lambda p_, hi: B["MNs"][p_, hi, 0:64]
            pkeys = [K("MBs"), K("MNs")]
            for it in range(6):
                for t_, hi in TH:
                    p_ = HALF[t_]
                    OP("pe", "matmul", pkeys + [K("Ub")], [kZ], pZ[p_, hi, :], lhsT=PT(p_, hi), rhs=B["Ub"][p_, hi, :],
                       start=True, stop=True)
                yield
                OP("dve", "tensor_tensor", [kZ, K("Uf")], [K("Uf")], out=B["Uf"][:], in0=pZ, in1=B["Uf"][:],
                   op=ALU.add)
                OP("act", "activation", [K("Uf")], [K("Ub")], out=B["Ub"][:], in_=B["Uf"][:], func=AF.Copy)
                if it < 5:
                    for t_, hi in TH:
                        p_ = HALF[t_]
                        OP("pe", "matmul", pkeys, [kPP], pPP[p_, hi, 0:64], lhsT=Pm(p_, hi), rhs=PT(p_, hi), start=True,
                           stop=True)
                        OP("pe", "matmul", pkeys, [kPP], pPP[p_, hi, 64:128], lhsT=PT(p_, hi), rhs=Pm(p_, hi), start=True,
                           stop=True)
                    pp = B["PPs"][it % 2]
                    yield
                    OP("act", "activation", [kPP], [K("PPs%d" % (it % 2))], out=pp[:], in_=pPP, func=AF.Copy)
                    PT = lambda p_, hi, pp=pp: pp[p_, hi, 0:64]
                    Pm = lambda p_, hi, pp=pp: pp[p_, hi, 64:128]
                    pkeys = [K("PPs%d" % (it % 2))]
            if n >= 14:
                ci = n - 14
                for t_, hi in TH:
                    p_ = HALF[t_]
                    OP("pe", "matmul", prepk + [K("STb")], [kY], pY[p_, hi, :], lhsT=AR[p_, hi, n, 64:128],
                       rhs=B["STb"][p_, hi, :], start=True, stop=False)
                    OP("pe", "matmul", [K("MBs"), K("Ub")], [kY], pY[p_, hi, :], lhsT=B["MBs"][p_, hi, 64:128],
                       rhs=B["Ub"][p_, hi, :], start=False, stop=False)
                    OP("pe", "matmul", [K("MKs"), K("MNs")], [kY], pY[p_, hi, :], lhsT=B["MKs"][p_, hi, 64:128],
                       rhs=B["MNs"][p_, hi, 64:128], start=False, stop=True)
                for t_ in range(2):
                    p_ = HALF[t_]
                    gc = slice((4 * q4 + 2 * t_) * 64, (4 * q4 + 2 * t_) * 64 + 128)
                    OP("pe", "matmul", ["SG0", "gupb"], [kG], pG[p_, :], lhsT=SG0[:, n * 64:(n + 1) * 64],
                       rhs=gupb[:, 0, gc], start=True, stop=False)
                    OP("pe", "matmul", ["SG1", "gupb"], [kG], pG[p_, :], lhsT=SG1[:, n * 64:(n + 1) * 64],
                       rhs=gupb[:, 1, gc], start=False, stop=True)
                for t_, hi in TH:
                    p_ = HALF[t_]
                    OP("pe", "matmul", [("RK", par, n // 8), "bd"], [kBo], pBo[p_, hi:hi + 1],
                       lhsT=B["RK"][p_, hi, n * 64:(n + 1) * 64], rhs=bd[p_, t_ * 64:t_ * 64 + 1], start=True, stop=True)
                yield
                stt_ = B["st"]
                ks = K("st")
                OP("dve", "tensor_reduce", [kY], [ks], out=stt_[:, 0:2], in_=pY, axis=AX.X, op=ALU.add)
                for hi in range(2):
                    OP("act", "activation", [kY], [K("jk"), ks], out=B["jk"][:], in_=pY[:, hi, :],
                       func=AF.Square, accum_out=stt_[:, 2 + hi:3 + hi])
                OP("act", "activation", [kBo], [ks], out=stt_[:, 10:12], in_=pBo, func=AF.Copy)
                OP("dve", "tensor_scalar", [ks], [ks], stt_[:, 4:6], stt_[:, 0:2], 1.0 / 64, None, ALU.mult)
                OP("dve", "tensor_tensor", [ks], [ks], out=stt_[:, 6:8], in0=stt_[:, 4:6], in1=stt_[:, 4:6], op=ALU.mult)
                OP("dve", "scalar_tensor_tensor", [ks], [ks], out=stt_[:, 8:10], in0=stt_[:, 2:4], scalar=1.0 / 64,
                   in1=stt_[:, 6:8], op0=ALU.mult, op1=ALU.subtract)
                OP("dve", "tensor_scalar", [ks], [ks], stt_[:, 8:10], stt_[:, 8:10], LNX_EPS, None, ALU.add)
                OP("act", "activation", [ks], [ks], out=stt_[:, 8:10], in_=stt_[:, 8:10], func=AF.Ln)
                OP("act", "activation", [ks], [ks], out=stt_[:, 8:10], in_=stt_[:, 8:10], func=AF.Exp, scale=-0.5)
                for hi in range(2):
                    OP("dve", "tensor_scalar", [kY, ks], [K("yn")], B["yn"][:, hi, :], pY[:, hi, :],
                       stt_[:, 4 + hi:5 + hi], stt_[:, 8 + hi:9 + hi], ALU.subtract, ALU.mult)
                ynf = B["yn"][:].rearrange("p h c -> p (h c)")
                OP("pool", "tensor_tensor", [K("yn"), "lnw2"], [K("yn")], out=ynf, in0=ynf, in1=lnw2[:, q4, :], op=ALU.mult)
                OP("pool", "tensor_tensor", [K("yn"), "lnb2"], [K("yn")], out=ynf, in0=ynf, in1=lnb2[:, q4, :], op=ALU.add)
                for hi in range(2):
                    OP("dve", "scalar_tensor_tensor", [K("MNs"), ks, K("yn")], [K("yn")], out=B["yn"][:, hi, :],
                       in0=B["MNs"][:, hi, 64:128], scalar=stt_[:, 10 + hi:11 + hi], in1=B["yn"][:, hi, :],
                       op0=ALU.mult, op1=ALU.add)
                OP("dve", "tensor_tensor", [K("yn"), kG], [K("Ost")], out=B["Ost"][:, ci, :], in0=ynf, in1=pG, op=ALU.mult)
            for t_, hi in TH:
                p_ = HALF[t_]
                OP("pe", "matmul", [K("MBs"), K("Ub")], [kS], pS[p_, hi, :], lhsT=B["MBs"][p_, hi, 128:192],
                   rhs=B["Ub"][p_, hi, :], start=True, stop=False)
                OP("pe", "matmul", [K("MKs"), K("MNs")], [kS], pS[p_, hi, :], lhsT=B["MKs"][p_, hi, 128:192],
                   rhs=B["MNs"][p_, hi, 64:128], start=False, stop=True)
            yield
            OP("dve", "tensor_tensor", [kS, K("STf")], [K("tmpS")], out=B["tmpS"][:], in0=pS, in1=B["STf"][:],
               op=ALU.add)
            OP("pool", "tensor_tensor", [K("tmpS"), ("WL", par, n // 8)], [K("STf")], out=B["STf"][:], in0=B["tmpS"][:],
               in1=B["WL"][:, :, n:n + 1].broadcast_to([128, 2, 64]), op=ALU.mult)
            OP("act", "activation", [K("STf")], [K("STb")], out=B["STb"][:], in_=B["STf"][:], func=AF.Copy)

        def chain_block(q4, par, tb):
            for n in range(tb * 8, tb * 8 + 8):
                yield from chunk_step(q4, n, par)

        def prep_seq(q0, tb):
            for par in range(2):
                yield from prep_block(q0 + par, par, tb)

        def round_robin(gens):
            live = list(gens)
            while live:
                for g in list(live):
                    try:
                        next(g)
                    except StopIteration:
                        live.remove(g)

        for q0 in range(0, 4, 2):
            for par in range(2):
                B = bufs[par]
                OP("pool", "memset", [], [("STf", par)], B["STf"][:], 0.0)
                OP("pool", "memset", [], [("STb", par)], B["STb"][:], 0.0)
            round_robin([prep_seq(q0, 0)])
            for tb in range(4):
                gens = [chain_block(q0 + par, par, tb) for par in range(2)]
                if tb < 3:
                    gens.append(prep_seq(q0, tb + 1))
                round_robin(gens)
            for q4 in (q0, q0 + 1):
                par = q4 % 2
                B = bufs[par]
                for t_, hi in TH:
                    p_ = HALF[t_]
                    OP("pe", "matmul", [("STf", par), "identf2"], [("pmm", 3)], pmm[3][p_, hi * 64:(hi + 1) * 64],
                       lhsT=B["STf"][p_, hi, :], rhs=identf2[p_, :], start=True, stop=True)
                OP("dve", "tensor_copy", [("pmm", 3)], ["wkvo"], out=wkvo[:].rearrange("p h c -> p (h c)"),
                   in_=pmm[3][:, 0:128])
                for t_ in range(2):
                    p_ = HALF[t_]
                    h0 = 4 * q4 + 2 * t_
                    DMA("sp", pwkv_out[h0:h0 + 2, :, :].rearrange("h i j -> i h j"), wkvo[p_, :, :], ["wkvo"],
                        [("pwkv", q4, t_)])
                    DMA("sp", O_scr[0:1152, 1024 + h0 * 64:1024 + h0 * 64 + 128].rearrange("(c p) f -> p c f", p=64),
                        B["Ost"][p_, :, :], [("Ost", par)], [("Oscr_rw", 2 * q4 + t_)])
        P.barrier()
        stC2.close()
        sbC = sbC_outer
        mark("C_prompt_done")
        XC = {}
        for nm, r0 in (("r", 0), ("k", 1024), ("v", 2048)):
            tcx = sbC("xc_" + nm, [64, 16, 4])
            DMA("sp", tcx[:], XS[r0:r0 + 1024, 2048:2052].rearrange("(h p) b -> p h b", p=64),
                [("XS", ci_, 2048) for ci_ in range(r0 // 128, r0 // 128 + 8)], ["xc_" + nm])
            XC[nm] = tcx
        lnwc = sbC("lnwc", [64, 16])
        lnbc = sbC("lnbc", [64, 16])
        DMA("sp", lnwc[:], rw_lnx_w_c.rearrange("(h p) -> p h", p=64), [], ["lnwc"], allow_slow_non_contiguous=True)
        DMA("sp", lnbc[:], rw_lnx_b_c.rearrange("(h p) -> p h", p=64), [], ["lnbc"], allow_slow_non_contiguous=True)
        onesf = sbC("onesf", [64, 64])
        OP("pool", "memset", [], ["onesf"], onesf[:], 1.0)
        SW = {nm: sbC("sw_" + nm, [64, 16, 4]) for nm in
              ("lw", "a", "kk", "kk2", "rn", "t1", "keff", "b", "rk", "g", "y", "y2", "mean", "var", "yn", "bon", "o")}
        vec5 = sbC("vec5", [64, 16, 4, 5])
        ob16 = sbC("ob16", [64, 16, 4], BF16)
        bc16 = lambda t: t[:, :].unsqueeze(2).broadcast_to([64, 16, 4])
        pz = pmm[0][0:64, 0:64].rearrange("p (h b) -> p h b", b=4)
        pz2 = pmm[1][0:64, 0:64].rearrange("p (h b) -> p h b", b=4)
        pz3 = pmm[2][0:64, 0:64].rearrange("p (h b) -> p h b", b=4)
        for h in range(16):
            hc = slice(h * 64, (h + 1) * 64)
            OP("pe", "matmul", ["lora_w", "TW"], [("pmm", 0)], pz[:, h, :], lhsT=lora_w[:, hc], rhs=TW[:, 2048:2052],
               start=True, stop=True)
            OP("pe", "matmul", ["lora_a", "AD"], [("pmm", 1)], pz2[:, h, :], lhsT=lora_a[:, hc], rhs=AD[:, 2048:2052],
               start=True, stop=True)
            OP("pe", "matmul", ["SG0", "gupb"], [("pmm", 2)], pz3[:, h, :], lhsT=gupb[:, 0, hc], rhs=SG0[:, 2048:2052],
               start=True, stop=False)
            OP("pe", "matmul", ["SG1", "gupb"], [("pmm", 2)], pz3[:, h, :], lhsT=gupb[:, 1, hc], rhs=SG1[:, 2048:2052],
               start=False, stop=True)
        OP("dve", "tensor_tensor", [("pmm", 0), "col_w0"], ["sw_lw"], out=SW["lw"][:], in0=pz, in1=bc16(colv["w0"]),
           op=ALU.add)
        OP("act", "activation", ["sw_lw"], ["sw_lw"], out=SW["lw"][:], in_=SW["lw"][:], func=AF.Sigmoid)
        OP("act", "activation", ["sw_lw"], ["vec5"], out=vec5[:, :, :, 1], in_=SW["lw"][:], func=AF.Exp,
           scale=-0.6065306597126334)
        OP("dve", "tensor_tensor", [("pmm", 1), "col_a0"], ["sw_a"], out=SW["a"][:], in0=pz2, in1=bc16(colv["a0"]),
           op=ALU.add)
        OP("act", "activation", ["sw_a"], ["sw_a"], out=SW["a"][:], in_=SW["a"][:], func=AF.Sigmoid)
        OP("act", "activation", [("pmm", 2)], ["sw_g"], out=SW["g"][:], in_=pz3, func=AF.Copy)
        OP("dve", "tensor_tensor", ["xc_k", "col_kk"], ["sw_kk"], out=SW["kk"][:], in0=XC["k"][:], in1=bc16(colv["kk"]),
           op=ALU.mult)
        OP("dve", "tensor_tensor", ["sw_kk"], ["sw_kk2"], out=SW["kk2"][:], in0=SW["kk"][:], in1=SW["kk"][:], op=ALU.mult)
        fl = lambda t: t[:].rearrange("p h b -> p (h b)")
        OP("pe", "matmul", ["onesf", "sw_kk2"], [("pmm", 0)], pmm[0][0:64, 0:64], lhsT=onesf[:], rhs=fl(SW["kk2"]),
           start=True, stop=True)
        OP("dve", "tensor_scalar_max", [("pmm", 0)], ["sw_rn"], out=fl(SW["rn"]), in0=pmm[0][0:64, 0:64], scalar1=1e-24)
        OP("act", "activation", ["sw_rn"], ["sw_rn"], out=SW["rn"][:], in_=SW["rn"][:], func=AF.Ln)
        OP("act", "activation", ["sw_rn"], ["sw_rn"], out=SW["rn"][:], in_=SW["rn"][:], func=AF.Exp, scale=-0.5)
        OP("dve", "tensor_tensor", ["sw_kk", "sw_rn"], ["sw_kk"], out=SW["kk"][:], in0=SW["kk"][:], in1=SW["rn"][:],
           op=ALU.mult)
        OP("dve", "tensor_scalar", ["sw_kk"], ["vec5"], vec5[:, :, :, 0], SW["kk"][:], -1.0, None, ALU.mult)
        OP("dve", "tensor_tensor", ["sw_kk", "sw_a", "vec5"], ["vec5"], out=vec5[:, :, :, 2], in0=SW["kk"][:],
           in1=SW["a"][:], op=ALU.mult)
        OP("dve", "tensor_scalar", ["sw_a"], ["sw_t1"], SW["t1"][:], SW["a"][:], -1.0, None, ALU.add)
        OP("dve", "tensor_tensor", ["sw_t1", "col_ka"], ["sw_t1"], out=SW["t1"][:], in0=SW["t1"][:], in1=bc16(colv["ka"]),
           op=ALU.mult)
        OP("dve", "scalar_tensor_tensor", ["sw_t1", "xc_k"], ["sw_keff"], out=SW["keff"][:], in0=SW["t1"][:], scalar=1.0,
           in1=XC["k"][:], op0=ALU.add, op1=ALU.mult)
        OP("dve", "tensor_copy", ["sw_keff", "vec5"], ["vec5"], out=vec5[:, :, :, 3], in_=SW["keff"][:])
        OP("dve", "tensor_copy", ["xc_r", "vec5"], ["vec5"], out=vec5[:, :, :, 4], in_=XC["r"][:])
        OP("dve", "tensor_tensor", ["xc_r", "sw_keff"], ["sw_rk"], out=SW["rk"][:], in0=XC["r"][:], in1=SW["keff"][:],
           op=ALU.mult)
        OP("dve", "tensor_tensor", ["sw_rk", "col_rk"], ["sw_rk"], out=SW["rk"][:], in0=SW["rk"][:], in1=bc16(colv["rk"]),
           op=ALU.mult)
        OP("pe", "matmul", ["onesf", "sw_rk"], [("pmm", 1)], pmm[1][0:64, 0:64], lhsT=onesf[:], rhs=fl(SW["rk"]),
           start=True, stop=True)
        OP("dve", "tensor_tensor", [("pmm", 1), "xc_v"], ["sw_bon"], out=fl(SW["bon"]), in0=pmm[1][0:64, 0:64],
           in1=fl(XC["v"]), op=ALU.mult)
        Sin = [sbC("Sin%d" % i, [64, 16, 64]) for i in range(2)]
        Sout = [sbC("Sout%d" % i, [64, 16, 64]) for i in range(2)]
        D5 = [sbC("D5_%d" % i, [64, 5, 64]) for i in range(2)]
        tm1 = sbC("tm1", [64, 64])
        sa = sbC("sa_s", [64, 1])
        k5 = 0
        for bi in range(4):
            sb_ = bi % 2
            DMA("sp", Sin[sb_][:], swkv_in[bi].rearrange("h i j -> i h j"), [], [("Sin", sb_)])
            for h in range(16):
                d5 = D5[k5 % 2]
                dk = ("D5", k5 % 2)
                pbk = 3 + (k5 % 2)
                pBC = pmm[pbk][0:64, 0:320].rearrange("p (v j) -> p v j", j=64)
                k5 += 1
                OP("dve", "tensor_tensor", ["identf", "vec5"], [dk], out=d5[:],
                   in0=identf[:].unsqueeze(1).broadcast_to([64, 5, 64]),
                   in1=vec5[:, h, bi, :].unsqueeze(2).broadcast_to([64, 5, 64]), op=ALU.mult)
                OP("pe", "matmul", ["onesf", dk], [("pmm", pbk)], pmm[pbk][0:64, 0:320], lhsT=onesf[:],
                   rhs=d5[:].rearrange("p v j -> p (v j)"), start=True, stop=True)
                S_ = Sin[sb_][:, h, :]
                So = Sout[sb_][:, h, :]
                OP("dve", "tensor_tensor", [("Sin", sb_), ("pmm", pbk)], ["tm1"], out=tm1[:], in0=pBC[:, 0, :], in1=S_,
                   op=ALU.mult)
                OP("dve", "tensor_reduce", ["tm1"], ["sa_s"], out=sa[:], in_=tm1[:], axis=AX.X, op=ALU.add)
                OP("dve", "tensor_tensor", [("Sin", sb_), ("pmm", pbk)], [("Sout", sb_)], out=So, in0=pBC[:, 1, :], in1=S_,
                   op=ALU.mult)
                OP("dve", "scalar_tensor_tensor", [("pmm", pbk), "sa_s", ("Sout", sb_)], [("Sout", sb_)], out=So,
                   in0=pBC[:, 2, :], scalar=sa[:, 0:1], in1=So, op0=ALU.mult, op1=ALU.add)
                OP("dve", "scalar_tensor_tensor", [("pmm", pbk), "xc_v", ("Sout", sb_)], [("Sout", sb_)], out=So,
                   in0=pBC[:, 3, :], scalar=XC["v"][:, h, bi:bi + 1], in1=So, op0=ALU.mult, op1=ALU.add)
                OP("dve", "tensor_tensor", [("pmm", pbk), ("Sout", sb_)], ["tm1"], out=tm1[:], in0=pBC[:, 4, :], in1=So,
                   op=ALU.mult)
                OP("dve", "tensor_reduce", ["tm1"], ["sw_y"], out=SW["y"][:, h, bi:bi + 1], in_=tm1[:], axis=AX.X,
                   op=ALU.add)
            DMA("pool", swkv_out[bi].rearrange("h i j -> i h j"), Sout[sb_][:], [("Sout", sb_)], [("swkv", bi)])
        OP("pe", "matmul", ["onesf", "sw_y"], [("pmm", 0)], pmm[0][0:64, 0:64], lhsT=onesf[:], rhs=fl(SW["y"]),
           start=True, stop=True)
        OP("dve", "tensor_scalar", [("pmm", 0)], ["sw_mean"], fl(SW["mean"]), pmm[0][0:64, 0:64], 1.0 / 64, None, ALU.mult)
        OP("dve", "tensor_tensor", ["sw_y", "sw_mean"], ["sw_yn"], out=SW["yn"][:], in0=SW["y"][:], in1=SW["mean"][:],
           op=ALU.subtract)
        OP("dve", "tensor_tensor", ["sw_yn"], ["sw_y2"], out=SW["y2"][:], in0=SW["yn"][:], in1=SW["yn"][:], op=ALU.mult)
        OP("pe", "matmul", ["onesf", "sw_y2"], [("pmm", 1)], pmm[1][0:64, 0:64], lhsT=onesf[:], rhs=fl(SW["y2"]),
           start=True, stop=True)
        OP("act", "activation", [("pmm", 1), "epsl"], ["sw_var"], out=fl(SW["var"]), in_=pmm[1][0:64, 0:64], func=AF.Ln,
           scale=1.0 / 64, bias=epsl[0:64, 0:1])
        OP("act", "activation", ["sw_var"], ["sw_var"], out=SW["var"][:], in_=SW["var"][:], func=AF.Exp, scale=-0.5)
        OP("dve", "tensor_tensor", ["sw_yn", "sw_var"], ["sw_yn"], out=SW["yn"][:], in0=SW["yn"][:], in1=SW["var"][:],
           op=ALU.mult)
        OP("dve", "tensor_tensor", ["sw_yn", "lnwc"], ["sw_yn"], out=SW["yn"][:], in0=SW["yn"][:], in1=bc16(lnwc),
           op=ALU.mult)
        OP("dve", "tensor_tensor", ["sw_yn", "lnbc"], ["sw_yn"], out=SW["yn"][:], in0=SW["yn"][:], in1=bc16(lnbc),
           op=ALU.add)
        OP("dve", "tensor_tensor", ["sw_yn", "sw_bon"], ["sw_yn"], out=SW["yn"][:], in0=SW["yn"][:], in1=SW["bon"][:],
           op=ALU.add)
        OP("dve", "tensor_tensor", ["sw_yn", "sw_g"], ["ob16"], out=ob16[:], in0=SW["yn"][:], in1=SW["g"][:], op=ALU.mult)
        for bi in range(4):
            DMA("sp", O_scr[1152 + bi, 1024:2048].rearrange("(h p) -> p h", p=64), ob16[:, :, bi], ["ob16"],
                ["Oscr_samp_rw"], allow_slow_non_contiguous=True)
        P.barrier()
        stC.close()
        mark("C_done")
        NT2 = 1280
        stD = contextlib.ExitStack()
        def sbD(name, shape, dt=F32):
            return stD.enter_context(nc.sbuf_tensor(name, list(shape), dt))
        OT = sbD("OT", [128, 16, NT2], BF16)
        otile = [sbD("otile%d" % i, [128, 2048], BF16) for i in range(2)]
        zt = sbD("zt", [128, 2048], BF16)
        OP("pool", "memset", [], ["zt"], zt[:], 0.0)
        if not HAVE_SAMPLE:
            DMA("sp", O_scr[1152:1280, :], zt[:], ["zt"], ["Oscr_samp"])
        else:
            DMA("sp", O_scr[1156:1280, :], zt[0:124, :], ["zt"], ["Oscr_samp_pad"])
        okeys = ["Oscr_att", "Oscr_samp", "Oscr_samp_pad", "Oscr_samp_rw"] + [("Oscr_rw", i) for i in range(8)]
        for i in range(10):
            ob = i % 2
            DMA("sp", otile[ob][:], O_scr[i * 128:(i + 1) * 128, :], okeys, [("otile", ob)])
            for g4 in range(4):
                pb = g4 % 2
                for j in range(4):
                    kc = g4 * 4 + j
                    OP("pe", "transpose", [("otile", ob), "ident"], ["pT%d" % pb], out=pT[pb][:, j, :],
                       in_=otile[ob][:, kc * 128:(kc + 1) * 128], identity=ident[:])
                OP("dve", "tensor_copy", ["pT%d" % pb], [("OT", i)], out=OT[:, g4 * 4:(g4 + 1) * 4, i * 128:(i + 1) * 128],
                   in_=pT[pb])
        wbf_d = [sbD("wbf_d%d" % i, [128, 16, 256], BF16) for i in range(3)]
        xres = [sbD("xres%d" % i, [128, 256]) for i in range(3)]
        xmo = [sbD("xmo%d" % i, [128, 256]) for i in range(3)]
        cnt = {"i": 0}
        tile_row0 = lambda i: (896 + i * 128) if i < 9 else 2048
        for sl in range(8):
            c0 = sl * 256
            b = sl % 3
            for half in range(2):
                DMA("pool", wbf_d[b][:, half * 8:(half + 1) * 8, :],
                    w_o[half * 1024:(half + 1) * 1024, c0:c0 + 256].rearrange("(k p) c -> p k c", p=128),
                    [], [("wbf_d", b, half)])
            for i in range(10):
                pi = cnt["i"] % 4
                xi = cnt["i"] % 3
                cnt["i"] += 1
                if cnt["i"] == 1:
                    for pf in range(2):
                        r0 = tile_row0(pf)
                        DMA("sp", xres[pf][:], xin[r0:r0 + 128, 0:256], [], [("xres", pf)])
                nxt = cnt["i"] + 1
                if nxt < 80:
                    r0 = tile_row0(nxt % 10)
                    cn = (nxt // 10) * 256
                    DMA("sp", xres[nxt % 3][:], xin[r0:r0 + 128, cn:cn + 256], [], [("xres", nxt % 3)])
                for kc in range(16):
                    OP("pe", "matmul", [("wbf_d", b, 0), ("wbf_d", b, 1), ("OT", i)], [("pmm", pi)], pmm[pi][:, 0:256],
                       lhsT=OT[:, kc, i * 128:(i + 1) * 128], rhs=wbf_d[b][:, kc, :], start=(kc == 0), stop=(kc == 15))
                OP("dve", "tensor_tensor", [("pmm", pi), ("xres", xi)], [("xmo", xi)], out=xmo[xi][:],
                   in0=pmm[pi][:, 0:256], in1=xres[xi][:], op=ALU.add)
                DMA("sp", XM[i * 128:(i + 1) * 128, c0:c0 + 256], xmo[xi][:], [("xmo", xi)], [("XM", i)])
        P.barrier()
        stD.close()
        mark("D_done")
        stE = contextlib.ExitStack()
        def sbE(name, shape, dt=F32):
            return stE.enter_context(nc.sbuf_tensor(name, list(shape), dt))
        aT = sbE("aT", [128, 44, 1032], BF16)
        OP("pool", "memset", [], ["aT_pad"], aT[:, :, 1028:1032], 0.0)
        gcol2 = sbE("gcol2", [128, 16])
        DMA("sp", gcol2[:], g_ffn.rearrange("(k p) -> p k", p=128), [], ["gcol2"], allow_slow_non_contiguous=True)
        cvec = {}
        for nm, apd in (("w0", conv_w[0, :]), ("w1", conv_w[1, :]), ("w2", conv_w[2, :]), ("b", conv_b)):
            tcv = sbE("cv_" + nm, [128, 88])
            DMA("sp", tcv[:], apd.rearrange("(j p) -> p j", p=128), [], ["cv_" + nm], allow_slow_non_contiguous=True)
            cvec[nm] = tcv
        SF = sbE("SF", [128, 88, 8])
        DMA("sp", SF[:], sffnT.rearrange("(j p) r b -> p j (r b)", p=128), [], ["SF"])
        Ulast = sbE("Ulast", [128, 88, 2])
        Usamp = sbE("Usamp", [128, 88, 4])
        stE1 = contextlib.ExitStack()
        def sbE1(name, shape, dt=F32):
            return stE1.enter_context(nc.sbuf_tensor(name, list(shape), dt))
        hT2 = sbE1("hT2", [128, 16, NT2], BF16)
        stE0 = contextlib.ExitStack()
        xt = [stE0.enter_context(nc.sbuf_tensor("xt_e%d" % i, [128, D], F32)) for i in range(2)]
        xn = [stE0.enter_context(nc.sbuf_tensor("xn_e%d" % i, [128, D], BF16)) for i in range(2)]
        for i in range(10):
            norm_transpose(XM[i * 128:(i + 1) * 128, :], gcol2, "gcol2", hT2[:, :, i * 128:(i + 1) * 128], ("hT2", i), i,
                           src_keys=[("XM", i)])
        P.barrier()
        stE0.close()
        hT2_all = [("hT2", i) for i in range(10)]
        wu_bf = [[sbE1("wu_bf%d_%d" % (i, j), [128, 16, 256], BF16) for j in range(2)] for i in range(2)]
        wv_st = sbE1("wv_st", [128, 16, 256])
        U = [sbE1("U%d" % i, [128, 1154]) for i in range(2)]
        Us = [sbE1("Us%d" % i, [128, 4]) for i in range(2)]
        cg = sbE1("cg", [128, 1024])
        cv = sbE1("cv", [128, 1024])
        cgs = sbE1("cgs", [128, 4])
        cvs = sbE1("cvs", [128, 4])
        ecnt = {"pm": 0}
        for j in range(44):
            jb = (j // 2) % 2
            jo = (j % 2) * 128
            for gv in range(2):
                col0 = gv * DFF + j * 128
                jj = gv * 44 + j
                if j % 2 == 0:
                    for half in range(2):
                        if gv == 0:
                            DMA("pool", wu_bf[jb][gv][:, half * 8:(half + 1) * 8, :],
                                w_up[half * 1024:(half + 1) * 1024, col0:col0 + 256].rearrange("(k p) c -> p k c", p=128),
                                [], [("wu_bf", jb, gv, half)])
                        else:
                            DMA("sp", wv_st[:, half * 8:(half + 1) * 8, :],
                                w_up[half * 1024:(half + 1) * 1024, col0:col0 + 256].rearrange("(k p) c -> p k c", p=128),
                                [], [("wv_st", half)])
                            OP("pool", "tensor_copy", [("wv_st", half)], [("wu_bf", jb, gv, half)],
                               out=wu_bf[jb][gv][:, half * 8:(half + 1) * 8, :], in_=wv_st[:, half * 8:(half + 1) * 8, :])
                for (t0, n) in [(126, 344), (470, 343), (813, 343)]:
                    pi = ecnt["pm"] % 4
                    ecnt["pm"] += 1
                    for kc in range(16):
                        OP("pe", "matmul", [("wu_bf", jb, gv, 0), ("wu_bf", jb, gv, 1)] + hT2_all, [("pmm", pi)],
                           pmm[pi][:, 0:n], lhsT=wu_bf[jb][gv][:, kc, jo:jo + 128], rhs=hT2[:, kc, t0:t0 + n],
                           start=(kc == 0), stop=(kc == 15))
                    if t0 == 813:
                        OP("act", "activation", [("pmm", pi)], [("U", gv)], out=U[gv][:, 813:1152], in_=pmm[pi][:, 0:339],
                           func=AF.Copy)
                        OP("act", "activation", [("pmm", pi)], [("Us", gv)], out=Us[gv][:, :], in_=pmm[pi][:, 339:343],
                           func=AF.Copy)
                    else:
                        OP("act", "activation", [("pmm", pi)], [("U", gv)], out=U[gv][:, t0:t0 + n], in_=pmm[pi][:, 0:n],
                           func=AF.Copy)
                    if t0 == 126:
                        OP("dve", "tensor_scalar", [("U", gv), "flag"], [("U", gv)], U[gv][:, 126:128], U[gv][:, 126:128],
                           flag[:, 0:1], None, ALU.mult)
                dst = cg if gv == 0 else cv
                dk = "cg" if gv == 0 else "cv"
                OP("dve", "tensor_scalar", [("U", gv), "cv_w2", "cv_b"], [dk], dst[:], U[gv][:, 128:1152],
                   cvec["w2"][:, jj:jj + 1], cvec["b"][:, jj:jj + 1], ALU.mult, ALU.add)
                OP("dve", "scalar_tensor_tensor", [("U", gv), "cv_w1", dk], [dk], out=dst[:], in0=U[gv][:, 127:1151],
                   scalar=cvec["w1"][:, jj:jj + 1], in1=dst[:], op0=ALU.mult, op1=ALU.add)
                OP("dve", "scalar_tensor_tensor", [("U", gv), "cv_w0", dk], [dk], out=dst[:], in0=U[gv][:, 126:1150],
                   scalar=cvec["w0"][:, jj:jj + 1], in1=dst[:], op0=ALU.mult, op1=ALU.add)
                OP("pool", "tensor_copy", [("U", gv)], ["Ulast"], out=Ulast[:, jj, :], in_=U[gv][:, 1150:1152])
                dsts = cgs if gv == 0 else cvs
                dks = "cgs" if gv == 0 else "cvs"
                OP("dve", "tensor_scalar", [("Us", gv), "cv_w2", "cv_b"], [dks], dsts[:], Us[gv][:, :],
                   cvec["w2"][:, jj:jj + 1], cvec["b"][:, jj:jj + 1], ALU.mult, ALU.add)
                OP("dve", "scalar_tensor_tensor", ["SF", "cv_w1", dks], [dks], out=dsts[:], in0=SF[:, jj, 4:8],
                   scalar=cvec["w1"][:, jj:jj + 1], in1=dsts[:], op0=ALU.mult, op1=ALU.add)
                OP("dve", "scalar_tensor_tensor", ["SF", "cv_w0", dks], [dks], out=dsts[:], in0=SF[:, jj, 0:4],
                   scalar=cvec["w0"][:, jj:jj + 1], in1=dsts[:], op0=ALU.mult, op1=ALU.add)
                OP("pool", "tensor_copy", [("Us", gv)], ["Usamp"], out=Usamp[:, jj, :], in_=Us[gv][:, :])
            OP("act", "activation", ["cg"], ["cg"], out=cg[:], in_=cg[:], func=AF.Silu)
            OP("dve", "tensor_tensor", ["cg", "cv"], [("aT", j)], out=aT[:, j, 0:1024], in0=cg[:], in1=cv[:], op=ALU.mult)
            OP("act", "activation", ["cgs"], ["cgs"], out=cgs[:], in_=cgs[:], func=AF.Silu)
            OP("dve", "tensor_tensor", ["cgs", "cvs"], [("aT", j)], out=aT[:, j, 1024:1028], in0=cgs[:], in1=cvs[:],
               op=ALU.mult)
        for r in range(2):
            DMA("sp", pffn_out[r, :].rearrange("(j p) -> p j", p=128), Ulast[:, :, r], ["Ulast"], [("pffn", r)],
                allow_slow_non_contiguous=True)
        DMA("sp", sffn_outT.rearrange("(j p) b -> p j b", p=128), Usamp[:], ["Usamp"], ["sffn"])
        DMA("pool", sffn_row0[:, :], sffn_in1[:, :], [], ["sffn0"])
        P.barrier()
        stE1.close()
        mark("E_up_done")
        wd_bf = [sbE("wd_bf%d" % i, [128, 44, 256], BF16) for i in range(2)]
        xres2 = [sbE("xres2_%d" % i, [128, 256]) for i in range(3)]
        xo = [sbE("xo%d" % i, [128, 256]) for i in range(3)]
        aT_all = [("aT", j) for j in range(44)] + ["aT_pad"]
        cnt = {"i": 0}
        for sl in range(8):
            c0 = sl * 256
            b = sl % 2
            for g in range(4):
                DMA("pool", wd_bf[b][:, g * 11:(g + 1) * 11, :],
                    w_down[g * 1408:(g + 1) * 1408, c0:c0 + 256].rearrange("(k p) c -> p k c", p=128),
                    [], [("wd_bf", b, g)])
            wk = [("wd_bf", b, g) for g in range(4)]
            for i in range(9):
                pi = cnt["i"] % 4
                xi = cnt["i"] % 3
                cnt["i"] += 1
                if cnt["i"] == 1:
                    for pf in range(2):
                        DMA("sp", xres2[pf][:], XM[(pf + 1) * 128:(pf + 2) * 128, 0:256], [("XM", pf + 1)], [("xres2", pf)])
                nxt = cnt["i"] + 1
                if nxt < 72:
                    ti = nxt % 9
                    cn = (nxt // 9) * 256
                    DMA("sp", xres2[nxt % 3][:], XM[(ti + 1) * 128:(ti + 2) * 128, cn:cn + 256], [("XM", ti + 1)],
                        [("xres2", nxt % 3)])
                mrows = 128 if i < 8 else 8
                for kc in range(44):
                    OP("pe", "matmul", wk + aT_all, [("pmm", pi)], pmm[pi][0:mrows, 0:256],
                       lhsT=aT[:, kc, i * 128:i * 128 + mrows], rhs=wd_bf[b][:, kc, :], start=(kc == 0), stop=(kc == 43))
                OP("dve", "tensor_tensor", [("pmm", pi), ("xres2", xi)], [("xo", xi)], out=xo[xi][:],
                   in0=pmm[pi][:, 0:256], in1=xres2[xi][:], op=ALU.add)
                DMA("sp", XO[i * 128:(i + 1) * 128, c0:c0 + 256], xo[xi][:], [("xo", xi)], [("XO", i)])
        P.barrier()
        stE.close()
        mark("E_down_done")
        stF = contextlib.ExitStack()
        def sbF(name, shape, dt=F32):
            return stF.enter_context(nc.sbuf_tensor(name, list(shape), dt))
        gfin = sbF("gfin", [128, D])
        DMA("pool", gfin[:], g_fin[0:1, :].partition_broadcast(128), [], ["gfin"])
        xf = [sbF("xf%d" % i, [128, D]) for i in range(2)]
        yf = [sbF("yf%d" % i, [128, D]) for i in range(2)]
        for i in range(9):
            b = i % 2
            DMA("sp", xf[b][:], XO[i * 128:(i + 1) * 128, :], [("XO", i)], [("xf", b)])
            OP("act", "activation", [("xf", b)], [("yf", b), "ss%d" % b], out=yf[b][:], in_=xf[b][:], func=AF.Square,
               accum_out=ss[b][:])
            OP("act", "activation", ["ss%d" % b, "epst"], ["rstd%d" % b], out=rstd[b][:], in_=ss[b][:], func=AF.Ln,
               scale=1.0 / D, bias=epst[:, 0:1])
            OP("act", "activation", ["rstd%d" % b], ["rstd%d" % b], out=rstd[b][:], in_=rstd[b][:], func=AF.Exp,
               scale=-0.5)
            OP("dve", "scalar_tensor_tensor", [("xf", b), "rstd%d" % b, "gfin"], [("yf", b)], out=yf[b][:], in0=xf[b][:],
               scalar=rstd[b][:, 0:1], in1=gfin[:], op0=ALU.mult, op1=ALU.mult)
            if i < 8:
                DMA("pool", y_out[i * 128:(i + 1) * 128, :], yf[b][:], [("yf", b)], [("y", i)])
            else:
                DMA("pool", ys_out[0:4, :], yf[b][0:4, :], [("yf", b)], [("y", i)])
        P.barrier()
        stF.close()
        mark("end")
        P.emit()
    return nc


_NC_CACHE = {}


def _get_nc():
    if "nc" not in _NC_CACHE:
        _NC_CACHE["nc"] = build()
    return _NC_CACHE["nc"]


def _count_mask():
    m = np.zeros((128, 9, 512), np.float32)
    s_idx = np.arange(128)[:, None]
    t_idx = np.arange(512)[None, :]
    for i in range(9):
        d0 = -384 + 128 * i if i < 8 else 1024
        d = d0 + t_idx - s_idx
        cnt = ((d >= 0) & (d <= 128)).astype(np.float32)
        cnt += ((d >= 0) & (d <= 512) & (d % 4 == 0))
        cnt += ((d >= 0) & (d <= 2048) & (d % 16 == 0))
        m[:, i, :] = cnt
    return m.astype(ml_dtypes.bfloat16)


def _tri_masks():
    m = np.zeros((64, 320), np.float32)
    a = np.arange(64)
    m[:, 0:64] = (a[:, None] < a[None, :])
    m[:, 64:128] = (a[:, None] <= a[None, :])
    m[:, 128:192] = 1.0
    m[:, 192:256] = (a[None, :] < a[:, None])
    m[:, 256:320] = 1.0
    return m.astype(ml_dtypes.bfloat16)


def kernel(**inp):
    f32 = np.float32
    x_prompt = np.asarray(inp["x_prompt"], f32)
    x_sample = np.asarray(inp["x_sample"], f32)
    ident = np.eye(128).astype(ml_dtypes.bfloat16)
    masks = _count_mask()
    A = lambda k: np.ascontiguousarray(inp[k][0], dtype=f32)
    shared = {
        "ident": ident, "masks": masks,
        "w_in": A("w_in"), "norm_mix_g": A("norm_mix_g"), "att_out_g": A("att_out_g").reshape(1, 1024),
        "rw_mu": A("rw_mu"), "trim": _tri_masks(),
        "rw_w0": A("rw_w0"), "rw_a0": A("rw_a0"), "rw_k_k": A("rw_k_k"), "rw_k_a": A("rw_k_a"), "rw_r_k": A("rw_r_k"),
        "rw_w_up": A("rw_w_up"), "rw_a_up": A("rw_a_up"), "rw_g_up": A("rw_g_up"),
        "rw_lnx_w": A("rw_lnx_w").reshape(1, 1024), "rw_lnx_b": A("rw_lnx_b").reshape(1, 1024),
        "w_o": A("w_o"), "norm_ffn_g": A("norm_ffn_g"), "ffn_w_up": A("ffn_w_up"), "ffn_conv_w": A("ffn_conv_w"),
        "ffn_conv_b": A("ffn_conv_b"), "ffn_w_down": A("ffn_w_down"),
        "norm_final_g": np.ascontiguousarray(inp["norm_final_g"], dtype=f32).reshape(1, D),
        "rw_lnx_w_c": A("rw_lnx_w"), "rw_lnx_b_c": A("rw_lnx_b"),
    }
    in_maps = []
    for core in range(8):
        b, half = core // 2, core % 2
        xin = np.zeros((NTOK, D), f32)
        if half == 1:
            xin[0:1024] = x_prompt[b, 0:1024]
        xin[1024:2048] = x_prompt[b, half * 1024:(half + 1) * 1024]
        xin[2048:2052] = x_sample[4 * core:4 * core + 4, 0]
        m = dict(shared)
        m["xin"] = xin
        m["flag"] = np.full((128, 1), float(half), f32)
        sf = np.asarray(inp["state_ffn_conv"][0, 4 * core:4 * core + 4], f32)
        m["sffnT"] = np.ascontiguousarray(sf.transpose(2, 1, 0))
        m["cache_k"] = np.ascontiguousarray(inp["cache_att_k"][0, 4 * core:4 * core + 4], dtype=f32).reshape(4, 2048, 1024)
        m["cache_v"] = np.ascontiguousarray(inp["cache_att_v"][0, 4 * core:4 * core + 4], dtype=f32).reshape(4, 2048, 1024)
        m["swkv_in"] = np.ascontiguousarray(inp["state_rwkv_wkv"][0, 4 * core:4 * core + 4], dtype=f32)
        m["sffn_in1"] = np.ascontiguousarray(sf[:, 1, :])
        m["sshiftT"] = np.ascontiguousarray(inp["state_rwkv_shift"][0, 4 * core:4 * core + 4, 0, :].T, dtype=f32)
        in_maps.append(m)
    nc = _get_nc()
    res = run_bass_kernel_spmd(nc, in_maps, core_ids=list(range(8)))
    R = res.results
    _NC_CACHE["R"] = R
    pk = np.zeros((1, 4, 2048, 16, 64), f32)
    pv = np.zeros((1, 4, 2048, 16, 64), f32)
    sk = np.zeros((1, 32, 1, 16, 64), f32)
    sv = np.zeros((1, 32, 1, 16, 64), f32)
    pshift = np.zeros((1, 4, 1, C_SH), f32)
    pwkv = np.zeros((1, 4, 16, 64, 64), f32)
    yp = np.zeros((4, 2048, D), f32)
    ysm = np.zeros((32, 1, D), f32)
    pffn = np.zeros((1, 4, 2, 2 * DFF), f32)
    sffn = np.zeros((1, 32, 2, 2 * DFF), f32)
    swkv = np.zeros((1, 32, 16, 64, 64), f32)
    sshift = np.zeros((1, 32, 1, C_SH), f32)
    for core in range(8):
        b, half = core // 2, core % 2
        pk[0, b, half * 1024:(half + 1) * 1024] = R[core]["k_out"].reshape(1024, 16, 64)
        pv[0, b, half * 1024:(half + 1) * 1024] = R[core]["v_out"].reshape(1024, 16, 64)
        sk[0, 4 * core:4 * core + 4, 0] = R[core]["sk_out"].reshape(4, 16, 64)
        sv[0, 4 * core:4 * core + 4, 0] = R[core]["sv_out"].reshape(4, 16, 64)
        sshift[0, 4 * core:4 * core + 4, 0] = R[core]["sshift_outT"].T
        yp[b, half * 1024:(half + 1) * 1024] = R[core]["y_out"]
        ysm[4 * core:4 * core + 4, 0] = R[core]["ys_out"]
        swkv[0, 4 * core:4 * core + 4] = R[core]["swkv_out"]
        sffn[0, 4 * core:4 * core + 4, 0] = R[core]["sffn_row0"]
        sffn[0, 4 * core:4 * core + 4, 1] = R[core]["sffn_outT"].T
        if half == 1:
            pshift[0, b, 0] = R[core]["pshift_out"][:, 0]
            pwkv[0, b] = R[core]["pwkv_out"]
            pffn[0, b] = R[core]["pffn_out"]
    z = lambda *s: np.zeros(s, f32)
    return (yp, ysm, pk, pv, pshift, pwkv, pffn, sk, sv, sshift, swkv, sffn)
```

```python
import contextlib
import os
import numpy as np
import ml_dtypes
import concourse.bass as bass
import concourse.mybir as mybir
from concourse.bass_utils import run_bass_kernel_spmd

F32 = mybir.dt.float32
BF16 = mybir.dt.bfloat16
I32 = mybir.dt.int32
ALU = mybir.AluOpType
AF = mybir.ActivationFunctionType
AX = mybir.AxisListType

ENGS = ["pe", "act", "dve", "pool", "sp"]
EPOCH = 30000
N_DMA_SEMS = 32


class Prog:
    def __init__(self, nc):
        self.nc = nc
        self.streams = {e: [] for e in ENGS}
        self.cnt = {e: 0 for e in ENGS}
        self.seen = {e: {} for e in ENGS}
        self.lastw = {}
        self.readers = {}
        self.dma_sems = ["dma%d" % i for i in range(N_DMA_SEMS)]
        self.sdma_sems = ["sdma%d" % i for i in range(16)]
        self.dma_cnt = {s: 0 for s in self.dma_sems + self.sdma_sems}
        self.dma_rr = 0
        self.sdma_rr = 0
        self.semnames = set()

    def _need(self, reads, writes):
        need = {}
        for k in reads:
            lw = self.lastw.get(k)
            if lw is not None:
                need[lw[0]] = max(need.get(lw[0], 0), lw[1])
        for k in writes:
            lw = self.lastw.get(k)
            if lw is not None:
                need[lw[0]] = max(need.get(lw[0], 0), lw[1])
            for s, v in self.readers.get(k, {}).items():
                need[s] = max(need.get(s, 0), v)
        return need

    def _waits(self, eng, need):
        waits = []
        for s, v in need.items():
            if eng == "pe" and s.startswith("pe"):
                continue
            if self.seen[eng].get(s, 0) >= v:
                continue
            self.seen[eng][s] = v
            waits.append((s, v))
        return waits

    def _mark(self, reads, writes, sem, val):
        for k in reads:
            d = self.readers.setdefault(k, {})
            d[sem] = max(d.get(sem, 0), val)
        for k in writes:
            self.lastw[k] = (sem, val)
            self.readers[k] = {}

    def op(self, eng, reads, writes, meth, *a, **kw):
        fn = (meth, a, kw)
        waits = self._waits(eng, self._need(reads, writes))
        c = self.cnt[eng]
        sem = "%s_e%d" % (eng, c // EPOCH)
        val = c % EPOCH + 1
        self.cnt[eng] = c + 1
        self.semnames.add(sem)
        self.streams[eng].append((fn, waits, (sem, 1)))
        self._mark(reads, writes, sem, val)

    def dma(self, q, reads, writes, **kw):
        fn = ("dma_start", (), kw)
        if q == "pool":
            s = self.sdma_sems[self.sdma_rr % 16]
            self.sdma_rr += 1
        else:
            s = self.dma_sems[self.dma_rr % N_DMA_SEMS]
            self.dma_rr += 1
        need = self._need(reads, writes)
        prev = self.dma_cnt[s]
        if prev > 0:
            need[s] = max(need.get(s, 0), prev)
        waits = self._waits(q, need)
        self.dma_cnt[s] = prev + 16
        self.semnames.add(s)
        self.streams[q].append((fn, waits, (s, 16)))
        self._mark(reads, writes, s, prev + 16)

    def barrier(self):
        need = {}
        for sname in self.dma_sems + self.sdma_sems:
            if self.dma_cnt[sname] > 0:
                need[sname] = self.dma_cnt[sname]
        for e in ENGS:
            c = self.cnt[e]
            if c > 0:
                need["%s_e%d" % (e, (c - 1) // EPOCH)] = (c - 1) % EPOCH + 1
        for e in ENGS:
            waits = []
            for sn, v in need.items():
                if self.seen[e].get(sn, 0) >= v:
                    continue
                self.seen[e][sn] = v
                waits.append((sn, v))
            if waits:
                self.streams[e].append((None, waits, None))

    def emit(self):
        nc = self.nc
        names = sorted(self.semnames)
        final = []
        for s in names:
            if s.startswith("dma") or s.startswith("sdma"):
                final.append((s, self.dma_cnt[s]))
        for e in ENGS:
            c = self.cnt[e]
            if c > 0:
                final.append(("%s_e%d" % (e, (c - 1) // EPOCH), (c - 1) % EPOCH + 1))
        with contextlib.ExitStack() as st:
            sems = {n: st.enter_context(nc.semaphore(n)) for n in names}
            block = st.enter_context(nc.Block())
            streams = self.streams

            def run(engobj, lst, fin=None):
                for fn, waits, inc in lst:
                    for s, v in waits:
                        engobj.wait_ge(sems[s], v)
                    if fn is None:
                        continue
                    ins = getattr(engobj, fn[0])(*fn[1], **fn[2])
                    ins.then_inc(sems[inc[0]], inc[1])
                if fin:
                    for s, v in fin:
                        engobj.wait_ge(sems[s], v)

            @block.tensor
            def _(e):
                run(e, streams["pe"])

            @block.scalar
            def _(e):
                run(e, streams["act"])

            @block.vector
            def _(e):
                run(e, streams["dve"])

            @block.gpsimd
            def _(e):
                run(e, streams["pool"])

            @block.sync
            def _(e):
                run(e, streams["sp"], final)


D = 2048
NSLOT = 2048
NTOK = 2176
NTILE = 17
C_IN = 6432
C_SH = 3360
DFF = 5632
EPS = 1.0e-6
STAGE = 99
HAVE_SAMPLE = True


class Ctx:
    pass


def build(stage=STAGE):
    nc = bass.Bass("TRN2", target_bir_lowering=False)
    P = Prog(nc)

    def OP(eng, meth, reads, writes, *a, **kw):
        if eng in ("act", "dve"):
            extra = [k for k in reads if isinstance(k, tuple) and k[0] == "pmm"]
            if extra:
                writes = list(writes) + extra
        P.op(eng, reads, writes, meth, *a, **kw)

    def DMA(q, out, in_, reads, writes, **kw):
        P.dma(q, reads, writes, out=out, in_=in_, **kw)

    def din(name, shape, dt=F32):
        return nc.dram_tensor(name, list(shape), dt, kind="ExternalInput").ap()

    def dout(name, shape, dt=F32):
        return nc.dram_tensor(name, list(shape), dt, kind="ExternalOutput").ap()

    def dscr(name, shape, dt=F32):
        return nc.dram_tensor(name, list(shape), dt).ap()

    xin = din("xin", [NTOK, D])
    ident_d = din("ident", [128, 128], BF16)
    masks_d = din("masks", [128, 9, 512], BF16)
    flag_d = din("flag", [128, 1])
    w_in = din("w_in", [D, C_IN])
    g_mix = din("norm_mix_g", [D])
    g_att = din("att_out_g", [1, 1024])
    rw_mu = din("rw_mu", [C_SH])
    sshiftT = din("sshiftT", [C_SH, 4])
    trim_d = din("trim", [64, 320], BF16)
    rw_w0 = din("rw_w0", [1024])
    rw_a0 = din("rw_a0", [1024])
    rw_k_k = din("rw_k_k", [1024])
    rw_k_a = din("rw_k_a", [1024])
    rw_r_k = din("rw_r_k", [1024])
    rw_w_up = din("rw_w_up", [64, 1024])
    rw_a_up = din("rw_a_up", [64, 1024])
    rw_g_up = din("rw_g_up", [160, 1024])
    rw_lnx_w = din("rw_lnx_w", [1, 1024])
    rw_lnx_b = din("rw_lnx_b", [1, 1024])
    w_o = din("w_o", [D, D])
    g_ffn = din("norm_ffn_g", [D])
    w_up = din("ffn_w_up", [D, 2 * DFF])
    conv_w = din("ffn_conv_w", [3, 2 * DFF])
    conv_b = din("ffn_conv_b", [2 * DFF])
    w_down = din("ffn_w_down", [DFF, D])
    g_fin = din("norm_final_g", [1, D])
    sffnT = din("sffnT", [2 * DFF, 2, 4])
    sffn_in1 = din("sffn_in1", [4, 2 * DFF])
    cache_k = din("cache_k", [4, 2048, 1024])
    cache_v = din("cache_v", [4, 2048, 1024])
    swkv_in = din("swkv_in", [4, 16, 64, 64])
    rw_lnx_w_c = din("rw_lnx_w_c", [1024])
    rw_lnx_b_c = din("rw_lnx_b_c", [1024])
    k_out = dout("k_out", [1024, 1024])
    v_out = dout("v_out", [1024, 1024])
    sk_out = dout("sk_out", [4, 1024])
    sv_out = dout("sv_out", [4, 1024])
    pshift_out = dout("pshift_out", [C_SH, 1])
    sshift_outT = dout("sshift_outT", [C_SH, 4])
    pwkv_out = dout("pwkv_out", [16, 64, 64])
    y_out = dout("y_out", [1024, D])
    ys_out = dout("ys_out", [4, D])
    pffn_out = dout("pffn_out", [2, 2 * DFF])
    sffn_outT = dout("sffn_outT", [2 * DFF, 4])
    sffn_row0 = dout("sffn_row0", [4, 2 * DFF])
    swkv_out = dout("swkv_out", [4, 16, 64, 64])
    QT = dscr("QT", [1024, NTOK], BF16)
    KT = dscr("KT", [1024, NTOK], BF16)
    XS = dscr("XS", [C_SH, NTOK], F32)
    O_scr = dscr("O_scr", [1280, 2048], BF16)
    QS = dscr("QS", [4, 1024], F32)
    KS = dscr("KS", [4, 1024], F32)
    VS = dscr("VS", [4, 1024], F32)
    XM = dscr("XM", [1280, D], F32)
    XO = dscr("XO", [1152, D], F32)

    with contextlib.ExitStack() as st:
        def sb(name, shape, dt=F32):
            return st.enter_context(nc.sbuf_tensor(name, list(shape), dt))

        def ps(name, shape, dt=F32):
            return st.enter_context(nc.psum_tensor(name, list(shape), dt))

        ident = sb("ident_sb", [128, 128], BF16)
        DMA("sp", ident[:], ident_d[:, :], [], ["ident"])
        epst = sb("epst", [128, 1])
        OP("pool", "memset", [], ["epst"], epst[:], EPS)
        epsl = sb("epsl", [128, 1])
        OP("pool", "memset", [], ["epsl"], epsl[:], 64 * 1e-5)
        flag = sb("flag_sb", [128, 1])
        DMA("sp", flag[:], flag_d[:, :], [], ["flag"])
        gcol = sb("gcol", [128, 16])
        DMA("sp", gcol[:], g_mix.rearrange("(k p) -> p k", p=128), [], ["gcol"], allow_slow_non_contiguous=True)
        mucol = sb("mucol", [128, 27])
        DMA("sp", mucol[:, 0:26], rw_mu[0:3328].rearrange("(k p) -> p k", p=128), [], ["mucol"],
            allow_slow_non_contiguous=True)
        DMA("sp", mucol[0:32, 26:27], rw_mu[3328:3360].rearrange("(p o) -> p o", o=1), ["mucol"], ["mucol"],
            allow_slow_non_contiguous=True)

        ss = [sb("ss%d" % i, [128, 1]) for i in range(2)]
        rstd = [sb("rstd%d" % i, [128, 1]) for i in range(2)]
        pmm = [ps("pmm%d" % i, [128, 512]) for i in range(8)]
        pT = [pmm[6 + i][:, :].bitcast(BF16).rearrange("p (a b) -> p a b", b=128)[:, 0:4, :] for i in range(2)]
        stAB = contextlib.ExitStack()
        def sbAB(name, shape, dt=F32):
            return stAB.enter_context(nc.sbuf_tensor(name, list(shape), dt))
        VA = sbAB("VA", [128, 17, 16, 65], BF16)
        OP("pool", "memset", [], ["VAones"], VA[:, :, :, 64:65], 1.0)
        OP("pool", "tensor_scalar", ["flag", "VAones"], ["VAones"], VA[:, 0:8, :, 64:65], VA[:, 0:8, :, 64:65],
           flag[:, 0:1], None, ALU.mult)

        MARK = {}
        def mark(name):
            MARK[name] = dict(P.cnt)
        _NC_CACHE["MARK"] = MARK
        stA = contextlib.ExitStack()
        def sbA(name, shape, dt=F32):
            return stA.enter_context(nc.sbuf_tensor(name, list(shape), dt))
        hT = sbA("hT", [128, 16, NTOK], BF16)
        xt = [sbA("xt%d" % i, [128, D]) for i in range(2)]
        xn = [sbA("xn%d" % i, [128, D], BF16) for i in range(2)]

        def norm_transpose(src_ap, gc, gkey, dst, dst_key, it, src_keys=()):
            b = it % 2
            DMA("sp", xt[b][:], src_ap, list(src_keys), ["xt%d" % b])
            OP("act", "activation", ["xt%d" % b], ["xn%d" % b, "ss%d" % b], out=xn[b][:], in_=xt[b][:], func=AF.Square,
               accum_out=ss[b][:])
            OP("act", "activation", ["ss%d" % b, "epst"], ["rstd%d" % b], out=rstd[b][:], in_=ss[b][:], func=AF.Ln,
               scale=1.0 / D, bias=epst[:, 0:1])
            OP("act", "activation", ["rstd%d" % b], ["rstd%d" % b], out=rstd[b][:], in_=rstd[b][:], func=AF.Exp,
               scale=-0.5)
            OP("dve", "tensor_scalar", ["xt%d" % b, "rstd%d" % b], ["xn%d" % b], xn[b][:], xt[b][:],
               rstd[b][:, 0:1], None, ALU.mult)
            for g4 in range(4):
                pb = g4 % 2
                for j in range(4):
                    kc = g4 * 4 + j
                    OP("pe", "transpose", ["xn%d" % b, "ident"], ["pT%d" % pb], out=pT[pb][:, j, :],
                       in_=xn[b][:, kc * 128:(kc + 1) * 128], identity=ident[:])
                OP("dve", "tensor_tensor", ["pT%d" % pb, gkey], [dst_key], out=dst[:, g4 * 4:(g4 + 1) * 4, :],
                   in0=pT[pb], in1=gc[:, g4 * 4:(g4 + 1) * 4].unsqueeze(2).broadcast_to([128, 4, 128]),
                   op=ALU.mult)

        for t in range(NTILE):
            norm_transpose(xin[t * 128:(t + 1) * 128, :], gcol, "gcol", hT[:, :, t * 128:(t + 1) * 128], ("hT", t), t)

        mark("A_norm_done")
        SL = 256
        wbf = [sbA("wbf%d" % i, [128, 16, SL], BF16) for i in range(3)]
        ev32 = [sbA("ev32_%d" % i, [128, 512]) for i in range(4)]
        ev16 = [sbA("ev16_%d" % i, [128, 512], BF16) for i in range(4)]
        cb = [sbA("cb%d" % i, [128, 513]) for i in range(2)]
        dsh = sbA("dsh", [128, 512])
        xsb = [sbA("xsb%d" % i, [128, 512]) for i in range(2)]
        sprev = sbA("sprev", [128, 27, 4])
        DMA("sp", sprev[:, 0:26, :], sshiftT[0:3328, :].rearrange("(k p) f -> p k f", p=128), [], ["sprev"])
        DMA("sp", sprev[0:32, 26, :], sshiftT[3328:3360, :], ["sprev"], ["sprev"])
        state = {"pm": 0, "evq": 0, "rwb": 0}
        hT_all = [("hT", t) for t in range(NTILE)]

        def evac(dst_sb, src_ps, rkeys, wkeys):
            state["evq"] += 1
            if state["evq"] % 2 == 0:
                OP("act", "activation", rkeys, wkeys, out=dst_sb, in_=src_ps, func=AF.Copy)
            else:
                OP("dve", "tensor_copy", rkeys, wkeys, out=dst_sb, in_=src_ps)

        nslab = (C_IN + SL - 1) // SL
        for s in range(nslab):
            c0 = s * SL
            cw = min(SL, C_IN - c0)
            b = s % 3
            for half in range(2):
                DMA("pool", wbf[b][:, half * 8:(half + 1) * 8, 0:cw],
                    w_in[half * 1024:(half + 1) * 1024, c0:c0 + cw].rearrange("(k p) c -> p k c", p=128),
                    [], [("wbf", b, half)])
            wkeys = [("wbf", b, 0), ("wbf", b, 1)]
            kind = "q" if c0 < 1024 else "k" if c0 < 2048 else "v" if c0 < 3072 else "rw"
            if kind in ("q", "k", "rw"):
                for m in range((cw + 127) // 128):
                    mw = min(128, cw - m * 128)
                    cc = c0 + m * 128
                    if kind == "q":
                        blocks = [(896, 128), (1024, 512), (1536, 512), (2048, 128)]
                    else:
                        blocks = [(0, 512), (512, 512), (1024, 512), (1536, 512), (2048, 128)]
                    for (t0, n) in blocks:
                        pi = state["pm"] % 4
                        state["pm"] += 1
                        for kc in range(16):
                            OP("pe", "matmul", wkeys + hT_all, [("pmm", pi)], pmm[pi][0:mw, 0:n],
                               lhsT=wbf[b][:, kc, m * 128:m * 128 + mw], rhs=hT[:, kc, t0:t0 + n],
                               start=(kc == 0), stop=(kc == 15))
                        if kind in ("q", "k"):
                            dstT = QT if kind == "q" else KT
                            r0 = cc - (0 if kind == "q" else 1024)
                            evac(ev16[pi][0:mw, 0:n], pmm[pi][0:mw, 0:n], [("pmm", pi)], [("ev16", pi)])
                            DMA("sp", dstT[r0:r0 + mw, t0:t0 + n], ev16[pi][0:mw, 0:n], [("ev16", pi)],
                                [("scr", kind, r0 // 64, t0), ("scr", kind, r0 // 64 + 1, t0)])
                        else:
                            ch0 = cc - 3072
                            ci = ch0 // 128
                            rb = state["rwb"] % 2
                            state["rwb"] += 1
                            nb = cb[rb]
                            if t0 == 0:
                                OP("dve", "memset", [], [("cb", rb)], nb[0:mw, 0:1], 0.0)
                            OP("act", "activation", [("pmm", pi)], [("cb", rb)], out=nb[0:mw, 1:1 + n],
                               in_=pmm[pi][0:mw, 0:n], func=AF.Copy)
                            if t0 < 1536:
                                OP("dve", "tensor_copy", [("cb", rb)], [("cb", 1 - rb)], out=cb[1 - rb][0:mw, 0:1],
                                   in_=nb[0:mw, n:n + 1])
                            ne = n if t0 < 2048 else 4
                            prev_ap = nb[0:mw, 0:ne] if t0 < 2048 else sprev[0:mw, ci, :]
                            OP("dve", "tensor_tensor", [("cb", rb), "sprev"], ["dsh"], out=dsh[0:mw, 0:ne],
                               in0=prev_ap, in1=nb[0:mw, 1:1 + ne], op=ALU.subtract)
                            xb = state["rwb"] % 2
                            OP("dve", "scalar_tensor_tensor", ["dsh", ("cb", rb), "mucol"], [("xsb", xb)],
                               out=xsb[xb][0:mw, 0:ne], in0=dsh[0:mw, 0:ne], scalar=mucol[0:mw, ci:ci + 1],
                               in1=nb[0:mw, 1:1 + ne], op0=ALU.mult, op1=ALU.add)
                            DMA("sp", XS[ch0:ch0 + mw, t0:t0 + ne], xsb[xb][0:mw, 0:ne], [("xsb", xb)],
                                [("XS", ci, t0)])
                            if t0 == 1536:
                                DMA("sp", pshift_out[ch0:ch0 + mw, :], nb[0:mw, 512:513], [("cb", rb)],
                                    [("pshift", ci)])
                            if t0 == 2048:
                                DMA("sp", sshift_outT[ch0:ch0 + mw, :], nb[0:mw, 1:5], [("cb", rb)],
                                    [("sshift", ci)])
            if kind in ("q", "k", "v"):
                tiles = [16] if kind == "q" else list(range(8, 17)) if kind == "k" else list(range(17))
                for t in tiles:
                    pi = state["pm"] % 4
                    state["pm"] += 1
                    for kc in range(16):
                        OP("pe", "matmul", wkeys + [("hT", t)], [("pmm", pi)], pmm[pi][:, 0:cw],
                           lhsT=hT[:, kc, t * 128:(t + 1) * 128], rhs=wbf[b][:, kc, 0:cw],
                           start=(kc == 0), stop=(kc == 15))
                    evac(ev32[pi][:, 0:cw], pmm[pi][:, 0:cw], [("pmm", pi)], [("ev32", pi)])
                    col0 = c0 - (0 if kind == "q" else 1024 if kind == "k" else 2048)
                    if kind == "q":
                        DMA("sp", QS[0:4, col0:col0 + cw], ev32[pi][0:4, 0:cw], [("ev32", pi)], [("QS", col0)])
                        continue
                    if t == 16:
                        scr = KS if kind == "k" else VS
                        DMA("sp", scr[0:4, col0:col0 + cw], ev32[pi][0:4, 0:cw], [("ev32", pi)],
                            [("KVS", kind, col0)])
                    if 8 <= t < 16:
                        o = k_out if kind == "k" else v_out
                        DMA("sp", o[(t - 8) * 128:(t - 7) * 128, col0:col0 + cw], ev32[pi][:, 0:cw],
                            [("ev32", pi)], [("kvout", kind, t, col0)])
                    if t == 16:
                        o = sk_out if kind == "k" else sv_out
                        DMA("sp", o[0:4, col0:col0 + cw], ev32[pi][0:4, 0:cw], [("ev32", pi)],
                            [("skvout", kind, col0)])
                    if kind == "v":
                        h0 = col0 // 64
                        src = ev32[pi][:, 0:cw].rearrange("p (h e) -> p h e", e=64)
                        if t < 8:
                            OP("dve", "tensor_scalar", [("ev32", pi), "flag"], [("VA", t)],
                               VA[:, t, h0:h0 + 4, 0:64], src, flag[:, 0:1], None, ALU.mult)
                        else:
                            OP("dve", "tensor_copy", [("ev32", pi)], [("VA", t)], out=VA[:, t, h0:h0 + 4, 0:64],
                               in_=src)
        P.barrier()
        stA.close()
        mark("A_done")
        stB = contextlib.ExitStack()
        def sbB(name, shape, dt=F32):
            return stB.enter_context(nc.sbuf_tensor(name, list(shape), dt))
        Oatt = sbB("Oatt", [128, 9, 1024], BF16)
        msk = sbB("msk", [128, 9, 512], BF16)
        gatt = sbB("gatt", [128, 1024])
        DMA("sp", msk[:], masks_d[:, :, :], [], ["msk"])
        DMA("pool", gatt[:], g_att[0:1, :].partition_broadcast(128), [], ["gatt"])
        qh = [sbB("qh%d" % i, [64, 1152], BF16) for i in range(2)]
        kh = [sbB("kh%d" % i, [64, 2048], BF16) for i in range(2)]
        pTs = [sbB("pTs%d" % i, [128, 512], BF16) for i in range(4)]
        SBK = [0, 1, 6, 7]
        junk64 = sbB("junk64", [128, 64])
        fin = [[sbB("fin%d_%d" % (i, j), [128, 1]) for j in range(5)] for i in range(2)]
        stt = {"s": 0, "mq": 0, "f": 0}
        scr_q = lambda hh: [("scr", "q", hh, t0) for t0 in (896, 1024, 1536, 2048)]
        scr_k = lambda hh: [("scr", "k", hh, t0) for t0 in (0, 512, 1024, 1536, 2048)]
        items = []
        for h in range(16):
            b = h % 2
            first_of_head = [True]
            for (q0, n) in [(896, 128), (1024, 512), (1536, 512)]:
                nsub = n // 128
                qt0 = q0 // 128
                nkc = qt0 + nsub
                for kc in range(0, nkc):
                    def s1(h=h, b=b, q0=q0, n=n, kc=kc, qt0=qt0, ld=first_of_head[0]):
                        if ld:
                            DMA("sp", qh[b][:, :], QT[h * 64:(h + 1) * 64, 896:2048], scr_q(h), [("qh", b)])
                            DMA("sp", kh[b][:, :], KT[h * 64:(h + 1) * 64, 0:2048], scr_k(h), [("kh", b)])
                        s0 = kc * 128
                        d0 = q0 - s0
                        qi_min = max(0, kc - qt0)
                        c_lo = qi_min * 128
                        mi = (d0 + 384) // 128 if d0 <= 512 else 8
                        sj = stt["s"] % 4
                        si = SBK[sj]
                        stt["s"] += 1
                        OP("pe", "matmul", [("kh", b), ("qh", b)], [("pmm", si)], pmm[si][:, c_lo:n],
                           lhsT=kh[b][:, s0:s0 + 128], rhs=qh[b][:, q0 - 896 + c_lo:q0 - 896 + n], start=True, stop=True)
                        OP("act", "activation", [("pmm", si)], [("pTs", sj)], out=pTs[sj][:, c_lo:n],
                           in_=pmm[si][:, c_lo:n], func=AF.Exp, scale=0.125)
                        OP("dve", "tensor_tensor", [("pTs", sj), "msk"], [("pTs", sj)],
                           out=pTs[sj][:, c_lo:n], in0=pTs[sj][:, c_lo:n], in1=msk[:, mi, c_lo:n], op=ALU.mult)
                        return sj, qi_min
                    def s2(sj, qi_min, h=h, q0=q0, n=n, kc=kc, qt0=qt0, nsub=nsub, last=(kc == nkc - 1)):
                        for qi in range(qi_min, nsub):
                            OP("pe", "matmul", [("pTs", sj), ("VA", kc), "VAones"], [("pmm", 2 + qi)],
                               pmm[2 + qi][:, 0:65], lhsT=pTs[sj][:, qi * 128:(qi + 1) * 128], rhs=VA[:, kc, h, :],
                               start=(kc == 0), stop=(kc == qt0 + qi))
                        if not last:
                            return
                        for qi in range(nsub):
                            tile_i = qt0 + qi - 7
                            acc = pmm[2 + qi]
                            f = fin[stt["f"] % 2]
                            fk = ("fin", stt["f"] % 2)
                            stt["f"] += 1
                            OP("act", "activation", [("pmm", 2 + qi)], ["junk64", fk], out=junk64[:], in_=acc[:, 0:64],
                               func=AF.Square, accum_out=f[0][:])
                            OP("act", "activation", [("pmm", 2 + qi)], [fk], out=f[1][:], in_=acc[:, 64:65], func=AF.Copy)
                            OP("dve", "scalar_tensor_tensor", [fk], [fk], out=f[2][:], in0=f[1][:], scalar=EPS, in1=f[1][:],
                               op0=ALU.mult, op1=ALU.mult)
                            OP("dve", "scalar_tensor_tensor", [fk], [fk], out=f[3][:], in0=f[0][:], scalar=1.0 / 64,
                               in1=f[2][:], op0=ALU.mult, op1=ALU.add)
                            OP("dve", "tensor_scalar_max", [fk], [fk], out=f[3][:], in0=f[3][:], scalar1=1e-30)
                            OP("act", "activation", [fk], [fk], out=f[4][:], in_=f[3][:], func=AF.Ln)
                            OP("act", "activation", [fk], [fk], out=f[4][:], in_=f[4][:], func=AF.Exp, scale=-0.5)
                            OP("dve", "scalar_tensor_tensor", [("pmm", 2 + qi), fk, "gatt"], [("Oatt", tile_i)],
                               out=Oatt[:, tile_i, h * 64:(h + 1) * 64], in0=acc[:, 0:64], scalar=f[4][:, 0:1],
                               in1=gatt[:, h * 64:(h + 1) * 64], op0=ALU.mult, op1=ALU.mult)
                    items.append((s1, s2))
                    first_of_head[0] = False
        LA = 2
        pend = {}
        for k in range(len(items) + LA):
            if k < len(items):
                pend[k] = items[k][0]()
            if k - LA >= 0:
                items[k - LA][1](*pend.pop(k - LA))
        mark("B_prompt_done")
        qb = sbB("qb", [128, 1024])
        ktl = [sbB("ktl%d" % i, [128, 1024]) for i in range(2)]
        vtl = [sbB("vtl%d" % i, [128, 1024]) for i in range(2)]
        prod = sbB("prod", [128, 1024])
        pvb = sbB("pvb", [128, 1024], BF16)
        sc = sbB("sc", [128, 16])
        pb16 = sbB("pb16", [128, 16], BF16)
        onesb = sbB("onesb", [128, 1], BF16)
        OP("pool", "memset", [], ["onesb"], onesb[:], 1.0)
        srow = {nm: sbB("srow_" + nm, [1, 1024]) for nm in ("o", "sq", "o2")}
        srb = sbB("srow_b", [1, 1024], BF16)
        s16 = {nm: sbB("s16_" + nm, [1, 16]) for nm in ("den", "rden", "ms", "r")}
        qs_keys = [("QS", c) for c in (0, 256, 512, 768)]
        ks_keys = [("KVS", "k", c) for c in (0, 256, 512, 768)]
        vs_keys = [("KVS", "v", c) for c in (0, 256, 512, 768)]
        pnum = [pmm[0][0:1, :], pmm[1][0:1, :]]
        pden = pmm[2][0:1, 0:16]
        gi = {"i": 0}
        for bi in range(4):
            DMA("sp", qb[:], QS[bi:bi + 1, :].partition_broadcast(128), qs_keys, ["qb"])
            groups = [(2048 - 128 * r, r, 128) for r in (1, 4, 16)] + [(None, 0, 1)]
            for gidx, (start, rate, rows) in enumerate(groups):
                tb_ = gi["i"] % 2
                gi["i"] += 1
                kt_, vt_ = ktl[tb_], vtl[tb_]
                if start is not None:
                    DMA("sp", kt_[:], cache_k[bi, start:2048:rate, :], [], [("ktl", tb_)])
                    DMA("pool", vt_[:], cache_v[bi, start:2048:rate, :], [], [("vtl", tb_)])
                else:
                    DMA("sp", kt_[0:1, :], KS[bi:bi + 1, :], ks_keys, [("ktl", tb_)])
                    DMA("pool", vt_[0:1, :], VS[bi:bi + 1, :], vs_keys, [("vtl", tb_)])
                R_ = slice(0, rows)
                OP("dve", "tensor_tensor", [("ktl", tb_), "qb"], ["prod"], out=prod[R_, :], in0=kt_[R_, :], in1=qb[R_, :],
                   op=ALU.mult)
                OP("dve", "tensor_reduce", ["prod"], ["sc"], out=sc[R_, :],
                   in_=prod[R_, :].rearrange("p (h e) -> p h e", e=64), axis=AX.X, op=ALU.add)
                OP("act", "activation", ["sc"], ["sc"], out=sc[R_, :], in_=sc[R_, :], func=AF.Exp, scale=0.125)
                if start is None:
                    OP("dve", "tensor_scalar", ["sc"], ["sc"], sc[R_, :], sc[R_, :], 3.0, None, ALU.mult)
                OP("dve", "tensor_copy", ["sc"], ["pb16"], out=pb16[R_, :], in_=sc[R_, :])
                OP("dve", "tensor_tensor", [("vtl", tb_), "sc"], ["pvb"],
                   out=pvb[R_, :].rearrange("p (h e) -> p h e", e=64),
                   in0=vt_[R_, :].rearrange("p (h e) -> p h e", e=64),
                   in1=sc[R_, :].unsqueeze(2).broadcast_to([rows, 16, 64]), op=ALU.mult)
                first, last = (gidx == 0), (gidx == 3)
                for hf in range(2):
                    OP("pe", "matmul", ["pvb", "onesb"], [("pmm", hf)], pnum[hf], lhsT=onesb[R_, :],
                       rhs=pvb[R_, hf * 512:(hf + 1) * 512], start=first, stop=last)
                OP("pe", "matmul", ["pb16", "onesb"], [("pmm", 2)], pden, lhsT=onesb[R_, :], rhs=pb16[R_, :], start=first,
                   stop=last)
            OP("act", "activation", [("pmm", 2)], ["s16"], out=s16["den"][:], in_=pden, func=AF.Copy)
            OP("dve", "reciprocal", ["s16"], ["s16"], out=s16["rden"][:], in_=s16["den"][:])
            for hf in range(2):
                OP("dve", "tensor_tensor", [("pmm", hf), "s16"], ["srow_o"],
                   out=srow["o"][:, hf * 512:(hf + 1) * 512].rearrange("p (h e) -> p h e", e=64),
                   in0=pnum[hf].rearrange("p (h e) -> p h e", e=64),
                   in1=s16["rden"][:, hf * 8:(hf + 1) * 8].unsqueeze(2).broadcast_to([1, 8, 64]), op=ALU.mult)
            OP("dve", "tensor_tensor", ["srow_o"], ["srow_sq"], out=srow["sq"][:], in0=srow["o"][:], in1=srow["o"][:],
               op=ALU.mult)
            OP("dve", "tensor_reduce", ["srow_sq"], ["s16"], out=s16["ms"][:],
               in_=srow["sq"][:].rearrange("p (h e) -> p h e", e=64), axis=AX.X, op=ALU.add)
            OP("act", "activation", ["s16", "epst"], ["s16"], out=s16["r"][:], in_=s16["ms"][:], func=AF.Ln, scale=1.0 / 64,
               bias=epst[0:1, 0:1])
            OP("act", "activation", ["s16"], ["s16"], out=s16["r"][:], in_=s16["r"][:], func=AF.Exp, scale=-0.5)
            OP("dve", "tensor_tensor", ["srow_o", "s16"], ["srow_o2"],
               out=srow["o2"][:].rearrange("p (h e) -> p h e", e=64),
               in0=srow["o"][:].rearrange("p (h e) -> p h e", e=64),
               in1=s16["r"][:].unsqueeze(2).broadcast_to([1, 16, 64]), op=ALU.mult)
            OP("dve", "tensor_tensor", ["srow_o2", "gatt"], ["srow_b"], out=srb[:], in0=srow["o2"][:], in1=gatt[0:1, :],
               op=ALU.mult)
            DMA("sp", O_scr[1152 + bi:1153 + bi, 0:1024], srb[:], ["srow_b"], ["Oscr_samp"])
        DMA("sp", O_scr[0:1152, 0:1024].rearrange("(c p) f -> p c f", p=128), Oatt[:], [("Oatt", i) for i in range(9)],
            ["Oscr_att"])
        P.barrier()
        stB.close()
        stAB.close()
        mark("B_done")
        stC = contextlib.ExitStack()
        def sbC(name, shape, dt=F32):
            return stC.enter_context(nc.sbuf_tensor(name, list(shape), dt))
        i64 = ident[0:64, 0:64]
        ones64 = sbC("ones64", [64, 64], BF16)
        OP("pool", "memset", [], ["ones64"], ones64[:], 1.0)
        identf = sbC("identf", [64, 64])
        OP("dve", "tensor_copy", ["ident"], ["identf"], out=identf[:], in_=ident[0:64, 0:64])
        lora_w = sbC("lora_w", [64, 1024], BF16)
        lora_a = sbC("lora_a", [64, 1024], BF16)
        DMA("pool", lora_w[:], rw_w_up[:, :], [], ["lora_w"])
        DMA("pool", lora_a[:], rw_a_up[:, :], [], ["lora_a"])
        gupb = sbC("gupb", [128, 2, 1024], BF16)
        OP("pool", "memset", [], ["gupb"], gupb[:, 1, :], 0.0)
        DMA("pool", gupb[:, 0, :], rw_g_up[0:128, :], ["gupb"], ["gupb"])
        DMA("pool", gupb[0:32, 1, :], rw_g_up[128:160, :], ["gupb"], ["gupb"])
        colv = {}
        for nm, apd in (("w0", rw_w0), ("a0", rw_a0), ("kk", rw_k_k), ("ka", rw_k_a), ("rk", rw_r_k)):
            tcol = sbC("col_" + nm, [64, 16])
            DMA("sp", tcol[:], apd.rearrange("(h p) -> p h", p=64), [], ["col_" + nm], allow_slow_non_contiguous=True)
            colv[nm] = tcol
        lnw2 = sbC("lnw2", [128, 4, 128])
        lnb2 = sbC("lnb2", [128, 4, 128])
        for t_ in range(2):
            DMA("pool", lnw2[t_ * 64:(t_ + 1) * 64, :, :],
                rw_lnx_w[0:1, :].rearrange("o (q t f) -> o t q f", t=2, f=128)[:, t_].partition_broadcast(64), [], ["lnw2"])
            DMA("pool", lnb2[t_ * 64:(t_ + 1) * 64, :, :],
                rw_lnx_b[0:1, :].rearrange("o (q t f) -> o t q f", t=2, f=128)[:, t_].partition_broadcast(64), [], ["lnb2"])
        colv2 = {}
        for nm, apd in (("w0", rw_w0), ("a0", rw_a0), ("kk", rw_k_k), ("ka", rw_k_a), ("rk", rw_r_k)):
            tcol = sbC("col2_" + nm, [128, 4, 2])
            for t_ in range(2):
                for hi_ in range(2):
                    DMA("sp", tcol[t_ * 64:(t_ + 1) * 64, :, hi_],
                        apd.rearrange("(q t hi p) -> t hi p q", t=2, hi=2, p=64)[t_, hi_], ["col2_" + nm], ["col2_" + nm],
                        allow_slow_non_contiguous=True)
            colv2[nm] = tcol
        lora_w2 = sbC("lora_w2", [64, 4, 2, 2, 64], BF16)
        lora_a2 = sbC("lora_a2", [64, 4, 2, 2, 64], BF16)
        for q_ in range(4):
            for t_ in range(2):
                DMA("pool", lora_w2[:, q_, :, t_, :],
                    rw_w_up.rearrange("l (q t hi c) -> l q t hi c", t=2, hi=2, c=64)[:, q_, t_], [], ["lora_w2"])
                DMA("pool", lora_a2[:, q_, :, t_, :],
                    rw_a_up.rearrange("l (q t hi c) -> l q t hi c", t=2, hi=2, c=64)[:, q_, t_], [], ["lora_a2"])
        bd = sbC("bd", [128, 128], BF16)
        OP("pool", "memset", [], ["bd"], bd[:], 0.0)
        OP("pool", "memset", ["bd"], ["bd"], bd[0:64, 0:64], 1.0)
        OP("pool", "memset", ["bd"], ["bd"], bd[64:128, 64:128], 1.0)
        identf2 = sbC("identf2", [128, 64])
        OP("dve", "tensor_copy", ["ident"], ["identf2"], out=identf2[0:64, :], in_=ident[0:64, 0:64])
        OP("dve", "tensor_copy", ["ident", "identf2"], ["identf2"], out=identf2[64:128, :], in_=ident[64:128, 64:128])
        trim2 = sbC("trim2", [128, 320], BF16)
        DMA("sp", trim2[0:64, :], trim_d[:, :], [], ["trim2"])
        DMA("sp", trim2[64:128, :], trim_d[:, :], ["trim2"], ["trim2"])
        rm2 = sbC("rm2", [128, 512])
        OP("pool", "memset", [], ["rm2"], rm2[:], 1.0)
        OP("pool", "memset", ["rm2"], ["rm2"], rm2[:].rearrange("p (n l) -> p n l", l=64)[:, :, 0:1], 0.0)
        NTC = 2052
        TW = sbC("TW", [64, NTC], BF16)
        AD = sbC("AD", [64, NTC], BF16)
        SG0 = sbC("SG0", [128, NTC], BF16)
        SG1 = sbC("SG1", [128, NTC], BF16)
        OP("pool", "memset", [], ["SG1"], SG1[:], 0.0)
        la = sbC("la", [64, 512])
        lb = sbC("lb", [64, 512])
        lg0 = sbC("lg0", [128, 512])
        lg1 = sbC("lg1", [32, 512])
        for (t0, n) in [(0, 512), (512, 512), (1024, 512), (1536, 512), (2048, 4)]:
            DMA("sp", la[:, 0:n], XS[3072:3136, t0:t0 + n], [("XS", 24, t0)], ["la"])
            DMA("sp", lb[:, 0:n], XS[3136:3200, t0:t0 + n], [("XS", 24, t0)], ["lb"])
            DMA("sp", lg0[:, 0:n], XS[3200:3328, t0:t0 + n], [("XS", 25, t0)], ["lg0"])
            DMA("sp", lg1[:, 0:n], XS[3328:3360, t0:t0 + n], [("XS", 26, t0)], ["lg1"])
            OP("act", "activation", ["la"], ["TW"], out=TW[:, t0:t0 + n], in_=la[:, 0:n], func=AF.Tanh)
            OP("act", "activation", ["lb"], ["AD"], out=AD[:, t0:t0 + n], in_=lb[:, 0:n], func=AF.Copy)
            OP("act", "activation", ["lg0"], ["SG0"], out=SG0[:, t0:t0 + n], in_=lg0[:, 0:n], func=AF.Sigmoid)
            OP("act", "activation", ["lg1", "SG1"], ["SG1"], out=SG1[0:32, t0:t0 + n], in_=lg1[:, 0:n], func=AF.Sigmoid)
        stC2 = contextlib.ExitStack()
        sbC_outer = sbC
        def sbC(name, shape, dt=F32):
            return stC2.enter_context(nc.sbuf_tensor(name, list(shape), dt))
        W = {nm: sbC("w_" + nm, [128, 512]) for nm in
             ("r32", "k32", "v32", "lw", "asig", "kk", "rn", "t1", "keff", "bvec", "cum", "ep", "em", "ea")}
        kk2 = sbC("w_kk2", [128, 512], BF16)
        wkvo = sbC("wkvo", [128, 2, 64])
        bufs = []
        for par in range(2):
            d = {}
            d["AR"] = sbC("AR%d" % par, [128, 2, 32, 192], BF16)
            d["KT"] = sbC("KTt%d" % par, [128, 2, 32, 64], BF16)
            d["BT"] = sbC("BTt%d" % par, [128, 2, 32, 64], BF16)
            d["VT"] = sbC("VTt%d" % par, [128, 2, 32, 64], BF16)
            d["RK"] = sbC("RK%d" % par, [128, 2, 2048], BF16)
            d["WL"] = sbC("WL%d" % par, [128, 2, 32])
            d["STf"] = sbC("STf%d" % par, [128, 2, 64])
            d["STb"] = sbC("STb%d" % par, [128, 2, 64], BF16)
            d["MBs"] = sbC("MBs%d" % par, [128, 2, 192], BF16)
            d["MKs"] = sbC("MKs%d" % par, [128, 2, 192], BF16)
            d["MNs"] = sbC("MNs%d" % par, [128, 2, 128], BF16)
            d["Uf"] = sbC("Uf%d" % par, [128, 2, 64])
            d["Ub"] = sbC("Ub%d" % par, [128, 2, 64], BF16)
            d["PPs"] = [sbC("PPs%d_%d" % (par, i), [128, 2, 128], BF16) for i in range(2)]
            d["tmpS"] = sbC("tmpS%d" % par, [128, 2, 64])
            d["yn"] = sbC("yn%d" % par, [128, 2, 64])
            d["st"] = sbC("st%d" % par, [128, 12])
            d["jk"] = sbC("jk%d" % par, [128, 64])
            d["Ost"] = sbC("Ost%d" % par, [128, 18, 128], BF16)
            for t_i in range(2):
                for hi_i in range(2):
                    OP("dve", "tensor_copy", ["ident"], [("ARI", par)],
                       out=d["AR"][t_i * 64:(t_i + 1) * 64, hi_i, :, 128:192],
                       in_=ident[t_i * 64:(t_i + 1) * 64, t_i * 64:(t_i + 1) * 64].unsqueeze(1).broadcast_to([64, 32, 64]))
            bufs.append(d)

        PZW = (6, 384)
        PZA = (7, 384)
        PSS = (5, 260)

        def prep_block(q4, par, tb):
            B = bufs[par]
            kAR, kKT, kBT, kVT, kRK, kWL = [(x, par, tb) for x in ("AR", "KT", "BT", "VT", "RK", "WL")]
            t0 = tb * 512
            c0 = tb * 8
            for hi in range(2):
                hh = [4 * q4 + hi, 4 * q4 + 2 + hi]
                cs = slice(q4 * 2 + hi, q4 * 2 + hi + 1)
                cv_ = lambda nm: colv2[nm][:].rearrange("p q h -> p (q h)")[:, cs]
                for t_, h in enumerate(hh):
                    ps_ = slice(t_ * 64, (t_ + 1) * 64)
                    DMA("sp", W["r32"][ps_, :], XS[h * 64:(h + 1) * 64, t0:t0 + 512], [("XS", h // 2, t0)], ["r32"])
                    DMA("sp", W["k32"][ps_, :], XS[1024 + h * 64:1024 + (h + 1) * 64, t0:t0 + 512],
                        [("XS", 8 + h // 2, t0)], ["k32"])
                    DMA("sp", W["v32"][ps_, :], XS[2048 + h * 64:2048 + (h + 1) * 64, t0:t0 + 512],
                        [("XS", 16 + h // 2, t0)], ["v32"])
                yield
                for pc in range(4):
                    cc = slice(pc * 128, (pc + 1) * 128)
                    tc_ = slice(t0 + pc * 128, t0 + (pc + 1) * 128)
                    zw = pmm[PZW[0]][:, PZW[1]:PZW[1] + 128]
                    za = pmm[PZA[0]][:, PZA[1]:PZA[1] + 128]
                    OP("pe", "matmul", ["lora_w2", "TW"], [("pmm", PZW[0])], zw,
                       lhsT=lora_w2[:, q4, hi, :, :].rearrange("l t c -> l (t c)"), rhs=TW[:, tc_], start=True, stop=True)
                    OP("act", "activation", [("pmm", PZW[0]), "col2_w0"], ["lw"], out=W["lw"][:, cc], in_=zw,
                       func=AF.Sigmoid, bias=cv_("w0"))
                    OP("pe", "matmul", ["lora_a2", "AD"], [("pmm", PZA[0])], za,
                       lhsT=lora_a2[:, q4, hi, :, :].rearrange("l t c -> l (t c)"), rhs=AD[:, tc_], start=True, stop=True)
                    OP("act", "activation", [("pmm", PZA[0]), "col2_a0"], ["asig"], out=W["asig"][:, cc], in_=za,
                       func=AF.Sigmoid, bias=cv_("a0"))
                    yield
                OP("dve", "tensor_scalar", ["lw"], ["lw"], W["lw"][:], W["lw"][:], -0.6065306597126334, None, ALU.mult)
                OP("dve", "tensor_scalar", ["k32", "col2_kk"], ["kk"], W["kk"][:], W["k32"][:], cv_("kk"), None, ALU.mult)
                OP("dve", "tensor_tensor", ["kk"], ["kk2"], out=kk2[:], in0=W["kk"][:], in1=W["kk"][:], op=ALU.mult)
                yield
                for pc in range(4):
                    cc = slice(pc * 128, (pc + 1) * 128)
                    zs = pmm[PSS[0]][:, PSS[1]:PSS[1] + 128]
                    OP("pe", "matmul", ["bd", "kk2"], [("pmm", PSS[0])], zs, lhsT=bd[:], rhs=kk2[:, cc], start=True,
                       stop=True)
                    OP("dve", "tensor_scalar_max", [("pmm", PSS[0])], ["rn"], out=W["rn"][:, cc], in0=zs, scalar1=1e-24)
                    yield
                OP("act", "activation", ["rn"], ["rn"], out=W["rn"][:], in_=W["rn"][:], func=AF.Ln)
                OP("act", "activation", ["rn"], ["rn"], out=W["rn"][:], in_=W["rn"][:], func=AF.Exp, scale=-0.5)
                OP("dve", "tensor_tensor", ["kk", "rn"], ["kk"], out=W["kk"][:], in0=W["kk"][:], in1=W["rn"][:],
                   op=ALU.mult)
                yield
                OP("dve", "tensor_scalar", ["asig", "col2_ka"], ["t1"], W["t1"][:], W["asig"][:], -1.0, cv_("ka"),
                   ALU.add, ALU.mult)
                OP("dve", "scalar_tensor_tensor", ["t1", "k32"], ["keff"], out=W["keff"][:], in0=W["t1"][:],
                   scalar=1.0, in1=W["k32"][:], op0=ALU.add, op1=ALU.mult)
                OP("dve", "tensor_tensor", ["kk", "asig"], ["bvec"], out=W["bvec"][:], in0=W["kk"][:],
                   in1=W["asig"][:], op=ALU.mult)
                yield
                OP("dve", "tensor_tensor_scan", ["rm2", "lw"], ["cum"], out=W["cum"][:], data0=rm2[:],
                   data1=W["lw"][:], initial=0.0, op0=ALU.mult, op1=ALU.add)
                OP("act", "activation", ["cum"], ["ep"], out=W["ep"][:], in_=W["cum"][:], func=AF.Exp)
                OP("act", "activation", ["cum"], ["em"], out=W["em"][:], in_=W["cum"][:], func=AF.Exp, scale=-1.0)
                OP("dve", "tensor_tensor", ["cum", "lw"], ["ea"], out=W["ea"][:], in0=W["cum"][:], in1=W["lw"][:],
                   op=ALU.subtract)
                OP("act", "activation", ["ea"], ["ea"], out=W["ea"][:], in_=W["ea"][:], func=AF.Exp)
                yield
                v3 = lambda a: a[:].rearrange("p (n l) -> p n l", l=64)
                OP("dve", "tensor_tensor", ["r32", "ep"], [kAR], out=B["AR"][:, hi, c0:c0 + 8, 64:128],
                   in0=v3(W["r32"]), in1=v3(W["ep"]), op=ALU.mult)
                OP("dve", "scalar_tensor_tensor", ["kk", "ea"], [kAR], out=B["AR"][:, hi, c0:c0 + 8, 0:64],
                   in0=v3(W["kk"]), scalar=-1.0, in1=v3(W["ea"]), op0=ALU.mult, op1=ALU.mult)
                OP("pool", "tensor_tensor", ["keff", "em"], [kKT], out=B["KT"][:, hi, c0:c0 + 8, :],
                   in0=v3(W["keff"]), in1=v3(W["em"]), op=ALU.mult)
                yield
                OP("pool", "tensor_tensor", ["bvec", "em"], [kBT], out=B["BT"][:, hi, c0:c0 + 8, :],
                   in0=v3(W["bvec"]), in1=v3(W["em"]), op=ALU.mult)
                OP("act", "activation", ["v32"], [kVT], out=B["VT"][:, hi, c0:c0 + 8, :], in_=v3(W["v32"]),
                   func=AF.Copy)
                OP("dve", "tensor_copy", ["ep"], [kWL], out=B["WL"][:, hi, c0:c0 + 8], in_=v3(W["ep"])[:, :, 63])
                OP("dve", "scalar_tensor_tensor", ["r32", "col2_rk", "keff"], [kRK],
                   out=B["RK"][:, hi, t0:t0 + 512], in0=W["r32"][:], scalar=cv_("rk"), in1=W["keff"][:],
                   op0=ALU.mult, op1=ALU.mult)
                yield

        m6 = trim2[:, 0:192].unsqueeze(1).broadcast_to([128, 2, 192])
        ml = trim2[:, 192:320].unsqueeze(1).broadcast_to([128, 2, 128])
        def _v(bank, c0, ncol, c):
            return pmm[bank][:, c0:c0 + ncol].rearrange("p (h c) -> p h c", c=c)
        PV = [
            (_v(0, 0, 384, 192), _v(1, 0, 384, 192), _v(2, 0, 256, 128), _v(2, 256, 128, 64), _v(3, 0, 128, 64),
             _v(4, 0, 256, 128), _v(5, 0, 128, 64), pmm[5][:, 128:256], pmm[5][:, 256:258],
             ("pmm", 0), ("pmm", 1), ("pmm", 2), ("pmm", 2), ("pmm", 3), ("pmm", 4), ("pmm", 5), ("pmm", 5), ("pmm", 5)),
            (_v(6, 0, 384, 192), _v(7, 0, 384, 192), _v(3, 128, 256, 128), _v(3, 384, 128, 64), _v(0, 384, 128, 64),
             _v(4, 256, 256, 128), _v(1, 384, 128, 64), pmm[2][:, 384:512], pmm[5][:, 258:260],
             ("pmm", 6), ("pmm", 7), ("pmm", 3), ("pmm", 3), ("pmm", 0), ("pmm", 4), ("pmm", 1), ("pmm", 2), ("pmm", 5)),
        ]
        LNX_EPS = 64 * 1e-5
        HALF = [slice(0, 64), slice(64, 128)]
        TH = [(t_, hi) for t_ in range(2) for hi in range(2)]

        def chunk_step(q4, n, par):
            B = bufs[par]
            K = lambda x: (x, par)
            pMB, pMK, pMN, pS, pZ, pPP, pY, pG, pBo, kMB, kMK, kMN, kS, kZ, kPP, kY, kG, kBo = PV[par]
            AR, KT_, BT_, VT_ = B["AR"], B["KT"], B["BT"], B["VT"]
            prepk = [(x, par, n // 8) for x in ("AR", "KT", "BT", "VT")]
            for t_, hi in TH:
                p_ = HALF[t_]
                idh = ident[p_, p_]
                OP("pe", "matmul", prepk + [("ARI", par)], [kMB], pMB[p_, hi, :], lhsT=BT_[p_, hi, n, :],
                   rhs=AR[p_, hi, n, :], start=True, stop=True)
                OP("pe", "matmul", prepk + [("ARI", par)], [kMK], pMK[p_, hi, :], lhsT=KT_[p_, hi, n, :],
                   rhs=AR[p_, hi, n, :], start=True, stop=True)
                OP("pe", "matmul", prepk, [kMN], pMN[p_, hi, 0:64], lhsT=AR[p_, hi, n, 0:64], rhs=BT_[p_, hi, n, :],
                   start=True, stop=True)
                OP("pe", "matmul", prepk + ["ident"], [kMN], pMN[p_, hi, 64:128], lhsT=VT_[p_, hi, n, :], rhs=idh,
                   start=True, stop=True)
            yield
            OP("dve", "tensor_tensor", [kMB, "trim2"], [K("MBs")], out=B["MBs"][:], in0=pMB, in1=m6, op=ALU.mult)
            OP("dve", "tensor_tensor", [kMK, "trim2"], [K("MKs")], out=B["MKs"][:], in0=pMK, in1=m6, op=ALU.mult)
            OP("dve", "tensor_tensor", [kMN, "trim2"], [K("MNs")], out=B["MNs"][:], in0=pMN, in1=ml, op=ALU.mult)
            yield
            for t_, hi in TH:
                p_ = HALF[t_]
                OP("pe", "matmul", prepk + [K("STb")], [kZ], pZ[p_, hi, :], lhsT=AR[p_, hi, n, 0:64],
                   rhs=B["STb"][p_, hi, :], start=True, stop=False)
                OP("pe", "matmul", [K("MKs"), K("MNs")], [kZ], pZ[p_, hi, :], lhsT=B["MKs"][p_, hi, 0:64],
                   rhs=B["MNs"][p_, hi, 64:128], start=False, stop=True)
            OP("act", "activation", [kZ], [K("Uf")], out=B["Uf"][:], in_=pZ, func=AF.Copy)
            OP("act", "activation", [K("Uf")], [K("Ub")], out=B["Ub"][:], in_=B["Uf"][:], func=AF.Copy)
            yield
            PT = lambda p_, hi: B["MBs"][p_, hi, 0:64]
            Pm = lambda p_, hi: B["MNs"][p_, hi, 0:64]
            pkeys = [K("MBs"), K("MNs")]
            for it in range(6):
                for t_, hi in TH:
                    p_ = HALF[t_]
                    OP("pe", "matmul", pkeys + [K("Ub")], [kZ], pZ[p_, hi, :], lhsT=PT(p_, hi), rhs=B["Ub"][p_, hi, :],
                       start=True, stop=True)
                yield
                OP("dve", "tensor_tensor", [kZ, K("Uf")], [K("Uf")], out=B["Uf"][:], in0=pZ, in1=B["Uf"][:],
                   op=ALU.add)
                OP("act", "activation", [K("Uf")], [K("Ub")], out=B["Ub"][:], in_=B["Uf"][:], func=AF.Copy)
                if it < 5:
                    for t_, hi in TH:
                        p_ = HALF[t_]
                        OP("pe", "matmul", pkeys, [kPP], pPP[p_, hi, 0:64], lhsT=Pm(p_, hi), rhs=PT(p_, hi), start=True,
                           stop=True)
                        OP("pe", "matmul", pkeys, [kPP], pPP[p_, hi, 64:128], lhsT=PT(p_, hi), rhs=Pm(p_, hi), start=True,
                           stop=True)
                    pp = B["PPs"][it % 2]
                    yield
                    OP("act", "activation", [kPP], [K("PPs%d" % (it % 2))], out=pp[:], in_=pPP, func=AF.Copy)
                    PT = lambda p_, hi, pp=pp: pp[p_, hi, 0:64]
                    Pm = lambda p_, hi, pp=pp: pp[p_, hi, 64:128]
                    pkeys = [K("PPs%d" % (it % 2))]
            if n >= 14:
                ci = n - 14
                for t_, hi in TH:
                    p_ = HALF[t_]
                    OP("pe", "matmul", prepk + [K("STb")], [kY], pY[p_, hi, :], lhsT=AR[p_, hi, n, 64:128],
                       rhs=B["STb"][p_, hi, :], start=True, stop=False)
                    OP("pe", "matmul", [K("MBs"), K("Ub")], [kY], pY[p_, hi, :], lhsT=B["MBs"][p_, hi, 64:128],
                       rhs=B["Ub"][p_, hi, :], start=False, stop=False)
                    OP("pe", "matmul", [K("MKs"), K("MNs")], [kY], pY[p_, hi, :], lhsT=B["MKs"][p_, hi, 64:128],
                       rhs=B["MNs"][p_, hi, 64:128], start=False, stop=True)
                for t_ in range(2):
                    p_ = HALF[t_]
                    gc = slice((4 * q4 + 2 * t_) * 64, (4 * q4 + 2 * t_) * 64 + 128)
                    OP("pe", "matmul", ["SG0", "gupb"], [kG], pG[p_, :], lhsT=SG0[:, n * 64:(n + 1) * 64],
                       rhs=gupb[:, 0, gc], start=True, stop=False)
                    OP("pe", "matmul", ["SG1", "gupb"], [kG], pG[p_, :], lhsT=SG1[:, n * 64:(n + 1) * 64],
                       rhs=gupb[:, 1, gc], start=False, stop=True)
                for t_, hi in TH:
                    p_ = HALF[t_]
                    OP("pe", "matmul", [("RK", par, n // 8), "bd"], [kBo], pBo[p_, hi:hi + 1],
                       lhsT=B["RK"][p_, hi, n * 64:(n + 1) * 64], rhs=bd[p_, t_ * 64:t_ * 64 + 1], start=True, stop=True)
                yield
                stt_ = B["st"]
                ks = K("st")
                OP("dve", "tensor_reduce", [kY], [ks], out=stt_[:, 0:2], in_=pY, axis=AX.X, op=ALU.add)
                for hi in range(2):
                    OP("act", "activation", [kY], [K("jk"), ks], out=B["jk"][:], in_=pY[:, hi, :],
                       func=AF.Square, accum_out=stt_[:, 2 + hi:3 + hi])
                OP("act", "activation", [kBo], [ks], out=stt_[:, 10:12], in_=pBo, func=AF.Copy)
                OP("dve", "tensor_scalar", [ks], [ks], stt_[:, 4:6], stt_[:, 0:2], 1.0 / 64, None, ALU.mult)
                OP("dve", "tensor_tensor", [ks], [ks], out=stt_[:, 6:8], in0=stt_[:, 4:6], in1=stt_[:, 4:6], op=ALU.mult)
                OP("dve", "scalar_tensor_tensor", [ks], [ks], out=stt_[:, 8:10], in0=stt_[:, 2:4], scalar=1.0 / 64,
                   in1=stt_[:, 6:8], op0=ALU.mult, op1=ALU.subtract)
                OP("dve", "tensor_scalar", [ks], [ks], stt_[:, 8:10], stt_[:, 8:10], LNX_EPS, None, ALU.add)
                OP("act", "activation", [ks], [ks], out=stt_[:, 8:10], in_=stt_[:, 8:10], func=AF.Ln)
                OP("act", "activation", [ks], [ks], out=stt_[:, 8:10], in_=stt_[:, 8:10], func=AF.Exp, scale=-0.5)
                for hi in range(2):
                    OP("dve", "tensor_scalar", [kY, ks], [K("yn")], B["yn"][:, hi, :], pY[:, hi, :],
                       stt_[:, 4 + hi:5 + hi], stt_[:, 8 + hi:9 + hi], ALU.subtract, ALU.mult)
                ynf = B["yn"][:].rearrange("p h c -> p (h c)")
                OP("pool", "tensor_tensor", [K("yn"), "lnw2"], [K("yn")], out=ynf, in0=ynf, in1=lnw2[:, q4, :], op=ALU.mult)
                OP("pool", "tensor_tensor", [K("yn"), "lnb2"], [K("yn")], out=ynf, in0=ynf, in1=lnb2[:, q4, :], op=ALU.add)
                for hi in range(2):
                    OP("dve", "scalar_tensor_tensor", [K("MNs"), ks, K("yn")], [K("yn")], out=B["yn"][:, hi, :],
                       in0=B["MNs"][:, hi, 64:128], scalar=stt_[:, 10 + hi:11 + hi], in1=B["yn"][:, hi, :],
                       op0=ALU.mult, op1=ALU.add)
                OP("dve", "tensor_tensor", [K("yn"), kG], [K("Ost")], out=B["Ost"][:, ci, :], in0=ynf, in1=pG, op=ALU.mult)
            for t_, hi in TH:
                p_ = HALF[t_]
                OP("pe", "matmul", [K("MBs"), K("Ub")], [kS], pS[p_, hi, :], lhsT=B["MBs"][p_, hi, 128:192],
                   rhs=B["Ub"][p_, hi, :], start=True, stop=False)
                OP("pe", "matmul", [K("MKs"), K("MNs")], [kS], pS[p_, hi, :], lhsT=B["MKs"][p_, hi, 128:192],
                   rhs=B["MNs"][p_, hi, 64:128], start=False, stop=True)
            yield
            OP("dve", "tensor_tensor", [kS, K("STf")], [K("tmpS")], out=B["tmpS"][:], in0=pS, in1=B["STf"][:],
               op=ALU.add)
            OP("pool", "tensor_tensor", [K("tmpS"), ("WL", par, n // 8)], [K("STf")], out=B["STf"][:], in0=B["tmpS"][:],
               in1=B["WL"][:, :, n:n + 1].broadcast_to([128, 2, 64]), op=ALU.mult)
            OP("act", "activation", [K("STf")], [K("STb")], out=B["STb"][:], in_=B["STf"][:], func=AF.Copy)

        def chain_block(q4, par, tb):
            for n in range(tb * 8, tb * 8 + 8):
                yield from chunk_step(q4, n, par)

        def prep_seq(q0, tb):
            for par in range(2):
                yield from prep_block(q0 + par, par, tb)

        def round_robin(gens):
            live = list(gens)
            while live:
                for g in list(live):
                    try:
                        next(g)
                    except StopIteration:
                        live.remove(g)

        for q0 in range(0, 4, 2):
            for par in range(2):
                B = bufs[par]
                OP("pool", "memset", [], [("STf", par)], B["STf"][:], 0.0)
                OP("pool", "memset", [], [("STb", par)], B["STb"][:], 0.0)
            round_robin([prep_seq(q0, 0)])
            for tb in range(4):
                gens = [chain_block(q0 + par, par, tb) for par in range(2)]
                if tb < 3:
                    gens.append(prep_seq(q0, tb + 1))
                round_robin(gens)
            for q4 in (q0, q0 + 1):
                par = q4 % 2
                B = bufs[par]
                for t_, hi in TH:
                    p_ = HALF[t_]
                    OP("pe", "matmul", [("STf", par), "identf2"], [("pmm", 3)], pmm[3][p_, hi * 64:(hi + 1) * 64],
                       lhsT=B["STf"][p_, hi, :], rhs=identf2[p_, :], start=True, stop=True)
                OP("dve", "tensor_copy", [("pmm", 3)], ["wkvo"], out=wkvo[:].rearrange("p h c -> p (h c)"),
                   in_=pmm[3][:, 0:128])
                for t_ in range(2):
                    p_ = HALF[t_]
                    h0 = 4 * q4 + 2 * t_
                    DMA("sp", pwkv_out[h0:h0 + 2, :, :].rearrange("h i j -> i h j"), wkvo[p_, :, :], ["wkvo"],
                        [("pwkv", q4, t_)])
                    DMA("sp", O_scr[0:1152, 1024 + h0 * 64:1024 + h0 * 64 + 128].rearrange("(c p) f -> p c f", p=64),
                        B["Ost"][p_, :, :], [("Ost", par)], [("Oscr_rw", 2 * q4 + t_)])
        P.barrier()
        stC2.close()
        sbC = sbC_outer
        mark("C_prompt_done")
        XC = {}
        for nm, r0 in (("r", 0), ("k", 1024), ("v", 2048)):
            tcx = sbC("xc_" + nm, [64, 16, 4])
            DMA("sp", tcx[:], XS[r0:r0 + 1024, 2048:2052].rearrange("(h p) b -> p h b", p=64),
                [("XS", ci_, 2048) for ci_ in range(r0 // 128, r0 // 128 + 8)], ["xc_" + nm])
            XC[nm] = tcx
        lnwc = sbC("lnwc", [64, 16])
        lnbc = sbC("lnbc", [64, 16])
        DMA("sp", lnwc[:], rw_lnx_w_c.rearrange("(h p) -> p h", p=64), [], ["lnwc"], allow_slow_non_contiguous=True)
        DMA("sp", lnbc[:], rw_lnx_b_c.rearrange("(h p) -> p h", p=64), [], ["lnbc"], allow_slow_non_contiguous=True)
        onesf = sbC("onesf", [64, 64])
        OP("pool", "memset", [], ["onesf"], onesf[:], 1.0)
        SW = {nm: sbC("sw_" + nm, [64, 16, 4]) for nm in
              ("lw", "a", "kk", "kk2", "rn", "t1", "keff", "b", "rk", "g", "y", "y2", "mean", "var", "yn", "bon", "o")}
        vec5 = sbC("vec5", [64, 16, 4, 5])
        ob16 = sbC("ob16", [64, 16, 4], BF16)
        bc16 = lambda t: t[:, :].unsqueeze(2).broadcast_to([64, 16, 4])
        pz = pmm[0][0:64, 0:64].rearrange("p (h b) -> p h b", b=4)
        pz2 = pmm[1][0:64, 0:64].rearrange("p (h b) -> p h b", b=4)
        pz3 = pmm[2][0:64, 0:64].rearrange("p (h b) -> p h b", b=4)
        for h in range(16):
            hc = slice(h * 64, (h + 1) * 64)
            OP("pe", "matmul", ["lora_w", "TW"], [("pmm", 0)], pz[:, h, :], lhsT=lora_w[:, hc], rhs=TW[:, 2048:2052],
               start=True, stop=True)
            OP("pe", "matmul", ["lora_a", "AD"], [("pmm", 1)], pz2[:, h, :], lhsT=lora_a[:, hc], rhs=AD[:, 2048:2052],
               start=True, stop=True)
            OP("pe", "matmul", ["SG0", "gupb"], [("pmm", 2)], pz3[:, h, :], lhsT=gupb[:, 0, hc], rhs=SG0[:, 2048:2052],
               start=True, stop=False)
            OP("pe", "matmul", ["SG1", "gupb"], [("pmm", 2)], pz3[:, h, :], lhsT=gupb[:, 1, hc], rhs=SG1[:, 2048:2052],
               start=False, stop=True)
        OP("dve", "tensor_tensor", [("pmm", 0), "col_w0"], ["sw_lw"], out=SW["lw"][:], in0=pz, in1=bc16(colv["w0"]),
           op=ALU.add)
        OP("act", "activation", ["sw_lw"], ["sw_lw"], out=SW["lw"][:], in_=SW["lw"][:], func=AF.Sigmoid)
        OP("act", "activation", ["sw_lw"], ["vec5"], out=vec5[:, :, :, 1], in_=SW["lw"][:], func=AF.Exp,
           scale=-0.6065306597126334)
        OP("dve", "tensor_tensor", [("pmm", 1), "col_a0"], ["sw_a"], out=SW["a"][:], in0=pz2, in1=bc16(colv["a0"]),
           op=ALU.add)
        OP("act", "activation", ["sw_a"], ["sw_a"], out=SW["a"][:], in_=SW["a"][:], func=AF.Sigmoid)
        OP("act", "activation", [("pmm", 2)], ["sw_g"], out=SW["g"][:], in_=pz3, func=AF.Copy)
        OP("dve", "tensor_tensor", ["xc_k", "col_kk"], ["sw_kk"], out=SW["kk"][:], in0=XC["k"][:], in1=bc16(colv["kk"]),
           op=ALU.mult)
        OP("dve", "tensor_tensor", ["sw_kk"], ["sw_kk2"], out=SW["kk2"][:], in0=SW["kk"][:], in1=SW["kk"][:], op=ALU.mult)
        fl = lambda t: t[:].rearrange("p h b -> p (h b)")
        OP("pe", "matmul", ["onesf", "sw_kk2"], [("pmm", 0)], pmm[0][0:64, 0:64], lhsT=onesf[:], rhs=fl(SW["kk2"]),
           start=True, stop=True)
        OP("dve", "tensor_scalar_max", [("pmm", 0)], ["sw_rn"], out=fl(SW["rn"]), in0=pmm[0][0:64, 0:64], scalar1=1e-24)
        OP("act", "activation", ["sw_rn"], ["sw_rn"], out=SW["rn"][:], in_=SW["rn"][:], func=AF.Ln)
        OP("act", "activation", ["sw_rn"], ["sw_rn"], out=SW["rn"][:], in_=SW["rn"][:], func=AF.Exp, scale=-0.5)
        OP("dve", "tensor_tensor", ["sw_kk", "sw_rn"], ["sw_kk"], out=SW["kk"][:], in0=SW["kk"][:], in1=SW["rn"][:],
           op=ALU.mult)
        OP("dve", "tensor_scalar", ["sw_kk"], ["vec5"], vec5[:, :, :, 0], SW["kk"][:], -1.0, None, ALU.mult)
        OP("dve", "tensor_tensor", ["sw_kk", "sw_a", "vec5"], ["vec5"], out=vec5[:, :, :, 2], in0=SW["kk"][:],
           in1=SW["a"][:], op=ALU.mult)
        OP("dve", "tensor_scalar", ["sw_a"], ["sw_t1"], SW["t1"][:], SW["a"][:], -1.0, None, ALU.add)
        OP("dve", "tensor_tensor", ["sw_t1", "col_ka"], ["sw_t1"], out=SW["t1"][:], in0=SW["t1"][:], in1=bc16(colv["ka"]),
           op=ALU.mult)
        OP("dve", "scalar_tensor_tensor", ["sw_t1", "xc_k"], ["sw_keff"], out=SW["keff"][:], in0=SW["t1"][:], scalar=1.0,
           in1=XC["k"][:], op0=ALU.add, op1=ALU.mult)
        OP("dve", "tensor_copy", ["sw_keff", "vec5"], ["vec5"], out=vec5[:, :, :, 3], in_=SW["keff"][:])
        OP("dve", "tensor_copy", ["xc_r", "vec5"], ["vec5"], out=vec5[:, :, :, 4], in_=XC["r"][:])
        OP("dve", "tensor_tensor", ["xc_r", "sw_keff"], ["sw_rk"], out=SW["rk"][:], in0=XC["r"][:], in1=SW["keff"][:],
           op=ALU.mult)
        OP("dve", "tensor_tensor", ["sw_rk", "col_rk"], ["sw_rk"], out=SW["rk"][:], in0=SW["rk"][:], in1=bc16(colv["rk"]),
           op=ALU.mult)
        OP("pe", "matmul", ["onesf", "sw_rk"], [("pmm", 1)], pmm[1][0:64, 0:64], lhsT=onesf[:], rhs=fl(SW["rk"]),
           start=True, stop=True)
        OP("dve", "tensor_tensor", [("pmm", 1), "xc_v"], ["sw_bon"], out=fl(SW["bon"]), in0=pmm[1][0:64, 0:64],
           in1=fl(XC["v"]), op=ALU.mult)
        Sin = [sbC("Sin%d" % i, [64, 16, 64]) for i in range(2)]
        Sout = [sbC("Sout%d" % i, [64, 16, 64]) for i in range(2)]
        D5 = [sbC("D5_%d" % i, [64, 5, 64]) for i in range(2)]
        tm1 = sbC("tm1", [64, 64])
        sa = sbC("sa_s", [64, 1])
        k5 = 0
        for bi in range(4):
            sb_ = bi % 2
            DMA("sp", Sin[sb_][:], swkv_in[bi].rearrange("h i j -> i h j"), [], [("Sin", sb_)])
            for h in range(16):
                d5 = D5[k5 % 2]
                dk = ("D5", k5 % 2)
                pbk = 3 + (k5 % 2)
                pBC = pmm[pbk][0:64, 0:320].rearrange("p (v j) -> p v j", j=64)
                k5 += 1
                OP("dve", "tensor_tensor", ["identf", "vec5"], [dk], out=d5[:],
                   in0=identf[:].unsqueeze(1).broadcast_to([64, 5, 64]),
                   in1=vec5[:, h, bi, :].unsqueeze(2).broadcast_to([64, 5, 64]), op=ALU.mult)
                OP("pe", "matmul", ["onesf", dk], [("pmm", pbk)], pmm[pbk][0:64, 0:320], lhsT=onesf[:],
                   rhs=d5[:].rearrange("p v j -> p (v j)"), start=True, stop=True)
                S_ = Sin[sb_][:, h, :]
                So = Sout[sb_][:, h, :]
                OP("dve", "tensor_tensor", [("Sin", sb_), ("pmm", pbk)], ["tm1"], out=tm1[:], in0=pBC[:, 0, :], in1=S_,
                   op=ALU.mult)
                OP("dve", "tensor_reduce", ["tm1"], ["sa_s"], out=sa[:], in_=tm1[:], axis=AX.X, op=ALU.add)
                OP("dve", "tensor_tensor", [("Sin", sb_), ("pmm", pbk)], [("Sout", sb_)], out=So, in0=pBC[:, 1, :], in1=S_,
                   op=ALU.mult)
                OP("dve", "scalar_tensor_tensor", [("pmm", pbk), "sa_s", ("Sout", sb_)], [("Sout", sb_)], out=So,
                   in0=pBC[:, 2, :], scalar=sa[:, 0:1], in1=So, op0=ALU.mult, op1=ALU.add)
                OP("dve", "scalar_tensor_tensor", [("pmm", pbk), "xc_v", ("Sout", sb_)], [("Sout", sb_)], out=So,
                   in0=pBC[:, 3, :], scalar=XC["v"][:, h, bi:bi + 1], in1=So, op0=ALU.mult, op1=ALU.add)
                OP("dve", "tensor_tensor", [("pmm", pbk), ("Sout", sb_)], ["tm1"], out=tm1[:], in0=pBC[:, 4, :], in1=So,
                   op=ALU.mult)
                OP("dve", "tensor_reduce", ["tm1"], ["sw_y"], out=SW["y"][:, h, bi:bi + 1], in_=tm1[:], axis=AX.X,
                   op=ALU.add)
            DMA("pool", swkv_out[bi].rearrange("h i j -> i h j"), Sout[sb_][:], [("Sout", sb_)], [("swkv", bi)])
        OP("pe", "matmul", ["onesf", "sw_y"], [("pmm", 0)], pmm[0][0:64, 0:64], lhsT=onesf[:], rhs=fl(SW["y"]),
           start=True, stop=True)
        OP("dve", "tensor_scalar", [("pmm", 0)], ["sw_mean"], fl(SW["mean"]), pmm[0][0:64, 0:64], 1.0 / 64, None, ALU.mult)
        OP("dve", "tensor_tensor", ["sw_y", "sw_mean"], ["sw_yn"], out=SW["yn"][:], in0=SW["y"][:], in1=SW["mean"][:],
           op=ALU.subtract)
        OP("dve", "tensor_tensor", ["sw_yn"], ["sw_y2"], out=SW["y2"][:], in0=SW["yn"][:], in1=SW["yn"][:], op=ALU.mult)
        OP("pe", "matmul", ["onesf", "sw_y2"], [("pmm", 1)], pmm[1][0:64, 0:64], lhsT=onesf[:], rhs=fl(SW["y2"]),
           start=True, stop=True)
        OP("act", "activation", [("pmm", 1), "epsl"], ["sw_var"], out=fl(SW["var"]), in_=pmm[1][0:64, 0:64], func=AF.Ln,
           scale=1.0 / 64, bias=epsl[0:64, 0:1])
        OP("act", "activation", ["sw_var"], ["sw_var"], out=SW["var"][:], in_=SW["var"][:], func=AF.Exp, scale=-0.5)
        OP("dve", "tensor_tensor", ["sw_yn", "sw_var"], ["sw_yn"], out=SW["yn"][:], in0=SW["yn"][:], in1=SW["var"][:],
           op=ALU.mult)
        OP("dve", "tensor_tensor", ["sw_yn", "lnwc"], ["sw_yn"], out=SW["yn"][:], in0=SW["yn"][:], in1=bc16(lnwc),
           op=ALU.mult)
        OP("dve", "tensor_tensor", ["sw_yn", "lnbc"], ["sw_yn"], out=SW["yn"][:], in0=SW["yn"][:], in1=bc16(lnbc),
           op=ALU.add)
        OP("dve", "tensor_tensor", ["sw_yn", "sw_bon"], ["sw_yn"], out=SW["yn"][:], in0=SW["yn"][:], in1=SW["bon"][:],
           op=ALU.add)
        OP("dve", "tensor_tensor", ["sw_yn", "sw_g"], ["ob16"], out=ob16[:], in0=SW["yn"][:], in1=SW["g"][:], op=ALU.mult)
        for bi in range(4):
            DMA("sp", O_scr[1152 + bi, 1024:2048].rearrange("(h p) -> p h", p=64), ob16[:, :, bi], ["ob16"],
                ["Oscr_samp_rw"], allow_slow_non_contiguous=True)
        P.barrier()
        stC.close()
        mark("C_done")
        NT2 = 1280
        stD = contextlib.ExitStack()
        def sbD(name, shape, dt=F32):
            return stD.enter_context(nc.sbuf_tensor(name, list(shape), dt))
        OT = sbD("OT", [128, 16, NT2], BF16)
        otile = [sbD("otile%d" % i, [128, 2048], BF16) for i in range(2)]
        zt = sbD("zt", [128, 2048], BF16)
        OP("pool", "memset", [], ["zt"], zt[:], 0.0)
        if not HAVE_SAMPLE:
            DMA("sp", O_scr[1152:1280, :], zt[:], ["zt"], ["Oscr_samp"])
        else:
            DMA("sp", O_scr[1156:1280, :], zt[0:124, :], ["zt"], ["Oscr_samp_pad"])
        okeys = ["Oscr_att", "Oscr_samp", "Oscr_samp_pad", "Oscr_samp_rw"] + [("Oscr_rw", i) for i in range(8)]
        for i in range(10):
            ob = i % 2
            DMA("sp", otile[ob][:], O_scr[i * 128:(i + 1) * 128, :], okeys, [("otile", ob)])
            for g4 in range(4):
                pb = g4 % 2
                for j in range(4):
                    kc = g4 * 4 + j
                    OP("pe", "transpose", [("otile", ob), "ident"], ["pT%d" % pb], out=pT[pb][:, j, :],
                       in_=otile[ob][:, kc * 128:(kc + 1) * 128], identity=ident[:])
                OP("dve", "tensor_copy", ["pT%d" % pb], [("OT", i)], out=OT[:, g4 * 4:(g4 + 1) * 4, i * 128:(i + 1) * 128],
                   in_=pT[pb])
        wbf_d = [sbD("wbf_d%d" % i, [128, 16, 256], BF16) for i in range(3)]
        xres = [sbD("xres%d" % i, [128, 256]) for i in range(3)]
        xmo = [sbD("xmo%d" % i, [128, 256]) for i in range(3)]
        cnt = {"i": 0}
        tile_row0 = lambda i: (896 + i * 128) if i < 9 else 2048
        for sl in range(8):
            c0 = sl * 256
            b = sl % 3
            for half in range(2):
                DMA("pool", wbf_d[b][:, half * 8:(half + 1) * 8, :],
                    w_o[half * 1024:(half + 1) * 1024, c0:c0 + 256].rearrange("(k p) c -> p k c", p=128),
                    [], [("wbf_d", b, half)])
            for i in range(10):
                pi = cnt["i"] % 4
                xi = cnt["i"] % 3
                cnt["i"] += 1
                if cnt["i"] == 1:
                    for pf in range(2):
                        r0 = tile_row0(pf)
                        DMA("sp", xres[pf][:], xin[r0:r0 + 128, 0:256], [], [("xres", pf)])
                nxt = cnt["i"] + 1
                if nxt < 80:
                    r0 = tile_row0(nxt % 10)
                    cn = (nxt // 10) * 256
                    DMA("sp", xres[nxt % 3][:], xin[r0:r0 + 128, cn:cn + 256], [], [("xres", nxt % 3)])
                for kc in range(16):
                    OP("pe", "matmul", [("wbf_d", b, 0), ("wbf_d", b, 1), ("OT", i)], [("pmm", pi)], pmm[pi][:, 0:256],
                       lhsT=OT[:, kc, i * 128:(i + 1) * 128], rhs=wbf_d[b][:, kc, :], start=(kc == 0), stop=(kc == 15))
                OP("dve", "tensor_tensor", [("pmm", pi), ("xres", xi)], [("xmo", xi)], out=xmo[xi][:],
                   in0=pmm[pi][:, 0:256], in1=xres[xi][:], op=ALU.add)
                DMA("sp", XM[i * 128:(i + 1) * 128, c0:c0 + 256], xmo[xi][:], [("xmo", xi)], [("XM", i)])
        P.barrier()
        stD.close()
        mark("D_done")
        stE = contextlib.ExitStack()
        def sbE(name, shape, dt=F32):
            return stE.enter_context(nc.sbuf_tensor(name, list(shape), dt))
        aT = sbE("aT", [128, 44, 1032], BF16)
        OP("pool", "memset", [], ["aT_pad"], aT[:, :, 1028:1032], 0.0)
        gcol2 = sbE("gcol2", [128, 16])
        DMA("sp", gcol2[:], g_ffn.rearrange("(k p) -> p k", p=128), [], ["gcol2"], allow_slow_non_contiguous=True)
        cvec = {}
        for nm, apd in (("w0", conv_w[0, :]), ("w1", conv_w[1, :]), ("w2", conv_w[2, :]), ("b", conv_b)):
            tcv = sbE("cv_" + nm, [128, 88])
            DMA("sp", tcv[:], apd.rearrange("(j p) -> p j", p=128), [], ["cv_" + nm], allow_slow_non_contiguous=True)
            cvec[nm] = tcv
        SF = sbE("SF", [128, 88, 8])
        DMA("sp", SF[:], sffnT.rearrange("(j p) r b -> p j (r b)", p=128), [], ["SF"])
        Ulast = sbE("Ulast", [128, 88, 2])
        Usamp = sbE("Usamp", [128, 88, 4])
        stE1 = contextlib.ExitStack()
        def sbE1(name, shape, dt=F32):
            return stE1.enter_context(nc.sbuf_tensor(name, list(shape), dt))
        hT2 = sbE1("hT2", [128, 16, NT2], BF16)
        stE0 = contextlib.ExitStack()
        xt = [stE0.enter_context(nc.sbuf_tensor("xt_e%d" % i, [128, D], F32)) for i in range(2)]
        xn = [stE0.enter_context(nc.sbuf_tensor("xn_e%d" % i, [128, D], BF16)) for i in range(2)]
        for i in range(10):
            norm_transpose(XM[i * 128:(i + 1) * 128, :], gcol2, "gcol2", hT2[:, :, i * 128:(i + 1) * 128], ("hT2", i), i,
                           src_keys=[("XM", i)])
        P.barrier()
        stE0.close()
        hT2_all = [("hT2", i) for i in range(10)]
        wu_bf = [[sbE1("wu_bf%d_%d" % (i, j), [128, 16, 256], BF16) for j in range(2)] for i in range(3)]
        U = [sbE1("U%d" % i, [128, 1154]) for i in range(2)]
        Us = [sbE1("Us%d" % i, [128, 4]) for i in range(2)]
        cg = sbE1("cg", [128, 1024])
        cv = sbE1("cv", [128, 1024])
        cgs = sbE1("cgs", [128, 4])
        cvs = sbE1("cvs", [128, 4])
        ecnt = {"pm": 0}
        for j in range(44):
            jb = (j // 2) % 3
            jo = (j % 2) * 128
            for gv in range(2):
                col0 = gv * DFF + j * 128
                jj = gv * 44 + j
                if j % 2 == 0:
                    for half in range(2):
                        DMA("pool", wu_bf[jb][gv][:, half * 8:(half + 1) * 8, :],
                            w_up[half * 1024:(half + 1) * 1024, col0:col0 + 256].rearrange("(k p) c -> p k c", p=128),
                            [], [("wu_bf", jb, gv, half)])
                for (t0, n) in [(126, 344), (470, 343), (813, 343)]:
                    pi = ecnt["pm"] % 4
                    ecnt["pm"] += 1
                    for kc in range(16):
                        OP("pe", "matmul", [("wu_bf", jb, gv, 0), ("wu_bf", jb, gv, 1)] + hT2_all, [("pmm", pi)],
                           pmm[pi][:, 0:n], lhsT=wu_bf[jb][gv][:, kc, jo:jo + 128], rhs=hT2[:, kc, t0:t0 + n],
                           start=(kc == 0), stop=(kc == 15))
                    if t0 == 813:
                        OP("act", "activation", [("pmm", pi)], [("U", gv)], out=U[gv][:, 813:1152], in_=pmm[pi][:, 0:339],
                           func=AF.Copy)
                        OP("act", "activation", [("pmm", pi)], [("Us", gv)], out=Us[gv][:, :], in_=pmm[pi][:, 339:343],
                           func=AF.Copy)
                    else:
                        OP("act", "activation", [("pmm", pi)], [("U", gv)], out=U[gv][:, t0:t0 + n], in_=pmm[pi][:, 0:n],
                           func=AF.Copy)
                    if t0 == 126:
                        OP("dve", "tensor_scalar", [("U", gv), "flag"], [("U", gv)], U[gv][:, 126:128], U[gv][:, 126:128],
                           flag[:, 0:1], None, ALU.mult)
                dst = cg if gv == 0 else cv
                dk = "cg" if gv == 0 else "cv"
                OP("dve", "tensor_scalar", [("U", gv), "cv_w2", "cv_b"], [dk], dst[:], U[gv][:, 128:1152],
                   cvec["w2"][:, jj:jj + 1], cvec["b"][:, jj:jj + 1], ALU.mult, ALU.add)
                OP("dve", "scalar_tensor_tensor", [("U", gv), "cv_w1", dk], [dk], out=dst[:], in0=U[gv][:, 127:1151],
                   scalar=cvec["w1"][:, jj:jj + 1], in1=dst[:], op0=ALU.mult, op1=ALU.add)
                OP("dve", "scalar_tensor_tensor", [("U", gv), "cv_w0", dk], [dk], out=dst[:], in0=U[gv][:, 126:1150],
                   scalar=cvec["w0"][:, jj:jj + 1], in1=dst[:], op0=ALU.mult, op1=ALU.add)
                OP("pool", "tensor_copy", [("U", gv)], ["Ulast"], out=Ulast[:, jj, :], in_=U[gv][:, 1150:1152])
                dsts = cgs if gv == 0 else cvs
                dks = "cgs" if gv == 0 else "cvs"
                OP("dve", "tensor_scalar", [("Us", gv), "cv_w2", "cv_b"], [dks], dsts[:], Us[gv][:, :],
                   cvec["w2"][:, jj:jj + 1], cvec["b"][:, jj:jj + 1], ALU.mult, ALU.add)
                OP("dve", "scalar_tensor_tensor", ["SF", "cv_w1", dks], [dks], out=dsts[:], in0=SF[:, jj, 4:8],
                   scalar=cvec["w1"][:, jj:jj + 1], in1=dsts[:], op0=ALU.mult, op1=ALU.add)
                OP("dve", "scalar_tensor_tensor", ["SF", "cv_w0", dks], [dks], out=dsts[:], in0=SF[:, jj, 0:4],
                   scalar=cvec["w0"][:, jj:jj + 1], in1=dsts[:], op0=ALU.mult, op1=ALU.add)
                OP("pool", "tensor_copy", [("Us", gv)], ["Usamp"], out=Usamp[:, jj, :], in_=Us[gv][:, :])
            OP("act", "activation", ["cg"], ["cg"], out=cg[:], in_=cg[:], func=AF.Silu)
            OP("dve", "tensor_tensor", ["cg", "cv"], [("aT", j)], out=aT[:, j, 0:1024], in0=cg[:], in1=cv[:], op=ALU.mult)
            OP("act", "activation", ["cgs"], ["cgs"], out=cgs[:], in_=cgs[:], func=AF.Silu)
            OP("dve", "tensor_tensor", ["cgs", "cvs"], [("aT", j)], out=aT[:, j, 1024:1028], in0=cgs[:], in1=cvs[:],
               op=ALU.mult)
        for r in range(2):
            DMA("sp", pffn_out[r, :].rearrange("(j p) -> p j", p=128), Ulast[:, :, r], ["Ulast"], [("pffn", r)],
                allow_slow_non_contiguous=True)
        DMA("sp", sffn_outT.rearrange("(j p) b -> p j b", p=128), Usamp[:], ["Usamp"], ["sffn"])
        DMA("pool", sffn_row0[:, :], sffn_in1[:, :], [], ["sffn0"])
        P.barrier()
        stE1.close()
        mark("E_up_done")
        wd_bf = [sbE("wd_bf%d" % i, [128, 44, 512], BF16) for i in range(2)]
        xres2 = [sbE("xres2_%d" % i, [128, 512]) for i in range(3)]
        xo = [sbE("xo%d" % i, [128, 512]) for i in range(3)]
        aT_all = [("aT", j) for j in range(44)] + ["aT_pad"]
        cnt = {"i": 0}
        for sl in range(4):
            c0 = sl * 512
            b = sl % 2
            for g in range(4):
                DMA("pool", wd_bf[b][:, g * 11:(g + 1) * 11, :],
                    w_down[g * 1408:(g + 1) * 1408, c0:c0 + 512].rearrange("(k p) c -> p k c", p=128),
                    [], [("wd_bf", b, g)])
            wk = [("wd_bf", b, g) for g in range(4)]
            for i in range(9):
                pi = cnt["i"] % 4
                xi = cnt["i"] % 3
                cnt["i"] += 1
                if cnt["i"] == 1:
                    for pf in range(2):
                        DMA("sp", xres2[pf][:], XM[(pf + 1) * 128:(pf + 2) * 128, 0:512], [("XM", pf + 1)], [("xres2", pf)])
                nxt = cnt["i"] + 1
                if nxt < 36:
                    ti = nxt % 9
                    cn = (nxt // 9) * 512
                    DMA("sp", xres2[nxt % 3][:], XM[(ti + 1) * 128:(ti + 2) * 128, cn:cn + 512], [("XM", ti + 1)],
                        [("xres2", nxt % 3)])
                mrows = 128 if i < 8 else 8
                for kc in range(44):
                    OP("pe", "matmul", wk + aT_all, [("pmm", pi)], pmm[pi][0:mrows, 0:512],
                       lhsT=aT[:, kc, i * 128:i * 128 + mrows], rhs=wd_bf[b][:, kc, :], start=(kc == 0), stop=(kc == 43))
                OP("dve", "tensor_tensor", [("pmm", pi), ("xres2", xi)], [("xo", xi)], out=xo[xi][:],
                   in0=pmm[pi][:, 0:512], in1=xres2[xi][:], op=ALU.add)
                DMA("sp", XO[i * 128:(i + 1) * 128, c0:c0 + 512], xo[xi][:], [("xo", xi)], [("XO", i)])
        P.barrier()
        stE.close()
        mark("E_down_done")
        stF = contextlib.ExitStack()
        def sbF(name, shape, dt=F32):
            return stF.enter_context(nc.sbuf_tensor(name, list(shape), dt))
        gfin = sbF("gfin", [128, D])
        DMA("pool", gfin[:], g_fin[0:1, :].partition_broadcast(128), [], ["gfin"])
        xf = [sbF("xf%d" % i, [128, D]) for i in range(2)]
        yf = [sbF("yf%d" % i, [128, D]) for i in range(2)]
        for i in range(9):
            b = i % 2
            DMA("sp", xf[b][:], XO[i * 128:(i + 1) * 128, :], [("XO", i)], [("xf", b)])
            OP("act", "activation", [("xf", b)], [("yf", b), "ss%d" % b], out=yf[b][:], in_=xf[b][:], func=AF.Square,
               accum_out=ss[b][:])
            OP("act", "activation", ["ss%d" % b, "epst"], ["rstd%d" % b], out=rstd[b][:], in_=ss[b][:], func=AF.Ln,
               scale=1.0 / D, bias=epst[:, 0:1])
            OP("act", "activation", ["rstd%d" % b], ["rstd%d" % b], out=rstd[b][:], in_=rstd[b][:], func=AF.Exp,
               scale=-0.5)
            OP("dve", "scalar_tensor_tensor", [("xf", b), "rstd%d" % b, "gfin"], [("yf", b)], out=yf[b][:], in0=xf[b][:],
               scalar=rstd[b][:, 0:1], in1=gfin[:], op0=ALU.mult, op1=ALU.mult)
            if i < 8:
                DMA("pool", y_out[i * 128:(i + 1) * 128, :], yf[b][:], [("yf", b)], [("y", i)])
            else:
                DMA("pool", ys_out[0:4, :], yf[b][0:4, :], [("yf", b)], [("y", i)])
        P.barrier()
        stF.close()
        mark("end")
        P.emit()
    return nc


_NC_CACHE = {}


def _get_nc():
    if "nc" not in _NC_CACHE:
        _NC_CACHE["nc"] = build()
    return _NC_CACHE["nc"]


def _count_mask():
    m = np.zeros((128, 9, 512), np.float32)
    s_idx = np.arange(128)[:, None]
    t_idx = np.arange(512)[None, :]
    for i in range(9):
        d0 = -384 + 128 * i if i < 8 else 1024
        d = d0 + t_idx - s_idx
        cnt = ((d >= 0) & (d <= 128)).astype(np.float32)
        cnt += ((d >= 0) & (d <= 512) & (d % 4 == 0))
        cnt += ((d >= 0) & (d <= 2048) & (d % 16 == 0))
        m[:, i, :] = cnt
    return m.astype(ml_dtypes.bfloat16)


def _tri_masks():
    m = np.zeros((64, 320), np.float32)
    a = np.arange(64)
    m[:, 0:64] = (a[:, None] < a[None, :])
    m[:, 64:128] = (a[:, None] <= a[None, :])
    m[:, 128:192] = 1.0
    m[:, 192:256] = (a[None, :] < a[:, None])
    m[:, 256:320] = 1.0
    return m.astype(ml_dtypes.bfloat16)


def kernel(**inp):
    f32 = np.float32
    x_prompt = np.asarray(inp["x_prompt"], f32)
    x_sample = np.asarray(inp["x_sample"], f32)
    ident = np.eye(128).astype(ml_dtypes.bfloat16)
    masks = _count_mask()
    A = lambda k: np.ascontiguousarray(inp[k][0], dtype=f32)
    shared = {
        "ident": ident, "masks": masks,
        "w_in": A("w_in"), "norm_mix_g": A("norm_mix_g"), "att_out_g": A("att_out_g").reshape(1, 1024),
        "rw_mu": A("rw_mu"), "trim": _tri_masks(),
        "rw_w0": A("rw_w0"), "rw_a0": A("rw_a0"), "rw_k_k": A("rw_k_k"), "rw_k_a": A("rw_k_a"), "rw_r_k": A("rw_r_k"),
        "rw_w_up": A("rw_w_up"), "rw_a_up": A("rw_a_up"), "rw_g_up": A("rw_g_up"),
        "rw_lnx_w": A("rw_lnx_w").reshape(1, 1024), "rw_lnx_b": A("rw_lnx_b").reshape(1, 1024),
        "w_o": A("w_o"), "norm_ffn_g": A("norm_ffn_g"), "ffn_w_up": A("ffn_w_up"), "ffn_conv_w": A("ffn_conv_w"),
        "ffn_conv_b": A("ffn_conv_b"), "ffn_w_down": A("ffn_w_down"),
        "norm_final_g": np.ascontiguousarray(inp["norm_final_g"], dtype=f32).reshape(1, D),
        "rw_lnx_w_c": A("rw_lnx_w"), "rw_lnx_b_c": A("rw_lnx_b"),
    }
    in_maps = []
    for core in range(8):
        b, half = core // 2, core % 2
        xin = np.zeros((NTOK, D), f32)
        if half == 1:
            xin[0:1024] = x_prompt[b, 0:1024]
        xin[1024:2048] = x_prompt[b, half * 1024:(half + 1) * 1024]
        xin[2048:2052] = x_sample[4 * core:4 * core + 4, 0]
        m = dict(shared)
        m["xin"] = xin
        m["flag"] = np.full((128, 1), float(half), f32)
        sf = np.asarray(inp["state_ffn_conv"][0, 4 * core:4 * core + 4], f32)
        m["sffnT"] = np.ascontiguousarray(sf.transpose(2, 1, 0))
        m["cache_k"] = np.ascontiguousarray(inp["cache_att_k"][0, 4 * core:4 * core + 4], dtype=f32).reshape(4, 2048, 1024)
        m["cache_v"] = np.ascontiguousarray(inp["cache_att_v"][0, 4 * core:4 * core + 4], dtype=f32).reshape(4, 2048, 1024)
        m["swkv_in"] = np.ascontiguousarray(inp["state_rwkv_wkv"][0, 4 * core:4 * core + 4], dtype=f32)
        m["sffn_in1"] = np.ascontiguousarray(sf[:, 1, :])
        m["sshiftT"] = np.ascontiguousarray(inp["state_rwkv_shift"][0, 4 * core:4 * core + 4, 0, :].T, dtype=f32)
        in_maps.append(m)
    nc = _get_nc()
    res = run_bass_kernel_spmd(nc, in_maps, core_ids=list(range(8)))
    R = res.results
    _NC_CACHE["R"] = R
    pk = np.zeros((1, 4, 2048, 16, 64), f32)
    pv = np.zeros((1, 4, 2048, 16, 64), f32)
    sk = np.zeros((1, 32, 1, 16, 64), f32)
    sv = np.zeros((1, 32, 1, 16, 64), f32)
    pshift = np.zeros((1, 4, 1, C_SH), f32)
    pwkv = np.zeros((1, 4, 16, 64, 64), f32)
    yp = np.zeros((4, 2048, D), f32)
    ysm = np.zeros((32, 1, D), f32)
    pffn = np.zeros((1, 4, 2, 2 * DFF), f32)
    sffn = np.zeros((1, 32, 2, 2 * DFF), f32)
    swkv = np.zeros((1, 32, 16, 64, 64), f32)
    sshift = np.zeros((1, 32, 1, C_SH), f32)
    for core in range(8):
        b, half = core // 2, core % 2
        pk[0, b, half * 1024:(half + 1) * 1024] = R[core]["k_out"].reshape(1024, 16, 64)
        pv[0, b, half * 1024:(half + 1) * 1024] = R[core]["v_out"].reshape(1024, 16, 64)
        sk[0, 4 * core:4 * core + 4, 0] = R[core]["sk_out"].reshape(4, 16, 64)
        sv[0, 4 * core:4 * core + 4, 0] = R[core]["sv_out"].reshape(4, 16, 64)
        sshift[0, 4 * core:4 * core + 4, 0] = R[core]["sshift_outT"].T
        yp[b, half * 1024:(half + 1) * 1024] = R[core]["y_out"]
        ysm[4 * core:4 * core + 4, 0] = R[core]["ys_out"]
        swkv[0, 4 * core:4 * core + 4] = R[core]["swkv_out"]
        sffn[0, 4 * core:4 * core + 4, 0] = R[core]["sffn_row0"]
        sffn[0, 4 * core:4 * core + 4, 1] = R[core]["sffn_outT"].T
        if half == 1:
            pshift[0, b, 0] = R[core]["pshift_out"][:, 0]
            pwkv[0, b] = R[core]["pwkv_out"]
            pffn[0, b] = R[core]["pffn_out"]
    z = lambda *s: np.zeros(s, f32)
    return (yp, ysm, pk, pv, pshift, pwkv, pffn, sk, sv, sshift, swkv, sffn)
```

```python
import contextlib
import os
import numpy as np
import ml_dtypes
import concourse.bass as bass
import concourse.mybir as mybir
from concourse.bass_utils import run_bass_kernel_spmd

F32 = mybir.dt.float32
BF16 = mybir.dt.bfloat16
I32 = mybir.dt.int32
ALU = mybir.AluOpType
AF = mybir.ActivationFunctionType
AX = mybir.AxisListType

ENGS = ["pe", "act", "dve", "pool", "sp"]
EPOCH = 30000
N_DMA_SEMS = 32


class Prog:
    def __init__(self, nc):
        self.nc = nc
        self.streams = {e: [] for e in ENGS}
        self.cnt = {e: 0 for e in ENGS}
        self.seen = {e: {} for e in ENGS}
        self.lastw = {}
        self.readers = {}
        self.dma_sems = ["dma%d" % i for i in range(N_DMA_SEMS)]
        self.sdma_sems = ["sdma%d" % i for i in range(16)]
        self.dma_cnt = {s: 0 for s in self.dma_sems + self.sdma_sems}
        self.dma_rr = 0
        self.sdma_rr = 0
        self.semnames = set()

    def _need(self, reads, writes):
        need = {}
        for k in reads:
            lw = self.lastw.get(k)
            if lw is not None:
                need[lw[0]] = max(need.get(lw[0], 0), lw[1])
        for k in writes:
            lw = self.lastw.get(k)
            if lw is not None:
                need[lw[0]] = max(need.get(lw[0], 0), lw[1])
            for s, v in self.readers.get(k, {}).items():
                need[s] = max(need.get(s, 0), v)
        return need

    def _waits(self, eng, need):
        waits = []
        for s, v in need.items():
            if eng == "pe" and s.startswith("pe"):
                continue
            if self.seen[eng].get(s, 0) >= v:
                continue
            self.seen[eng][s] = v
            waits.append((s, v))
        return waits

    def _mark(self, reads, writes, sem, val):
        for k in reads:
            d = self.readers.setdefault(k, {})
            d[sem] = max(d.get(sem, 0), val)
        for k in writes:
            self.lastw[k] = (sem, val)
            self.readers[k] = {}

    def op(self, eng, reads, writes, meth, *a, **kw):
        fn = (meth, a, kw)
        waits = self._waits(eng, self._need(reads, writes))
        c = self.cnt[eng]
        sem = "%s_e%d" % (eng, c // EPOCH)
        val = c % EPOCH + 1
        self.cnt[eng] = c + 1
        self.semnames.add(sem)
        self.streams[eng].append((fn, waits, (sem, 1)))
        self._mark(reads, writes, sem, val)

    def dma(self, q, reads, writes, **kw):
        fn = ("dma_start", (), kw)
        if q == "pool":
            s = self.sdma_sems[self.sdma_rr % 16]
            self.sdma_rr += 1
        else:
            s = self.dma_sems[self.dma_rr % N_DMA_SEMS]
            self.dma_rr += 1
        need = self._need(reads, writes)
        prev = self.dma_cnt[s]
        if prev > 0:
            need[s] = max(need.get(s, 0), prev)
        waits = self._waits(q, need)
        self.dma_cnt[s] = prev + 16
        self.semnames.add(s)
        self.streams[q].append((fn, waits, (s, 16)))
        self._mark(reads, writes, s, prev + 16)

    def barrier(self):
        need = {}
        for sname in self.dma_sems + self.sdma_sems:
            if self.dma_cnt[sname] > 0:
                need[sname] = self.dma_cnt[sname]
        for e in ENGS:
            c = self.cnt[e]
            if c > 0:
                need["%s_e%d" % (e, (c - 1) // EPOCH)] = (c - 1) % EPOCH + 1
        for e in ENGS:
            waits = []
            for sn, v in need.items():
                if self.seen[e].get(sn, 0) >= v:
                    continue
                self.seen[e][sn] = v
                waits.append((sn, v))
            if waits:
                self.streams[e].append((None, waits, None))

    def emit(self):
        nc = self.nc
        names = sorted(self.semnames)
        final = []
        for s in names:
            if s.startswith("dma") or s.startswith("sdma"):
                final.append((s, self.dma_cnt[s]))
        for e in ENGS:
            c = self.cnt[e]
            if c > 0:
                final.append(("%s_e%d" % (e, (c - 1) // EPOCH), (c - 1) % EPOCH + 1))
        with contextlib.ExitStack() as st:
            sems = {n: st.enter_context(nc.semaphore(n)) for n in names}
            block = st.enter_context(nc.Block())
            streams = self.streams

            def run(engobj, lst, fin=None):
                for fn, waits, inc in lst:
                    for s, v in waits:
                        engobj.wait_ge(sems[s], v)
                    if fn is None:
                        continue
                    ins = getattr(engobj, fn[0])(*fn[1], **fn[2])
                    ins.then_inc(sems[inc[0]], inc[1])
                if fin:
                    for s, v in fin:
                        engobj.wait_ge(sems[s], v)

            @block.tensor
            def _(e):
                run(e, streams["pe"])

            @block.scalar
            def _(e):
                run(e, streams["act"])

            @block.vector
            def _(e):
                run(e, streams["dve"])

            @block.gpsimd
            def _(e):
                run(e, streams["pool"])

            @block.sync
            def _(e):
                run(e, streams["sp"], final)


D = 2048
NSLOT = 2048
NTOK = 2176
NTILE = 17
C_IN = 6432
C_SH = 3360
DFF = 5632
EPS = 1.0e-6
STAGE = 99
HAVE_SAMPLE = True


class Ctx:
    pass


def build(stage=STAGE):
    nc = bass.Bass("TRN2", target_bir_lowering=False)
    P = Prog(nc)

    def OP(eng, meth, reads, writes, *a, **kw):
        if eng in ("act", "dve"):
            extra = [k for k in reads if isinstance(k, tuple) and k[0] == "pmm"]
            if extra:
                writes = list(writes) + extra
        P.op(eng, reads, writes, meth, *a, **kw)

    def DMA(q, out, in_, reads, writes, **kw):
        P.dma(q, reads, writes, out=out, in_=in_, **kw)

    def din(name, shape, dt=F32):
        return nc.dram_tensor(name, list(shape), dt, kind="ExternalInput").ap()

    def dout(name, shape, dt=F32):
        return nc.dram_tensor(name, list(shape), dt, kind="ExternalOutput").ap()

    def dscr(name, shape, dt=F32):
        return nc.dram_tensor(name, list(shape), dt).ap()

    xin = din("xin", [NTOK, D])
    ident_d = din("ident", [128, 128], BF16)
    masks_d = din("masks", [128, 9, 512], BF16)
    flag_d = din("flag", [128, 1])
    w_in = din("w_in", [D, C_IN])
    g_mix = din("norm_mix_g", [D])
    g_att = din("att_out_g", [1, 1024])
    rw_mu = din("rw_mu", [C_SH])
    sshiftT = din("sshiftT", [C_SH, 4])
    trim_d = din("trim", [64, 320], BF16)
    rw_w0 = din("rw_w0", [1024])
    rw_a0 = din("rw_a0", [1024])
    rw_k_k = din("rw_k_k", [1024])
    rw_k_a = din("rw_k_a", [1024])
    rw_r_k = din("rw_r_k", [1024])
    rw_w_up = din("rw_w_up", [64, 1024])
    rw_a_up = din("rw_a_up", [64, 1024])
    rw_g_up = din("rw_g_up", [160, 1024])
    rw_lnx_w = din("rw_lnx_w", [1, 1024])
    rw_lnx_b = din("rw_lnx_b", [1, 1024])
    w_o = din("w_o", [D, D])
    g_ffn = din("norm_ffn_g", [D])
    w_up = din("ffn_w_up", [D, 2 * DFF])
    conv_w = din("ffn_conv_w", [3, 2 * DFF])
    conv_b = din("ffn_conv_b", [2 * DFF])
    w_down = din("ffn_w_down", [DFF, D])
    g_fin = din("norm_final_g", [1, D])
    sffnT = din("sffnT", [2 * DFF, 2, 4])
    sffn_in1 = din("sffn_in1", [4, 2 * DFF])
    cache_k = din("cache_k", [4, 2048, 1024])
    cache_v = din("cache_v", [4, 2048, 1024])
    swkv_in = din("swkv_in", [4, 16, 64, 64])
    rw_lnx_w_c = din("rw_lnx_w_c", [1024])
    rw_lnx_b_c = din("rw_lnx_b_c", [1024])
    k_out = dout("k_out", [1024, 1024])
    v_out = dout("v_out", [1024, 1024])
    sk_out = dout("sk_out", [4, 1024])
    sv_out = dout("sv_out", [4, 1024])
    pshift_out = dout("pshift_out", [C_SH, 1])
    sshift_outT = dout("sshift_outT", [C_SH, 4])
    pwkv_out = dout("pwkv_out", [16, 64, 64])
    y_out = dout("y_out", [1024, D])
    ys_out = dout("ys_out", [4, D])
    pffn_out = dout("pffn_out", [2, 2 * DFF])
    sffn_outT = dout("sffn_outT", [2 * DFF, 4])
    sffn_row0 = dout("sffn_row0", [4, 2 * DFF])
    swkv_out = dout("swkv_out", [4, 16, 64, 64])
    QT = dscr("QT", [1024, NTOK], BF16)
    KT = dscr("KT", [1024, NTOK], BF16)
    XS = dscr("XS", [C_SH, NTOK], F32)
    O_scr = dscr("O_scr", [1280, 2048], BF16)
    QS = dscr("QS", [4, 1024], F32)
    KS = dscr("KS", [4, 1024], F32)
    VS = dscr("VS", [4, 1024], F32)
    XM = dscr("XM", [1280, D], F32)
    XO = dscr("XO", [1152, D], F32)

    with contextlib.ExitStack() as st:
        def sb(name, shape, dt=F32):
            return st.enter_context(nc.sbuf_tensor(name, list(shape), dt))

        def ps(name, shape, dt=F32):
            return st.enter_context(nc.psum_tensor(name, list(shape), dt))

        ident = sb("ident_sb", [128, 128], BF16)
        DMA("sp", ident[:], ident_d[:, :], [], ["ident"])
        epst = sb("epst", [128, 1])
        OP("pool", "memset", [], ["epst"], epst[:], EPS)
        epsl = sb("epsl", [128, 1])
        OP("pool", "memset", [], ["epsl"], epsl[:], 64 * 1e-5)
        flag = sb("flag_sb", [128, 1])
        DMA("sp", flag[:], flag_d[:, :], [], ["flag"])
        gcol = sb("gcol", [128, 16])
        DMA("sp", gcol[:], g_mix.rearrange("(k p) -> p k", p=128), [], ["gcol"], allow_slow_non_contiguous=True)
        mucol = sb("mucol", [128, 27])
        DMA("sp", mucol[:, 0:26], rw_mu[0:3328].rearrange("(k p) -> p k", p=128), [], ["mucol"],
            allow_slow_non_contiguous=True)
        DMA("sp", mucol[0:32, 26:27], rw_mu[3328:3360].rearrange("(p o) -> p o", o=1), ["mucol"], ["mucol"],
            allow_slow_non_contiguous=True)

        ss = [sb("ss%d" % i, [128, 1]) for i in range(2)]
        rstd = [sb("rstd%d" % i, [128, 1]) for i in range(2)]
        pmm = [ps("pmm%d" % i, [128, 512]) for i in range(8)]
        pT = [pmm[6 + i][:, :].bitcast(BF16).rearrange("p (a b) -> p a b", b=128)[:, 0:4, :] for i in range(2)]
        stAB = contextlib.ExitStack()
        def sbAB(name, shape, dt=F32):
            return stAB.enter_context(nc.sbuf_tensor(name, list(shape), dt))
        VA = sbAB("VA", [128, 17, 16, 65], BF16)
        OP("pool", "memset", [], ["VAones"], VA[:, :, :, 64:65], 1.0)
        OP("pool", "tensor_scalar", ["flag", "VAones"], ["VAones"], VA[:, 0:8, :, 64:65], VA[:, 0:8, :, 64:65],
           flag[:, 0:1], None, ALU.mult)

        MARK = {}
        def mark(name):
            MARK[name] = dict(P.cnt)
        _NC_CACHE["MARK"] = MARK
        stA = contextlib.ExitStack()
        def sbA(name, shape, dt=F32):
            return stA.enter_context(nc.sbuf_tensor(name, list(shape), dt))
        hT = sbA("hT", [128, 16, NTOK], BF16)
        xt = [sbA("xt%d" % i, [128, D]) for i in range(2)]
        xn = [sbA("xn%d" % i, [128, D], BF16) for i in range(2)]

        def norm_transpose(src_ap, gc, gkey, dst, dst_key, it, src_keys=()):
            b = it % 2
            DMA("sp", xt[b][:], src_ap, list(src_keys), ["xt%d" % b])
            OP("act", "activation", ["xt%d" % b], ["xn%d" % b, "ss%d" % b], out=xn[b][:], in_=xt[b][:], func=AF.Square,
               accum_out=ss[b][:])
            OP("act", "activation", ["ss%d" % b, "epst"], ["rstd%d" % b], out=rstd[b][:], in_=ss[b][:], func=AF.Ln,
               scale=1.0 / D, bias=epst[:, 0:1])
            OP("act", "activation", ["rstd%d" % b], ["rstd%d" % b], out=rstd[b][:], in_=rstd[b][:], func=AF.Exp,
               scale=-0.5)
            OP("dve", "tensor_scalar", ["xt%d" % b, "rstd%d" % b], ["xn%d" % b], xn[b][:], xt[b][:],
               rstd[b][:, 0:1], None, ALU.mult)
            for g4 in range(4):
                pb = g4 % 2
                for j in range(4):
                    kc = g4 * 4 + j
                    OP("pe", "transpose", ["xn%d" % b, "ident"], ["pT%d" % pb], out=pT[pb][:, j, :],
                       in_=xn[b][:, kc * 128:(kc + 1) * 128], identity=ident[:])
                OP("dve", "tensor_tensor", ["pT%d" % pb, gkey], [dst_key], out=dst[:, g4 * 4:(g4 + 1) * 4, :],
                   in0=pT[pb], in1=gc[:, g4 * 4:(g4 + 1) * 4].unsqueeze(2).broadcast_to([128, 4, 128]),
                   op=ALU.mult)

        for t in range(NTILE):
            norm_transpose(xin[t * 128:(t + 1) * 128, :], gcol, "gcol", hT[:, :, t * 128:(t + 1) * 128], ("hT", t), t)

        mark("A_norm_done")
        SL = 256
        wbf = [sbA("wbf%d" % i, [128, 16, SL], BF16) for i in range(3)]
        ev32 = [sbA("ev32_%d" % i, [128, 512]) for i in range(4)]
        ev16 = [sbA("ev16_%d" % i, [128, 512], BF16) for i in range(4)]
        cb = [sbA("cb%d" % i, [128, 513]) for i in range(2)]
        dsh = sbA("dsh", [128, 512])
        xsb = [sbA("xsb%d" % i, [128, 512]) for i in range(2)]
        sprev = sbA("sprev", [128, 27, 4])
        DMA("sp", sprev[:, 0:26, :], sshiftT[0:3328, :].rearrange("(k p) f -> p k f", p=128), [], ["sprev"])
        DMA("sp", sprev[0:32, 26, :], sshiftT[3328:3360, :], ["sprev"], ["sprev"])
        state = {"pm": 0, "evq": 0, "rwb": 0}
        hT_all = [("hT", t) for t in range(NTILE)]

        def evac(dst_sb, src_ps, rkeys, wkeys):
            state["evq"] += 1
            if state["evq"] % 2 == 0:
                OP("act", "activation", rkeys, wkeys, out=dst_sb, in_=src_ps, func=AF.Copy)
            else:
                OP("dve", "tensor_copy", rkeys, wkeys, out=dst_sb, in_=src_ps)

        nslab = (C_IN + SL - 1) // SL
        for s in range(nslab):
            c0 = s * SL
            cw = min(SL, C_IN - c0)
            b = s % 3
            for half in range(2):
                DMA("pool", wbf[b][:, half * 8:(half + 1) * 8, 0:cw],
                    w_in[half * 1024:(half + 1) * 1024, c0:c0 + cw].rearrange("(k p) c -> p k c", p=128),
                    [], [("wbf", b, half)])
            wkeys = [("wbf", b, 0), ("wbf", b, 1)]
            kind = "q" if c0 < 1024 else "k" if c0 < 2048 else "v" if c0 < 3072 else "rw"
            if kind in ("q", "k", "rw"):
                for m in range((cw + 127) // 128):
                    mw = min(128, cw - m * 128)
                    cc = c0 + m * 128
                    if kind == "q":
                        blocks = [(896, 128), (1024, 512), (1536, 512), (2048, 128)]
                    else:
                        blocks = [(0, 512), (512, 512), (1024, 512), (1536, 512), (2048, 128)]
                    for (t0, n) in blocks:
                        pi = state["pm"] % 4
                        state["pm"] += 1
                        for kc in range(16):
                            OP("pe", "matmul", wkeys + hT_all, [("pmm", pi)], pmm[pi][0:mw, 0:n],
                               lhsT=wbf[b][:, kc, m * 128:m * 128 + mw], rhs=hT[:, kc, t0:t0 + n],
                               start=(kc == 0), stop=(kc == 15))
                        if kind in ("q", "k"):
                            dstT = QT if kind == "q" else KT
                            r0 = cc - (0 if kind == "q" else 1024)
                            evac(ev16[pi][0:mw, 0:n], pmm[pi][0:mw, 0:n], [("pmm", pi)], [("ev16", pi)])
                            DMA("sp", dstT[r0:r0 + mw, t0:t0 + n], ev16[pi][0:mw, 0:n], [("ev16", pi)],
                                [("scr", kind, r0 // 64, t0), ("scr", kind, r0 // 64 + 1, t0)])
                        else:
                            ch0 = cc - 3072
                            ci = ch0 // 128
                            rb = state["rwb"] % 2
                            state["rwb"] += 1
                            nb = cb[rb]
                            if t0 == 0:
                                OP("dve", "memset", [], [("cb", rb)], nb[0:mw, 0:1], 0.0)
                            OP("act", "activation", [("pmm", pi)], [("cb", rb)], out=nb[0:mw, 1:1 + n],
                               in_=pmm[pi][0:mw, 0:n], func=AF.Copy)
                            if t0 < 1536:
                                OP("dve", "tensor_copy", [("cb", rb)], [("cb", 1 - rb)], out=cb[1 - rb][0:mw, 0:1],
                                   in_=nb[0:mw, n:n + 1])
                            ne = n if t0 < 2048 else 4
                            prev_ap = nb[0:mw, 0:ne] if t0 < 2048 else sprev[0:mw, ci, :]
                            OP("dve", "tensor_tensor", [("cb", rb), "sprev"], ["dsh"], out=dsh[0:mw, 0:ne],
                               in0=prev_ap, in1=nb[0:mw, 1:1 + ne], op=ALU.subtract)
                            xb = state["rwb"] % 2
                            OP("dve", "scalar_tensor_tensor", ["dsh", ("cb", rb), "mucol"], [("xsb", xb)],
                               out=xsb[xb][0:mw, 0:ne], in0=dsh[0:mw, 0:ne], scalar=mucol[0:mw, ci:ci + 1],
                               in1=nb[0:mw, 1:1 + ne], op0=ALU.mult, op1=ALU.add)
                            DMA("sp", XS[ch0:ch0 + mw, t0:t0 + ne], xsb[xb][0:mw, 0:ne], [("xsb", xb)],
                                [("XS", ci, t0)])
                            if t0 == 1536:
                                DMA("sp", pshift_out[ch0:ch0 + mw, :], nb[0:mw, 512:513], [("cb", rb)],
                                    [("pshift", ci)])
                            if t0 == 2048:
                                DMA("sp", sshift_outT[ch0:ch0 + mw, :], nb[0:mw, 1:5], [("cb", rb)],
                                    [("sshift", ci)])
            if kind in ("q", "k", "v"):
                tiles = [16] if kind == "q" else list(range(8, 17)) if kind == "k" else list(range(17))
                for t in tiles:
                    pi = state["pm"] % 4
                    state["pm"] += 1
                    for kc in range(16):
                        OP("pe", "matmul", wkeys + [("hT", t)], [("pmm", pi)], pmm[pi][:, 0:cw],
                           lhsT=hT[:, kc, t * 128:(t + 1) * 128], rhs=wbf[b][:, kc, 0:cw],
                           start=(kc == 0), stop=(kc == 15))
                    evac(ev32[pi][:, 0:cw], pmm[pi][:, 0:cw], [("pmm", pi)], [("ev32", pi)])
                    col0 = c0 - (0 if kind == "q" else 1024 if kind == "k" else 2048)
                    if kind == "q":
                        DMA("sp", QS[0:4, col0:col0 + cw], ev32[pi][0:4, 0:cw], [("ev32", pi)], [("QS", col0)])
                        continue
                    if t == 16:
                        scr = KS if kind == "k" else VS
                        DMA("sp", scr[0:4, col0:col0 + cw], ev32[pi][0:4, 0:cw], [("ev32", pi)],
                            [("KVS", kind, col0)])
                    if 8 <= t < 16:
                        o = k_out if kind == "k" else v_out
                        DMA("sp", o[(t - 8) * 128:(t - 7) * 128, col0:col0 + cw], ev32[pi][:, 0:cw],
                            [("ev32", pi)], [("kvout", kind, t, col0)])
                    if t == 16:
                        o = sk_out if kind == "k" else sv_out
                        DMA("sp", o[0:4, col0:col0 + cw], ev32[pi][0:4, 0:cw], [("ev32", pi)],
                            [("skvout", kind, col0)])
                    if kind == "v":
                        h0 = col0 // 64
                        src = ev32[pi][:, 0:cw].rearrange("p (h e) -> p h e", e=64)
                        if t < 8:
                            OP("dve", "tensor_scalar", [("ev32", pi), "flag"], [("VA", t)],
                               VA[:, t, h0:h0 + 4, 0:64], src, flag[:, 0:1], None, ALU.mult)
                        else:
                            OP("dve", "tensor_copy", [("ev32", pi)], [("VA", t)], out=VA[:, t, h0:h0 + 4, 0:64],
                               in_=src)
        P.barrier()
        stA.close()
        mark("A_done")
        stB = contextlib.ExitStack()
        def sbB(name, shape, dt=F32):
            return stB.enter_context(nc.sbuf_tensor(name, list(shape), dt))
        Oatt = sbB("Oatt", [128, 9, 1024], BF16)
        msk = sbB("msk", [128, 9, 512], BF16)
        gatt = sbB("gatt", [128, 1024])
        DMA("sp", msk[:], masks_d[:, :, :], [], ["msk"])
        DMA("pool", gatt[:], g_att[0:1, :].partition_broadcast(128), [], ["gatt"])
        qh = [sbB("qh%d" % i, [64, 1152], BF16) for i in range(2)]
        kh = [sbB("kh%d" % i, [64, 2048], BF16) for i in range(2)]
        pTs = [sbB("pTs%d" % i, [128, 512], BF16) for i in range(4)]
        SBK = [0, 1, 6, 7]
        junk64 = sbB("junk64", [128, 64])
        fin = [[sbB("fin%d_%d" % (i, j), [128, 1]) for j in range(5)] for i in range(2)]
        stt = {"s": 0, "mq": 0, "f": 0}
        scr_q = lambda hh: [("scr", "q", hh, t0) for t0 in (896, 1024, 1536, 2048)]
        scr_k = lambda hh: [("scr", "k", hh, t0) for t0 in (0, 512, 1024, 1536, 2048)]
        items = []
        for h in range(16):
            b = h % 2
            first_of_head = [True]
            for (q0, n) in [(896, 128), (1024, 512), (1536, 512)]:
                nsub = n // 128
                qt0 = q0 // 128
                nkc = qt0 + nsub
                for kc in range(0, nkc):
                    def s1(h=h, b=b, q0=q0, n=n, kc=kc, qt0=qt0, ld=first_of_head[0]):
                        if ld:
                            DMA("sp", qh[b][:, :], QT[h * 64:(h + 1) * 64, 896:2048], scr_q(h), [("qh", b)])
                            DMA("sp", kh[b][:, :], KT[h * 64:(h + 1) * 64, 0:2048], scr_k(h), [("kh", b)])
                        s0 = kc * 128
                        d0 = q0 - s0
                        qi_min = max(0, kc - qt0)
                        c_lo = qi_min * 128
                        mi = (d0 + 384) // 128 if d0 <= 512 else 8
                        sj = stt["s"] % 4
                        si = SBK[sj]
                        stt["s"] += 1
                        OP("pe", "matmul", [("kh", b), ("qh", b)], [("pmm", si)], pmm[si][:, c_lo:n],
                           lhsT=kh[b][:, s0:s0 + 128], rhs=qh[b][:, q0 - 896 + c_lo:q0 - 896 + n], start=True, stop=True)
                        OP("act", "activation", [("pmm", si)], [("pTs", sj)], out=pTs[sj][:, c_lo:n],
                           in_=pmm[si][:, c_lo:n], func=AF.Exp, scale=0.125)
                        OP("dve", "tensor_tensor", [("pTs", sj), "msk"], [("pTs", sj)],
                           out=pTs[sj][:, c_lo:n], in0=pTs[sj][:, c_lo:n], in1=msk[:, mi, c_lo:n], op=ALU.mult)
                        return sj, qi_min
                    def s2(sj, qi_min, h=h, q0=q0, n=n, kc=kc, qt0=qt0, nsub=nsub, last=(kc == nkc - 1)):
                        for qi in range(qi_min, nsub):
                            OP("pe", "matmul", [("pTs", sj), ("VA", kc), "VAones"], [("pmm", 2 + qi)],
                               pmm[2 + qi][:, 0:65], lhsT=pTs[sj][:, qi * 128:(qi + 1) * 128], rhs=VA[:, kc, h, :],
                               start=(kc == 0), stop=(kc == qt0 + qi))
                        if not last:
                            return
                        for qi in range(nsub):
                            tile_i = qt0 + qi - 7
                            acc = pmm[2 + qi]
                            f = fin[stt["f"] % 2]
                            fk = ("fin", stt["f"] % 2)
                            stt["f"] += 1
                            OP("act", "activation", [("pmm", 2 + qi)], ["junk64", fk], out=junk64[:], in_=acc[:, 0:64],
                               func=AF.Square, accum_out=f[0][:])
                            OP("act", "activation", [("pmm", 2 + qi)], [fk], out=f[1][:], in_=acc[:, 64:65], func=AF.Copy)
                            OP("dve", "scalar_tensor_tensor", [fk], [fk], out=f[2][:], in0=f[1][:], scalar=EPS, in1=f[1][:],
                               op0=ALU.mult, op1=ALU.mult)
                            OP("dve", "scalar_tensor_tensor", [fk], [fk], out=f[3][:], in0=f[0][:], scalar=1.0 / 64,
                               in1=f[2][:], op0=ALU.mult, op1=ALU.add)
                            OP("dve", "tensor_scalar_max", [fk], [fk], out=f[3][:], in0=f[3][:], scalar1=1e-30)
                            OP("act", "activation", [fk], [fk], out=f[4][:], in_=f[3][:], func=AF.Ln)
                            OP("act", "activation", [fk], [fk], out=f[4][:], in_=f[4][:], func=AF.Exp, scale=-0.5)
                            OP("dve", "scalar_tensor_tensor", [("pmm", 2 + qi), fk, "gatt"], [("Oatt", tile_i)],
                               out=Oatt[:, tile_i, h * 64:(h + 1) * 64], in0=acc[:, 0:64], scalar=f[4][:, 0:1],
                               in1=gatt[:, h * 64:(h + 1) * 64], op0=ALU.mult, op1=ALU.mult)
                    items.append((s1, s2))
                    first_of_head[0] = False
        LA = 2
        pend = {}
        for k in range(len(items) + LA):
            if k < len(items):
                pend[k] = items[k][0]()
            if k - LA >= 0:
                items[k - LA][1](*pend.pop(k - LA))
        mark("B_prompt_done")
        qb = sbB("qb", [128, 1024])
        ktl = [sbB("ktl%d" % i, [128, 1024]) for i in range(2)]
        vtl = [sbB("vtl%d" % i, [128, 1024]) for i in range(2)]
        prod = sbB("prod", [128, 1024])
        pvb = sbB("pvb", [128, 1024], BF16)
        sc = sbB("sc", [128, 16])
        pb16 = sbB("pb16", [128, 16], BF16)
        onesb = sbB("onesb", [128, 1], BF16)
        OP("pool", "memset", [], ["onesb"], onesb[:], 1.0)
        srow = {nm: sbB("srow_" + nm, [1, 1024]) for nm in ("o", "sq", "o2")}
        srb = sbB("srow_b", [1, 1024], BF16)
        s16 = {nm: sbB("s16_" + nm, [1, 16]) for nm in ("den", "rden", "ms", "r")}
        qs_keys = [("QS", c) for c in (0, 256, 512, 768)]
        ks_keys = [("KVS", "k", c) for c in (0, 256, 512, 768)]
        vs_keys = [("KVS", "v", c) for c in (0, 256, 512, 768)]
        pnum = [pmm[0][0:1, :], pmm[1][0:1, :]]
        pden = pmm[2][0:1, 0:16]
        gi = {"i": 0}
        for bi in range(4):
            DMA("sp", qb[:], QS[bi:bi + 1, :].partition_broadcast(128), qs_keys, ["qb"])
            groups = [(2048 - 128 * r, r, 128) for r in (1, 4, 16)] + [(None, 0, 1)]
            for gidx, (start, rate, rows) in enumerate(groups):
                tb_ = gi["i"] % 2
                gi["i"] += 1
                kt_, vt_ = ktl[tb_], vtl[tb_]
                if start is not None:
                    DMA("sp", kt_[:], cache_k[bi, start:2048:rate, :], [], [("ktl", tb_)])
                    DMA("pool", vt_[:], cache_v[bi, start:2048:rate, :], [], [("vtl", tb_)])
                else:
                    DMA("sp", kt_[0:1, :], KS[bi:bi + 1, :], ks_keys, [("ktl", tb_)])
                    DMA("pool", vt_[0:1, :], VS[bi:bi + 1, :], vs_keys, [("vtl", tb_)])
                R_ = slice(0, rows)
                OP("dve", "tensor_tensor", [("ktl", tb_), "qb"], ["prod"], out=prod[R_, :], in0=kt_[R_, :], in1=qb[R_, :],
                   op=ALU.mult)
                OP("dve", "tensor_reduce", ["prod"], ["sc"], out=sc[R_, :],
                   in_=prod[R_, :].rearrange("p (h e) -> p h e", e=64), axis=AX.X, op=ALU.add)
                OP("act", "activation", ["sc"], ["sc"], out=sc[R_, :], in_=sc[R_, :], func=AF.Exp, scale=0.125)
                if start is None:
                    OP("dve", "tensor_scalar", ["sc"], ["sc"], sc[R_, :], sc[R_, :], 3.0, None, ALU.mult)
                OP("dve", "tensor_copy", ["sc"], ["pb16"], out=pb16[R_, :], in_=sc[R_, :])
                OP("dve", "tensor_tensor", [("vtl", tb_), "sc"], ["pvb"],
                   out=pvb[R_, :].rearrange("p (h e) -> p h e", e=64),
                   in0=vt_[R_, :].rearrange("p (h e) -> p h e", e=64),
                   in1=sc[R_, :].unsqueeze(2).broadcast_to([rows, 16, 64]), op=ALU.mult)
                first, last = (gidx == 0), (gidx == 3)
                for hf in range(2):
                    OP("pe", "matmul", ["pvb", "onesb"], [("pmm", hf)], pnum[hf], lhsT=onesb[R_, :],
                       rhs=pvb[R_, hf * 512:(hf + 1) * 512], start=first, stop=last)
                OP("pe", "matmul", ["pb16", "onesb"], [("pmm", 2)], pden, lhsT=onesb[R_, :], rhs=pb16[R_, :], start=first,
                   stop=last)
            OP("act", "activation", [("pmm", 2)], ["s16"], out=s16["den"][:], in_=pden, func=AF.Copy)
            OP("dve", "reciprocal", ["s16"], ["s16"], out=s16["rden"][:], in_=s16["den"][:])
            for hf in range(2):
                OP("dve", "tensor_tensor", [("pmm", hf), "s16"], ["srow_o"],
                   out=srow["o"][:, hf * 512:(hf + 1) * 512].rearrange("p (h e) -> p h e", e=64),
                   in0=pnum[hf].rearrange("p (h e) -> p h e", e=64),
                   in1=s16["rden"][:, hf * 8:(hf + 1) * 8].unsqueeze(2).broadcast_to([1, 8, 64]), op=ALU.mult)
            OP("dve", "tensor_tensor", ["srow_o"], ["srow_sq"], out=srow["sq"][:], in0=srow["o"][:], in1=srow["o"][:],
               op=ALU.mult)
            OP("dve", "tensor_reduce", ["srow_sq"], ["s16"], out=s16["ms"][:],
               in_=srow["sq"][:].rearrange("p (h e) -> p h e", e=64), axis=AX.X, op=ALU.add)
            OP("act", "activation", ["s16", "epst"], ["s16"], out=s16["r"][:], in_=s16["ms"][:], func=AF.Ln, scale=1.0 / 64,
               bias=epst[0:1, 0:1])
            OP("act", "activation", ["s16"], ["s16"], out=s16["r"][:], in_=s16["r"][:], func=AF.Exp, scale=-0.5)
            OP("dve", "tensor_tensor", ["srow_o", "s16"], ["srow_o2"],
               out=srow["o2"][:].rearrange("p (h e) -> p h e", e=64),
               in0=srow["o"][:].rearrange("p (h e) -> p h e", e=64),
               in1=s16["r"][:].unsqueeze(2).broadcast_to([1, 16, 64]), op=ALU.mult)
            OP("dve", "tensor_tensor", ["srow_o2", "gatt"], ["srow_b"], out=srb[:], in0=srow["o2"][:], in1=gatt[0:1, :],
               op=ALU.mult)
            DMA("sp", O_scr[1152 + bi:1153 + bi, 0:1024], srb[:], ["srow_b"], ["Oscr_samp"])
        DMA("sp", O_scr[0:1152, 0:1024].rearrange("(c p) f -> p c f", p=128), Oatt[:], [("Oatt", i) for i in range(9)],
            ["Oscr_att"])
        P.barrier()
        stB.close()
        stAB.close()
        mark("B_done")
        stC = contextlib.ExitStack()
        def sbC(name, shape, dt=F32):
            return stC.enter_context(nc.sbuf_tensor(name, list(shape), dt))
        i64 = ident[0:64, 0:64]
        ones64 = sbC("ones64", [64, 64], BF16)
        OP("pool", "memset", [], ["ones64"], ones64[:], 1.0)
        identf = sbC("identf", [64, 64])
        OP("dve", "tensor_copy", ["ident"], ["identf"], out=identf[:], in_=ident[0:64, 0:64])
        lora_w = sbC("lora_w", [64, 1024], BF16)
        lora_a = sbC("lora_a", [64, 1024], BF16)
        DMA("pool", lora_w[:], rw_w_up[:, :], [], ["lora_w"])
        DMA("pool", lora_a[:], rw_a_up[:, :], [], ["lora_a"])
        gupb = sbC("gupb", [128, 2, 1024], BF16)
        OP("pool", "memset", [], ["gupb"], gupb[:, 1, :], 0.0)
        DMA("pool", gupb[:, 0, :], rw_g_up[0:128, :], ["gupb"], ["gupb"])
        DMA("pool", gupb[0:32, 1, :], rw_g_up[128:160, :], ["gupb"], ["gupb"])
        colv = {}
        for nm, apd in (("w0", rw_w0), ("a0", rw_a0), ("kk", rw_k_k), ("ka", rw_k_a), ("rk", rw_r_k)):
            tcol = sbC("col_" + nm, [64, 16])
            DMA("sp", tcol[:], apd.rearrange("(h p) -> p h", p=64), [], ["col_" + nm], allow_slow_non_contiguous=True)
            colv[nm] = tcol
        lnw2 = sbC("lnw2", [128, 4, 128])
        lnb2 = sbC("lnb2", [128, 4, 128])
        for t_ in range(2):
            DMA("pool", lnw2[t_ * 64:(t_ + 1) * 64, :, :],
                rw_lnx_w[0:1, :].rearrange("o (q t f) -> o t q f", t=2, f=128)[:, t_].partition_broadcast(64), [], ["lnw2"])
            DMA("pool", lnb2[t_ * 64:(t_ + 1) * 64, :, :],
                rw_lnx_b[0:1, :].rearrange("o (q t f) -> o t q f", t=2, f=128)[:, t_].partition_broadcast(64), [], ["lnb2"])
        colv2 = {}
        for nm, apd in (("w0", rw_w0), ("a0", rw_a0), ("kk", rw_k_k), ("ka", rw_k_a), ("rk", rw_r_k)):
            tcol = sbC("col2_" + nm, [128, 4, 2])
            for t_ in range(2):
                for hi_ in range(2):
                    DMA("sp", tcol[t_ * 64:(t_ + 1) * 64, :, hi_],
                        apd.rearrange("(q t hi p) -> t hi p q", t=2, hi=2, p=64)[t_, hi_], ["col2_" + nm], ["col2_" + nm],
                        allow_slow_non_contiguous=True)
            colv2[nm] = tcol
        lora_w2 = sbC("lora_w2", [64, 4, 2, 2, 64], BF16)
        lora_a2 = sbC("lora_a2", [64, 4, 2, 2, 64], BF16)
        for q_ in range(4):
            for t_ in range(2):
                DMA("pool", lora_w2[:, q_, :, t_, :],
                    rw_w_up.rearrange("l (q t hi c) -> l q t hi c", t=2, hi=2, c=64)[:, q_, t_], [], ["lora_w2"])
                DMA("pool", lora_a2[:, q_, :, t_, :],
                    rw_a_up.rearrange("l (q t hi c) -> l q t hi c", t=2, hi=2, c=64)[:, q_, t_], [], ["lora_a2"])
        bd = sbC("bd", [128, 128], BF16)
        OP("pool", "memset", [], ["bd"], bd[:], 0.0)
        OP("pool", "memset", ["bd"], ["bd"], bd[0:64, 0:64], 1.0)
        OP("pool", "memset", ["bd"], ["bd"], bd[64:128, 64:128], 1.0)
        identf2 = sbC("identf2", [128, 64])
        OP("dve", "tensor_copy", ["ident"], ["identf2"], out=identf2[0:64, :], in_=ident[0:64, 0:64])
        OP("dve", "tensor_copy", ["ident", "identf2"], ["identf2"], out=identf2[64:128, :], in_=ident[64:128, 64:128])
        trim2 = sbC("trim2", [128, 320], BF16)
        DMA("sp", trim2[0:64, :], trim_d[:, :], [], ["trim2"])
        DMA("sp", trim2[64:128, :], trim_d[:, :], ["trim2"], ["trim2"])
        rm2 = sbC("rm2", [128, 512])
        OP("pool", "memset", [], ["rm2"], rm2[:], 1.0)
        OP("pool", "memset", ["rm2"], ["rm2"], rm2[:].rearrange("p (n l) -> p n l", l=64)[:, :, 0:1], 0.0)
        NTC = 2052
        TW = sbC("TW", [64, NTC], BF16)
        AD = sbC("AD", [64, NTC], BF16)
        SG0 = sbC("SG0", [128, NTC], BF16)
        SG1 = sbC("SG1", [128, NTC], BF16)
        OP("pool", "memset", [], ["SG1"], SG1[:], 0.0)
        la = sbC("la", [64, 512])
        lb = sbC("lb", [64, 512])
        lg0 = sbC("lg0", [128, 512])
        lg1 = sbC("lg1", [32, 512])
        for (t0, n) in [(0, 512), (512, 512), (1024, 512), (1536, 512), (2048, 4)]:
            DMA("sp", la[:, 0:n], XS[3072:3136, t0:t0 + n], [("XS", 24, t0)], ["la"])
            DMA("sp", lb[:, 0:n], XS[3136:3200, t0:t0 + n], [("XS", 24, t0)], ["lb"])
            DMA("sp", lg0[:, 0:n], XS[3200:3328, t0:t0 + n], [("XS", 25, t0)], ["lg0"])
            DMA("sp", lg1[:, 0:n], XS[3328:3360, t0:t0 + n], [("XS", 26, t0)], ["lg1"])
            OP("act", "activation", ["la"], ["TW"], out=TW[:, t0:t0 + n], in_=la[:, 0:n], func=AF.Tanh)
            OP("act", "activation", ["lb"], ["AD"], out=AD[:, t0:t0 + n], in_=lb[:, 0:n], func=AF.Copy)
            OP("act", "activation", ["lg0"], ["SG0"], out=SG0[:, t0:t0 + n], in_=lg0[:, 0:n], func=AF.Sigmoid)
            OP("act", "activation", ["lg1", "SG1"], ["SG1"], out=SG1[0:32, t0:t0 + n], in_=lg1[:, 0:n], func=AF.Sigmoid)
        stC2 = contextlib.ExitStack()
        sbC_outer = sbC
        def sbC(name, shape, dt=F32):
            return stC2.enter_context(nc.sbuf_tensor(name, list(shape), dt))
        W = {nm: sbC("w_" + nm, [128, 512]) for nm in
             ("r32", "k32", "v32", "lw", "asig", "kk", "rn", "t1", "keff", "bvec", "cum", "ep", "em", "ea")}
        kk2 = sbC("w_kk2", [128, 512], BF16)
        wkvo = sbC("wkvo", [128, 2, 64])
        bufs = []
        for par in range(2):
            d = {}
            d["AR"] = sbC("AR%d" % par, [128, 2, 32, 192], BF16)
            d["KT"] = sbC("KTt%d" % par, [128, 2, 32, 64], BF16)
            d["BT"] = sbC("BTt%d" % par, [128, 2, 32, 64], BF16)
            d["VT"] = sbC("VTt%d" % par, [128, 2, 32, 64], BF16)
            d["RK"] = sbC("RK%d" % par, [128, 2, 2048], BF16)
            d["WL"] = sbC("WL%d" % par, [128, 2, 32])
            d["STf"] = sbC("STf%d" % par, [128, 2, 64])
            d["STb"] = sbC("STb%d" % par, [128, 2, 64], BF16)
            d["MBs"] = sbC("MBs%d" % par, [128, 2, 192], BF16)
            d["MKs"] = sbC("MKs%d" % par, [128, 2, 192], BF16)
            d["MNs"] = sbC("MNs%d" % par, [128, 2, 128], BF16)
            d["Uf"] = sbC("Uf%d" % par, [128, 2, 64])
            d["Ub"] = sbC("Ub%d" % par, [128, 2, 64], BF16)
            d["PPs"] = [sbC("PPs%d_%d" % (par, i), [128, 2, 128], BF16) for i in range(2)]
            d["tmpS"] = sbC("tmpS%d" % par, [128, 2, 64])
            d["yn"] = sbC("yn%d" % par, [128, 2, 64])
            d["st"] = sbC("st%d" % par, [128, 12])
            d["jk"] = sbC("jk%d" % par, [128, 64])
            d["Ost"] = sbC("Ost%d" % par, [128, 18, 128], BF16)
            for t_i in range(2):
                for hi_i in range(2):
                    OP("dve", "tensor_copy", ["ident"], [("ARI", par)],
                       out=d["AR"][t_i * 64:(t_i + 1) * 64, hi_i, :, 128:192],
                       in_=ident[t_i * 64:(t_i + 1) * 64, t_i * 64:(t_i + 1) * 64].unsqueeze(1).broadcast_to([64, 32, 64]))
            bufs.append(d)

        PZW = (6, 384)
        PZA = (7, 384)
        PSS = (5, 260)

        def prep_block(q4, par, tb):
            B = bufs[par]
            kAR, kKT, kBT, kVT, kRK, kWL = [(x, par, tb) for x in ("AR", "KT", "BT", "VT", "RK", "WL")]
            t0 = tb * 512
            c0 = tb * 8
            for hi in range(2):
                hh = [4 * q4 + hi, 4 * q4 + 2 + hi]
                cs = slice(q4 * 2 + hi, q4 * 2 + hi + 1)
                cv_ = lambda nm: colv2[nm][:].rearrange("p q h -> p (q h)")[:, cs]
                for t_, h in enumerate(hh):
                    ps_ = slice(t_ * 64, (t_ + 1) * 64)
                    DMA("sp", W["r32"][ps_, :], XS[h * 64:(h + 1) * 64, t0:t0 + 512], [("XS", h // 2, t0)], ["r32"])
                    DMA("sp", W["k32"][ps_, :], XS[1024 + h * 64:1024 + (h + 1) * 64, t0:t0 + 512],
                        [("XS", 8 + h // 2, t0)], ["k32"])
                    DMA("sp", W["v32"][ps_, :], XS[2048 + h * 64:2048 + (h + 1) * 64, t0:t0 + 512],
                        [("XS", 16 + h // 2, t0)], ["v32"])
                yield
                for pc in range(4):
                    cc = slice(pc * 128, (pc + 1) * 128)
                    tc_ = slice(t0 + pc * 128, t0 + (pc + 1) * 128)
                    zw = pmm[PZW[0]][:, PZW[1]:PZW[1] + 128]
                    za = pmm[PZA[0]][:, PZA[1]:PZA[1] + 128]
                    OP("pe", "matmul", ["lora_w2", "TW"], [("pmm", PZW[0])], zw,
                       lhsT=lora_w2[:, q4, hi, :, :].rearrange("l t c -> l (t c)"), rhs=TW[:, tc_], start=True, stop=True)
                    OP("act", "activation", [("pmm", PZW[0]), "col2_w0"], ["lw"], out=W["lw"][:, cc], in_=zw,
                       func=AF.Sigmoid, bias=cv_("w0"))
                    OP("pe", "matmul", ["lora_a2", "AD"], [("pmm", PZA[0])], za,
                       lhsT=lora_a2[:, q4, hi, :, :].rearrange("l t c -> l (t c)"), rhs=AD[:, tc_], start=True, stop=True)
                    OP("act", "activation", [("pmm", PZA[0]), "col2_a0"], ["asig"], out=W["asig"][:, cc], in_=za,
                       func=AF.Sigmoid, bias=cv_("a0"))
                    yield
                OP("dve", "tensor_scalar", ["lw"], ["lw"], W["lw"][:], W["lw"][:], -0.6065306597126334, None, ALU.mult)
                OP("dve", "tensor_scalar", ["k32", "col2_kk"], ["kk"], W["kk"][:], W["k32"][:], cv_("kk"), None, ALU.mult)
                OP("dve", "tensor_tensor", ["kk"], ["kk2"], out=kk2[:], in0=W["kk"][:], in1=W["kk"][:], op=ALU.mult)
                yield
                for pc in range(4):
                    cc = slice(pc * 128, (pc + 1) * 128)
                    zs = pmm[PSS[0]][:, PSS[1]:PSS[1] + 128]
                    OP("pe", "matmul", ["bd", "kk2"], [("pmm", PSS[0])], zs, lhsT=bd[:], rhs=kk2[:, cc], start=True,
                       stop=True)
                    OP("dve", "tensor_scalar_max", [("pmm", PSS[0])], ["rn"], out=W["rn"][:, cc], in0=zs, scalar1=1e-24)
                    yield
                OP("act", "activation", ["rn"], ["rn"], out=W["rn"][:], in_=W["rn"][:], func=AF.Ln)
                OP("act", "activation", ["rn"], ["rn"], out=W["rn"][:], in_=W["rn"][:], func=AF.Exp, scale=-0.5)
                OP("dve", "tensor_tensor", ["kk", "rn"], ["kk"], out=W["kk"][:], in0=W["kk"][:], in1=W["rn"][:],
                   op=ALU.mult)
                yield
                OP("dve", "tensor_scalar", ["asig", "col2_ka"], ["t1"], W["t1"][:], W["asig"][:], -1.0, cv_("ka"),
                   ALU.add, ALU.mult)
                OP("dve", "scalar_tensor_tensor", ["t1", "k32"], ["keff"], out=W["keff"][:], in0=W["t1"][:],
                   scalar=1.0, in1=W["k32"][:], op0=ALU.add, op1=ALU.mult)
                OP("dve", "tensor_tensor", ["kk", "asig"], ["bvec"], out=W["bvec"][:], in0=W["kk"][:],
                   in1=W["asig"][:], op=ALU.mult)
                yield
                OP("dve", "tensor_tensor_scan", ["rm2", "lw"], ["cum"], out=W["cum"][:], data0=rm2[:],
                   data1=W["lw"][:], initial=0.0, op0=ALU.mult, op1=ALU.add)
                OP("act", "activation", ["cum"], ["ep"], out=W["ep"][:], in_=W["cum"][:], func=AF.Exp)
                OP("act", "activation", ["cum"], ["em"], out=W["em"][:], in_=W["cum"][:], func=AF.Exp, scale=-1.0)
                OP("dve", "tensor_tensor", ["cum", "lw"], ["ea"], out=W["ea"][:], in0=W["cum"][:], in1=W["lw"][:],
                   op=ALU.subtract)
                OP("act", "activation", ["ea"], ["ea"], out=W["ea"][:], in_=W["ea"][:], func=AF.Exp)
                yield
                v3 = lambda a: a[:].rearrange("p (n l) -> p n l", l=64)
                OP("dve", "tensor_tensor", ["r32", "ep"], [kAR], out=B["AR"][:, hi, c0:c0 + 8, 64:128],
                   in0=v3(W["r32"]), in1=v3(W["ep"]), op=ALU.mult)
                OP("dve", "scalar_tensor_tensor", ["kk", "ea"], [kAR], out=B["AR"][:, hi, c0:c0 + 8, 0:64],
                   in0=v3(W["kk"]), scalar=-1.0, in1=v3(W["ea"]), op0=ALU.mult, op1=ALU.mult)
                OP("pool", "tensor_tensor", ["keff", "em"], [kKT], out=B["KT"][:, hi, c0:c0 + 8, :],
                   in0=v3(W["keff"]), in1=v3(W["em"]), op=ALU.mult)
                yield
                OP("pool", "tensor_tensor", ["bvec", "em"], [kBT], out=B["BT"][:, hi, c0:c0 + 8, :],
                   in0=v3(W["bvec"]), in1=v3(W["em"]), op=ALU.mult)
                OP("act", "activation", ["v32"], [kVT], out=B["VT"][:, hi, c0:c0 + 8, :], in_=v3(W["v32"]),
                   func=AF.Copy)
                OP("dve", "tensor_copy", ["ep"], [kWL], out=B["WL"][:, hi, c0:c0 + 8], in_=v3(W["ep"])[:, :, 63])
                OP("dve", "scalar_tensor_tensor", ["r32", "col2_rk", "keff"], [kRK],
                   out=B["RK"][:, hi, t0:t0 + 512], in0=W["r32"][:], scalar=cv_("rk"), in1=W["keff"][:],
                   op0=ALU.mult, op1=ALU.mult)
                yield

        m6 = trim2[:, 0:192].unsqueeze(1).broadcast_to([128, 2, 192])
        ml = trim2[:, 192:320].unsqueeze(1).broadcast_to([128, 2, 128])
        def _v(bank, c0, ncol, c):
            return pmm[bank][:, c0:c0 + ncol].rearrange("p (h c) -> p h c", c=c)
        PV = [
            (_v(0, 0, 384, 192), _v(1, 0, 384, 192), _v(2, 0, 256, 128), _v(2, 256, 128, 64), _v(3, 0, 128, 64),
             _v(4, 0, 256, 128), _v(5, 0, 128, 64), pmm[5][:, 128:256], pmm[5][:, 256:258],
             ("pmm", 0), ("pmm", 1), ("pmm", 2), ("pmm", 2), ("pmm", 3), ("pmm", 4), ("pmm", 5), ("pmm", 5), ("pmm", 5)),
            (_v(6, 0, 384, 192), _v(7, 0, 384, 192), _v(3, 128, 256, 128), _v(3, 384, 128, 64), _v(0, 384, 128, 64),
             _v(4, 256, 256, 128), _v(1, 384, 128, 64), pmm[2][:, 384:512], pmm[5][:, 258:260],
             ("pmm", 6), ("pmm", 7), ("pmm", 3), ("pmm", 3), ("pmm", 0), ("pmm", 4), ("pmm", 1), ("pmm", 2), ("pmm", 5)),
        ]
        LNX_EPS = 64 * 1e-5
        HALF = [slice(0, 64), slice(64, 128)]
        TH = [(t_, hi) for t_ in range(2) for hi in range(2)]

        def chunk_step(q4, n, par):
            B = bufs[par]
            K = lambda x: (x, par)
            pMB, pMK, pMN, pS, pZ, pPP, pY, pG, pBo, kMB, kMK, kMN, kS, kZ, kPP, kY, kG, kBo = PV[par]
            AR, KT_, BT_, VT_ = B["AR"], B["KT"], B["BT"], B["VT"]
            prepk = [(x, par, n // 8) for x in ("AR", "KT", "BT", "VT")]
            for t_, hi in TH:
                p_ = HALF[t_]
                idh = ident[p_, p_]
                OP("pe", "matmul", prepk + [("ARI", par)], [kMB], pMB[p_, hi, :], lhsT=BT_[p_, hi, n, :],
                   rhs=AR[p_, hi, n, :], start=True, stop=True)
                OP("pe", "matmul", prepk + [("ARI", par)], [kMK], pMK[p_, hi, :], lhsT=KT_[p_, hi, n, :],
                   rhs=AR[p_, hi, n, :], start=True, stop=True)
                OP("pe", "matmul", prepk, [kMN], pMN[p_, hi, 0:64], lhsT=AR[p_, hi, n, 0:64], rhs=BT_[p_, hi, n, :],
                   start=True, stop=True)
                OP("pe", "matmul", prepk + ["ident"], [kMN], pMN[p_, hi, 64:128], lhsT=VT_[p_, hi, n, :], rhs=idh,
                   start=True, stop=True)
            yield
            OP("dve", "tensor_tensor", [kMB, "trim2"], [K("MBs")], out=B["MBs"][:], in0=pMB, in1=m6, op=ALU.mult)
            OP("dve", "tensor_tensor", [kMK, "trim2"], [K("MKs")], out=B["MKs"][:], in0=pMK, in1=m6, op=ALU.mult)
            OP("dve", "tensor_tensor", [kMN, "trim2"], [K("MNs")], out=B["MNs"][:], in0=pMN, in1=ml, op=ALU.mult)
            yield
            for t_, hi in TH:
                p_ = HALF[t_]
                OP("pe", "matmul", prepk + [K("STb")], [kZ], pZ[p_, hi, :], lhsT=AR[p_, hi, n, 0:64],
                   rhs=B["STb"][p_, hi, :], start=True, stop=False)
                OP("pe", "matmul", [K("MKs"), K("MNs")], [kZ], pZ[p_, hi, :], lhsT=B["MKs"][p_, hi, 0:64],
                   rhs=B["MNs"][p_, hi, 64:128], start=False, stop=True)
            OP("act", "activation", [kZ], [K("Uf")], out=B["Uf"][:], in_=pZ, func=AF.Copy)
            OP("act", "activation", [K("Uf")], [K("Ub")], out=B["Ub"][:], in_=B["Uf"][:], func=AF.Copy)
            yield
            PT = lambda p_, hi: B["MBs"][p_, hi, 0:64]
            Pm = lambda p_, hi: B["MNs"][p_, hi, 0:64]
            pkeys = [K("MBs"), K("MNs")]
            for it in range(6):
                for t_, hi in TH:
                    p_ = HALF[t_]
                    OP("pe", "matmul", pkeys + [K("Ub")], [kZ], pZ[p_, hi, :], lhsT=PT(p_, hi), rhs=B["Ub"][p_, hi, :],
                       start=True, stop=True)
                yield
                OP("dve", "tensor_tensor", [kZ, K("Uf")], [K("Uf")], out=B["Uf"][:], in0=pZ, in1=B["Uf"][:],
                   op=ALU.add)
                OP("act", "activation", [K("Uf")], [K("Ub")], out=B["Ub"][:], in_=B["Uf"][:], func=AF.Copy)
                if it < 5:
                    for t_, hi in TH:
                        p_ = HALF[t_]
                        OP("pe", "matmul", pkeys, [kPP], pPP[p_, hi, 0:64], lhsT=Pm(p_, hi), rhs=PT(p_, hi), start=True,
                           stop=True)
                        OP("pe", "matmul", pkeys, [kPP], pPP[p_, hi, 64:128], lhsT=PT(p_, hi), rhs=Pm(p_, hi), start=True,
                           stop=True)
                    pp = B["PPs"][it % 2]
                    yield
                    OP("act", "activation", [kPP], [K("PPs%d" % (it % 2))], out=pp[:], in_=pPP, func=AF.Copy)
                    PT = lambda p_, hi, pp=pp: pp[p_, hi, 0:64]
                    Pm = lambda p_, hi, pp=pp: pp[p_, hi, 64:128]
                    pkeys = [K("PPs%d" % (it % 2))]
            if n >= 14:
                ci = n - 14
                for t_, hi in TH:
                    p_ = HALF[t_]
                    OP("pe", "matmul", prepk + [K("STb")], [kY], pY[p_, hi, :], lhsT=AR[p_, hi, n, 64:128],
                       rhs=B["STb"][p_, hi, :], start=True, stop=False)
                    OP("pe", "matmul", [K("MBs"), K("Ub")], [kY], pY[p_, hi, :], lhsT=B["MBs"][p_, hi, 64:128],
                       rhs=B["Ub"][p_, hi, :], start=False, stop=False)
                    OP("pe", "matmul", [K("MKs"), K("MNs")], [kY], pY[p_, hi, :], lhsT=B["MKs"][p_, hi, 64:128],
                       rhs=B["MNs"][p_, hi, 64:128], start=False, stop=True)
                for t_ in range(2):
                    p_ = HALF[t_]
                    gc = slice((4 * q4 + 2 * t_) * 64, (4 * q4 + 2 * t_) * 64 + 128)
                    OP("pe", "matmul", ["SG0", "gupb"], [kG], pG[p_, :], lhsT=SG0[:, n * 64:(n + 1) * 64],
                       rhs=gupb[:, 0, gc], start=True, stop=False)
                    OP("pe", "matmul", ["SG1", "gupb"], [kG], pG[p_, :], lhsT=SG1[:, n * 64:(n + 1) * 64],
                       rhs=gupb[:, 1, gc], start=False, stop=True)
                for t_, hi in TH:
                    p_ = HALF[t_]
                    OP("pe", "matmul", [("RK", par, n // 8), "bd"], [kBo], pBo[p_, hi:hi + 1],
                       lhsT=B["RK"][p_, hi, n * 64:(n + 1) * 64], rhs=bd[p_, t_ * 64:t_ * 64 + 1], start=True, stop=True)
                yield
                stt_ = B["st"]
                ks = K("st")
                OP("dve", "tensor_reduce", [kY], [ks], out=stt_[:, 0:2], in_=pY, axis=AX.X, op=ALU.add)
                for hi in range(2):
                    OP("act", "activation", [kY], [K("jk"), ks], out=B["jk"][:], in_=pY[:, hi, :],
                       func=AF.Square, accum_out=stt_[:, 2 + hi:3 + hi])
                OP("act", "activation", [kBo], [ks], out=stt_[:, 10:12], in_=pBo, func=AF.Copy)
                OP("dve", "tensor_scalar", [ks], [ks], stt_[:, 4:6], stt_[:, 0:2], 1.0 / 64, None, ALU.mult)
                OP("dve", "tensor_tensor", [ks], [ks], out=stt_[:, 6:8], in0=stt_[:, 4:6], in1=stt_[:, 4:6], op=ALU.mult)
                OP("dve", "scalar_tensor_tensor", [ks], [ks], out=stt_[:, 8:10], in0=stt_[:, 2:4], scalar=1.0 / 64,
                   in1=stt_[:, 6:8], op0=ALU.mult, op1=ALU.subtract)
                OP("dve", "tensor_scalar", [ks], [ks], stt_[:, 8:10], stt_[:, 8:10], LNX_EPS, None, ALU.add)
                OP("act", "activation", [ks], [ks], out=stt_[:, 8:10], in_=stt_[:, 8:10], func=AF.Ln)
                OP("act", "activation", [ks], [ks], out=stt_[:, 8:10], in_=stt_[:, 8:10], func=AF.Exp, scale=-0.5)
                for hi in range(2):
                    OP("dve", "tensor_scalar", [kY, ks], [K("yn")], B["yn"][:, hi, :], pY[:, hi, :],
                       stt_[:, 4 + hi:5 + hi], stt_[:, 8 + hi:9 + hi], ALU.subtract, ALU.mult)
                ynf = B["yn"][:].rearrange("p h c -> p (h c)")
                OP("pool", "tensor_tensor", [K("yn"), "lnw2"], [K("yn")], out=ynf, in0=ynf, in1=lnw2[:, q4, :], op=ALU.mult)
                OP("pool", "tensor_tensor", [K("yn"), "lnb2"], [K("yn")], out=ynf, in0=ynf, in1=lnb2[:, q4, :], op=ALU.add)
                for hi in range(2):
                    OP("dve", "scalar_tensor_tensor", [K("MNs"), ks, K("yn")], [K("yn")], out=B["yn"][:, hi, :],
                       in0=B["MNs"][:, hi, 64:128], scalar=stt_[:, 10 + hi:11 + hi], in1=B["yn"][:, hi, :],
                       op0=ALU.mult, op1=ALU.add)
                OP("dve", "tensor_tensor", [K("yn"), kG], [K("Ost")], out=B["Ost"][:, ci, :], in0=ynf, in1=pG, op=ALU.mult)
            for t_, hi in TH:
                p_ = HALF[t_]
                OP("pe", "matmul", [K("MBs"), K("Ub")], [kS], pS[p_, hi, :], lhsT=B["MBs"][p_, hi, 128:192],
                   rhs=B["Ub"][p_, hi, :], start=True, stop=False)
                OP("pe", "matmul", [K("MKs"), K("MNs")], [kS], pS[p_, hi, :], lhsT=B["MKs"][p_, hi, 128:192],
                   rhs=B["MNs"][p_, hi, 64:128], start=False, stop=True)
            yield
            OP("dve", "tensor_tensor", [kS, K("STf")], [K("tmpS")], out=B["tmpS"][:], in0=pS, in1=B["STf"][:],
               op=ALU.add)
            OP("pool", "tensor_tensor", [K("tmpS"), ("WL", par, n // 8)], [K("STf")], out=B["STf"][:], in0=B["tmpS"][:],
               in1=B["WL"][:, :, n:n + 1].broadcast_to([128, 2, 64]), op=ALU.mult)
            OP("act", "activation", [K("STf")], [K("STb")], out=B["STb"][:], in_=B["STf"][:], func=AF.Copy)

        def chain_block(q4, par, tb):
            for n in range(tb * 8, tb * 8 + 8):
                yield from chunk_step(q4, n, par)

        def prep_seq(q0, tb):
            for par in range(2):
                yield from prep_block(q0 + par, par, tb)

        def round_robin(gens):
            live = list(gens)
            while live:
                for g in list(live):
                    try:
                        next(g)
                    except StopIteration:
                        live.remove(g)

        for q0 in range(0, 4, 2):
            for par in range(2):
                B = bufs[par]
                OP("pool", "memset", [], [("STf", par)], B["STf"][:], 0.0)
                OP("pool", "memset", [], [("STb", par)], B["STb"][:], 0.0)
            round_robin([prep_seq(q0, 0)])
            for tb in range(4):
                gens = [chain_block(q0 + par, par, tb) for par in range(2)]
                if tb < 3:
                    gens.append(prep_seq(q0, tb + 1))
                round_robin(gens)
            for q4 in (q0, q0 + 1):
                par = q4 % 2
                B = bufs[par]
                for t_, hi in TH:
                    p_ = HALF[t_]
                    OP("pe", "matmul", [("STf", par), "identf2"], [("pmm", 3)], pmm[3][p_, hi * 64:(hi + 1) * 64],
                       lhsT=B["STf"][p_, hi, :], rhs=identf2[p_, :], start=True, stop=True)
                OP("dve", "tensor_copy", [("pmm", 3)], ["wkvo"], out=wkvo[:].rearrange("p h c -> p (h c)"),
                   in_=pmm[3][:, 0:128])
                for t_ in range(2):
                    p_ = HALF[t_]
                    h0 = 4 * q4 + 2 * t_
                    DMA("sp", pwkv_out[h0:h0 + 2, :, :].rearrange("h i j -> i h j"), wkvo[p_, :, :], ["wkvo"],
                        [("pwkv", q4, t_)])
                    DMA("sp", O_scr[0:1152, 1024 + h0 * 64:1024 + h0 * 64 + 128].rearrange("(c p) f -> p c f", p=64),
                        B["Ost"][p_, :, :], [("Ost", par)], [("Oscr_rw", 2 * q4 + t_)])
        P.barrier()
        stC2.close()
        sbC = sbC_outer
        mark("C_prompt_done")
        XC = {}
        for nm, r0 in (("r", 0), ("k", 1024), ("v", 2048)):
            tcx = sbC("xc_" + nm, [64, 16, 4])
            DMA("sp", tcx[:], XS[r0:r0 + 1024, 2048:2052].rearrange("(h p) b -> p h b", p=64),
                [("XS", ci_, 2048) for ci_ in range(r0 // 128, r0 // 128 + 8)], ["xc_" + nm])
            XC[nm] = tcx
        lnwc = sbC("lnwc", [64, 16])
        lnbc = sbC("lnbc", [64, 16])
        DMA("sp", lnwc[:], rw_lnx_w_c.rearrange("(h p) -> p h", p=64), [], ["lnwc"], allow_slow_non_contiguous=True)
        DMA("sp", lnbc[:], rw_lnx_b_c.rearrange("(h p) -> p h", p=64), [], ["lnbc"], allow_slow_non_contiguous=True)
        onesf = sbC("onesf", [64, 64])
        OP("pool", "memset", [], ["onesf"], onesf[:], 1.0)
        SW = {nm: sbC("sw_" + nm, [64, 16, 4]) for nm in
              ("lw", "a", "kk", "kk2", "rn", "t1", "keff", "b", "rk", "g", "y", "y2", "mean", "var", "yn", "bon", "o")}
        vec5 = sbC("vec5", [64, 16, 4, 5])
        ob16 = sbC("ob16", [64, 16, 4], BF16)
        bc16 = lambda t: t[:, :].unsqueeze(2).broadcast_to([64, 16, 4])
        pz = pmm[0][0:64, 0:64].rearrange("p (h b) -> p h b", b=4)
        pz2 = pmm[1][0:64, 0:64].rearrange("p (h b) -> p h b", b=4)
        pz3 = pmm[2][0:64, 0:64].rearrange("p (h b) -> p h b", b=4)
        for h in range(16):
            hc = slice(h * 64, (h + 1) * 64)
            OP("pe", "matmul", ["lora_w", "TW"], [("pmm", 0)], pz[:, h, :], lhsT=lora_w[:, hc], rhs=TW[:, 2048:2052],
               start=True, stop=True)
            OP("pe", "matmul", ["lora_a", "AD"], [("pmm", 1)], pz2[:, h, :], lhsT=lora_a[:, hc], rhs=AD[:, 2048:2052],
               start=True, stop=True)
            OP("pe", "matmul", ["SG0", "gupb"], [("pmm", 2)], pz3[:, h, :], lhsT=gupb[:, 0, hc], rhs=SG0[:, 2048:2052],
               start=True, stop=False)
            OP("pe", "matmul", ["SG1", "gupb"], [("pmm", 2)], pz3[:, h, :], lhsT=gupb[:, 1, hc], rhs=SG1[:, 2048:2052],
               start=False, stop=True)
        OP("dve", "tensor_tensor", [("pmm", 0), "col_w0"], ["sw_lw"], out=SW["lw"][:], in0=pz, in1=bc16(colv["w0"]),
           op=ALU.add)
        OP("act", "activation", ["sw_lw"], ["sw_lw"], out=SW["lw"][:], in_=SW["lw"][:], func=AF.Sigmoid)
        OP("act", "activation", ["sw_lw"], ["vec5"], out=vec5[:, :, :, 1], in_=SW["lw"][:], func=AF.Exp,
           scale=-0.6065306597126334)
        OP("dve", "tensor_tensor", [("pmm", 1), "col_a0"], ["sw_a"], out=SW["a"][:], in0=pz2, in1=bc16(colv["a0"]),
           op=ALU.add)
        OP("act", "activation", ["sw_a"], ["sw_a"], out=SW["a"][:], in_=SW["a"][:], func=AF.Sigmoid)
        OP("act", "activation", [("pmm", 2)], ["sw_g"], out=SW["g"][:], in_=pz3, func=AF.Copy)
        OP("dve", "tensor_tensor", ["xc_k", "col_kk"], ["sw_kk"], out=SW["kk"][:], in0=XC["k"][:], in1=bc16(colv["kk"]),
           op=ALU.mult)
        OP("dve", "tensor_tensor", ["sw_kk"], ["sw_kk2"], out=SW["kk2"][:], in0=SW["kk"][:], in1=SW["kk"][:], op=ALU.mult)
        fl = lambda t: t[:].rearrange("p h b -> p (h b)")
        OP("pe", "matmul", ["onesf", "sw_kk2"], [("pmm", 0)], pmm[0][0:64, 0:64], lhsT=onesf[:], rhs=fl(SW["kk2"]),
           start=True, stop=True)
        OP("dve", "tensor_scalar_max", [("pmm", 0)], ["sw_rn"], out=fl(SW["rn"]), in0=pmm[0][0:64, 0:64], scalar1=1e-24)
        OP("act", "activation", ["sw_rn"], ["sw_rn"], out=SW["rn"][:], in_=SW["rn"][:], func=AF.Ln)
        OP("act", "activation", ["sw_rn"], ["sw_rn"], out=SW["rn"][:], in_=SW["rn"][:], func=AF.Exp, scale=-0.5)
        OP("dve", "tensor_tensor", ["sw_kk", "sw_rn"], ["sw_kk"], out=SW["kk"][:], in0=SW["kk"][:], in1=SW["rn"][:],
           op=ALU.mult)
        OP("dve", "tensor_scalar", ["sw_kk"], ["vec5"], vec5[:, :, :, 0], SW["kk"][:], -1.0, None, ALU.mult)
        OP("dve", "tensor_tensor", ["sw_kk", "sw_a", "vec5"], ["vec5"], out=vec5[:, :, :, 2], in0=SW["kk"][:],
           in1=SW["a"][:], op=ALU.mult)
        OP("dve", "tensor_scalar", ["sw_a"], ["sw_t1"], SW["t1"][:], SW["a"][:], -1.0, None, ALU.add)
        OP("dve", "tensor_tensor", ["sw_t1", "col_ka"], ["sw_t1"], out=SW["t1"][:], in0=SW["t1"][:], in1=bc16(colv["ka"]),
           op=ALU.mult)
        OP("dve", "scalar_tensor_tensor", ["sw_t1", "xc_k"], ["sw_keff"], out=SW["keff"][:], in0=SW["t1"][:], scalar=1.0,
           in1=XC["k"][:], op0=ALU.add, op1=ALU.mult)
        OP("dve", "tensor_copy", ["sw_keff", "vec5"], ["vec5"], out=vec5[:, :, :, 3], in_=SW["keff"][:])
        OP("dve", "tensor_copy", ["xc_r", "vec5"], ["vec5"], out=vec5[:, :, :, 4], in_=XC["r"][:])
        OP("dve", "tensor_tensor", ["xc_r", "sw_keff"], ["sw_rk"], out=SW["rk"][:], in0=XC["r"][:], in1=SW["keff"][:],
           op=ALU.mult)
        OP("dve", "tensor_tensor", ["sw_rk", "col_rk"], ["sw_rk"], out=SW["rk"][:], in0=SW["rk"][:], in1=bc16(colv["rk"]),
           op=ALU.mult)
        OP("pe", "matmul", ["onesf", "sw_rk"], [("pmm", 1)], pmm[1][0:64, 0:64], lhsT=onesf[:], rhs=fl(SW["rk"]),
           start=True, stop=True)
        OP("dve", "tensor_tensor", [("pmm", 1), "xc_v"], ["sw_bon"], out=fl(SW["bon"]), in0=pmm[1][0:64, 0:64],
           in1=fl(XC["v"]), op=ALU.mult)
        Sin = [sbC("Sin%d" % i, [64, 16, 64]) for i in range(2)]
        Sout = [sbC("Sout%d" % i, [64, 16, 64]) for i in range(2)]
        D5 = [sbC("D5_%d" % i, [64, 5, 64]) for i in range(2)]
        tm1 = sbC("tm1", [64, 64])
        sa = sbC("sa_s", [64, 1])
        k5 = 0
        for bi in range(4):
            sb_ = bi % 2
            DMA("sp", Sin[sb_][:], swkv_in[bi].rearrange("h i j -> i h j"), [], [("Sin", sb_)])
            for h in range(16):
                d5 = D5[k5 % 2]
                dk = ("D5", k5 % 2)
                pbk = 3 + (k5 % 2)
                pBC = pmm[pbk][0:64, 0:320].rearrange("p (v j) -> p v j", j=64)
                k5 += 1
                OP("dve", "tensor_tensor", ["identf", "vec5"], [dk], out=d5[:],
                   in0=identf[:].unsqueeze(1).broadcast_to([64, 5, 64]),
                   in1=vec5[:, h, bi, :].unsqueeze(2).broadcast_to([64, 5, 64]), op=ALU.mult)
                OP("pe", "matmul", ["onesf", dk], [("pmm", pbk)], pmm[pbk][0:64, 0:320], lhsT=onesf[:],
                   rhs=d5[:].rearrange("p v j -> p (v j)"), start=True, stop=True)
                S_ = Sin[sb_][:, h, :]
                So = Sout[sb_][:, h, :]
                OP("dve", "tensor_tensor", [("Sin", sb_), ("pmm", pbk)], ["tm1"], out=tm1[:], in0=pBC[:, 0, :], in1=S_,
                   op=ALU.mult)
                OP("dve", "tensor_reduce", ["tm1"], ["sa_s"], out=sa[:], in_=tm1[:], axis=AX.X, op=ALU.add)
                OP("dve", "tensor_tensor", [("Sin", sb_), ("pmm", pbk)], [("Sout", sb_)], out=So, in0=pBC[:, 1, :], in1=S_,
                   op=ALU.mult)
                OP("dve", "scalar_tensor_tensor", [("pmm", pbk), "sa_s", ("Sout", sb_)], [("Sout", sb_)], out=So,
                   in0=pBC[:, 2, :], scalar=sa[:, 0:1], in1=So, op0=ALU.mult, op1=ALU.add)
                OP("dve", "scalar_tensor_tensor", [("pmm", pbk), "xc_v", ("Sout", sb_)], [("Sout", sb_)], out=So,
                   in0=pBC[:, 3, :], scalar=XC["v"][:, h, bi:bi + 1], in1=So, op0=ALU.mult, op1=ALU.add)
                OP("dve", "tensor_tensor", [("pmm", pbk), ("Sout", sb_)], ["tm1"], out=tm1[:], in0=pBC[:, 4, :], in1=So,
                   op=ALU.mult)
                OP("dve", "tensor_reduce", ["tm1"], ["sw_y"], out=SW["y"][:, h, bi:bi + 1], in_=tm1[:], axis=AX.X,
                   op=ALU.add)
            DMA("pool", swkv_out[bi].rearrange("h i j -> i h j"), Sout[sb_][:], [("Sout", sb_)], [("swkv", bi)])
        OP("pe", "matmul", ["onesf", "sw_y"], [("pmm", 0)], pmm[0][0:64, 0:64], lhsT=onesf[:], rhs=fl(SW["y"]),
           start=True, stop=True)
        OP("dve", "tensor_scalar", [("pmm", 0)], ["sw_mean"], fl(SW["mean"]), pmm[0][0:64, 0:64], 1.0 / 64, None, ALU.mult)
        OP("dve", "tensor_tensor", ["sw_y", "sw_mean"], ["sw_yn"], out=SW["yn"][:], in0=SW["y"][:], in1=SW["mean"][:],
           op=ALU.subtract)
        OP("dve", "tensor_tensor", ["sw_yn"], ["sw_y2"], out=SW["y2"][:], in0=SW["yn"][:], in1=SW["yn"][:], op=ALU.mult)
        OP("pe", "matmul", ["onesf", "sw_y2"], [("pmm", 1)], pmm[1][0:64, 0:64], lhsT=onesf[:], rhs=fl(SW["y2"]),
           start=True, stop=True)
        OP("act", "activation", [("pmm", 1), "epsl"], ["sw_var"], out=fl(SW["var"]), in_=pmm[1][0:64, 0:64], func=AF.Ln,
           scale=1.0 / 64, bias=epsl[0:64, 0:1])
        OP("act", "activation", ["sw_var"], ["sw_var"], out=SW["var"][:], in_=SW["var"][:], func=AF.Exp, scale=-0.5)
        OP("dve", "tensor_tensor", ["sw_yn", "sw_var"], ["sw_yn"], out=SW["yn"][:], in0=SW["yn"][:], in1=SW["var"][:],
           op=ALU.mult)
        OP("dve", "tensor_tensor", ["sw_yn", "lnwc"], ["sw_yn"], out=SW["yn"][:], in0=SW["yn"][:], in1=bc16(lnwc),
           op=ALU.mult)
        OP("dve", "tensor_tensor", ["sw_yn", "lnbc"], ["sw_yn"], out=SW["yn"][:], in0=SW["yn"][:], in1=bc16(lnbc),
           op=ALU.add)
        OP("dve", "tensor_tensor", ["sw_yn", "sw_bon"], ["sw_yn"], out=SW["yn"][:], in0=SW["yn"][:], in1=SW["bon"][:],
           op=ALU.add)
        OP("dve", "tensor_tensor", ["sw_yn", "sw_g"], ["ob16"], out=ob16[:], in0=SW["yn"][:], in1=SW["g"][:], op=ALU.mult)
        for bi in range(4):
            DMA("sp", O_scr[1152 + bi, 1024:2048].rearrange("(h p) -> p h", p=64), ob16[:, :, bi], ["ob16"],
                ["Oscr_samp_rw"], allow_slow_non_contiguous=True)
        P.barrier()
        stC.close()
        mark("C_done")
        NT2 = 1280
        stD = contextlib.ExitStack()
        def sbD(name, shape, dt=F32):
            return stD.enter_context(nc.sbuf_tensor(name, list(shape), dt))
        OT = sbD("OT", [128, 16, NT2], BF16)
        otile = [sbD("otile%d" % i, [128, 2048], BF16) for i in range(2)]
        zt = sbD("zt", [128, 2048], BF16)
        OP("pool", "memset", [], ["zt"], zt[:], 0.0)
        if not HAVE_SAMPLE:
            DMA("sp", O_scr[1152:1280, :], zt[:], ["zt"], ["Oscr_samp"])
        else:
            DMA("sp", O_scr[1156:1280, :], zt[0:124, :], ["zt"], ["Oscr_samp_pad"])
        okeys = ["Oscr_att", "Oscr_samp", "Oscr_samp_pad", "Oscr_samp_rw"] + [("Oscr_rw", i) for i in range(8)]
        for i in range(10):
            ob = i % 2
            DMA("sp", otile[ob][:], O_scr[i * 128:(i + 1) * 128, :], okeys, [("otile", ob)])
            for g4 in range(4):
                pb = g4 % 2
                for j in range(4):
                    kc = g4 * 4 + j
                    OP("pe", "transpose", [("otile", ob), "ident"], ["pT%d" % pb], out=pT[pb][:, j, :],
                       in_=otile[ob][:, kc * 128:(kc + 1) * 128], identity=ident[:])
                OP("dve", "tensor_copy", ["pT%d" % pb], [("OT", i)], out=OT[:, g4 * 4:(g4 + 1) * 4, i * 128:(i + 1) * 128],
                   in_=pT[pb])
        wbf_d = [sbD("wbf_d%d" % i, [128, 16, 256], BF16) for i in range(3)]
        xres = [sbD("xres%d" % i, [128, 256]) for i in range(3)]
        xmo = [sbD("xmo%d" % i, [128, 256]) for i in range(3)]
        cnt = {"i": 0}
        tile_row0 = lambda i: (896 + i * 128) if i < 9 else 2048
        for sl in range(8):
            c0 = sl * 256
            b = sl % 3
            for half in range(2):
                DMA("pool", wbf_d[b][:, half * 8:(half + 1) * 8, :],
                    w_o[half * 1024:(half + 1) * 1024, c0:c0 + 256].rearrange("(k p) c -> p k c", p=128),
                    [], [("wbf_d", b, half)])
            for i in range(10):
                pi = cnt["i"] % 4
                xi = cnt["i"] % 3
                cnt["i"] += 1
                if cnt["i"] == 1:
                    for pf in range(2):
                        r0 = tile_row0(pf)
                        DMA("sp", xres[pf][:], xin[r0:r0 + 128, 0:256], [], [("xres", pf)])
                nxt = cnt["i"] + 1
                if nxt < 80:
                    r0 = tile_row0(nxt % 10)
                    cn = (nxt // 10) * 256
                    DMA("sp", xres[nxt % 3][:], xin[r0:r0 + 128, cn:cn + 256], [], [("xres", nxt % 3)])
                for kc in range(16):
                    OP("pe", "matmul", [("wbf_d", b, 0), ("wbf_d", b, 1), ("OT", i)], [("pmm", pi)], pmm[pi][:, 0:256],
                       lhsT=OT[:, kc, i * 128:(i + 1) * 128], rhs=wbf_d[b][:, kc, :], start=(kc == 0), stop=(kc == 15))
                OP("dve", "tensor_tensor", [("pmm", pi), ("xres", xi)], [("xmo", xi)], out=xmo[xi][:],
                   in0=pmm[pi][:, 0:256], in1=xres[xi][:], op=ALU.add)
                DMA("sp", XM[i * 128:(i + 1) * 128, c0:c0 + 256], xmo[xi][:], [("xmo", xi)], [("XM", i)])
        P.barrier()
        stD.close()
        mark("D_done")
        stE = contextlib.ExitStack()
        def sbE(name, shape, dt=F32):
            return stE.enter_context(nc.sbuf_tensor(name, list(shape), dt))
        aT = sbE("aT", [128, 44, 1032], BF16)
        OP("pool", "memset", [], ["aT_pad"], aT[:, :, 1028:1032], 0.0)
        gcol2 = sbE("gcol2", [128, 16])
        DMA("sp", gcol2[:], g_ffn.rearrange("(k p) -> p k", p=128), [], ["gcol2"], allow_slow_non_contiguous=True)
        cvec = {}
        for nm, apd in (("w0", conv_w[0, :]), ("w1", conv_w[1, :]), ("w2", conv_w[2, :]), ("b", conv_b)):
            tcv = sbE("cv_" + nm, [128, 88])
            DMA("sp", tcv[:], apd.rearrange("(j p) -> p j", p=128), [], ["cv_" + nm], allow_slow_non_contiguous=True)
            cvec[nm] = tcv
        SF = sbE("SF", [128, 88, 8])
        DMA("sp", SF[:], sffnT.rearrange("(j p) r b -> p j (r b)", p=128), [], ["SF"])
        Ulast = sbE("Ulast", [128, 88, 2])
        Usamp = sbE("Usamp", [128, 88, 4])
        stE1 = contextlib.ExitStack()
        def sbE1(name, shape, dt=F32):
            return stE1.enter_context(nc.sbuf_tensor(name, list(shape), dt))
        hT2 = sbE1("hT2", [128, 16, NT2], BF16)
        stE0 = contextlib.ExitStack()
        xt = [stE0.enter_context(nc.sbuf_tensor("xt_e%d" % i, [128, D], F32)) for i in range(2)]
        xn = [stE0.enter_context(nc.sbuf_tensor("xn_e%d" % i, [128, D], BF16)) for i in range(2)]
        for i in range(10):
            norm_transpose(XM[i * 128:(i + 1) * 128, :], gcol2, "gcol2", hT2[:, :, i * 128:(i + 1) * 128], ("hT2", i), i,
                           src_keys=[("XM", i)])
        P.barrier()
        stE0.close()
        hT2_all = [("hT2", i) for i in range(10)]
        wu_bf = [[sbE1("wu_bf%d_%d" % (i, j), [128, 16, 256], BF16) for j in range(2)] for i in range(3)]
        U = [sbE1("U%d" % i, [128, 1154]) for i in range(2)]
        Us = [sbE1("Us%d" % i, [128, 4]) for i in range(2)]
        cg = sbE1("cg", [128, 1024])
        cv = sbE1("cv", [128, 1024])
        cgs = sbE1("cgs", [128, 4])
        cvs = sbE1("cvs", [128, 4])
        ecnt = {"pm": 0}
        for j in range(44):
            jb = (j // 2) % 3
            jo = (j % 2) * 128
            for gv in range(2):
                col0 = gv * DFF + j * 128
                jj = gv * 44 + j
                if j % 2 == 0:
                    for half in range(2):
                        DMA("pool", wu_bf[jb][gv][:, half * 8:(half + 1) * 8, :],
                            w_up[half * 1024:(half + 1) * 1024, col0:col0 + 256].rearrange("(k p) c -> p k c", p=128),
                            [], [("wu_bf", jb, gv, half)])
                for (t0, n) in [(126, 344), (470, 343), (813, 343)]:
                    pi = ecnt["pm"] % 4
                    ecnt["pm"] += 1
                    for kc in range(16):
                        OP("pe", "matmul", [("wu_bf", jb, gv, 0), ("wu_bf", jb, gv, 1)] + hT2_all, [("pmm", pi)],
                           pmm[pi][:, 0:n], lhsT=wu_bf[jb][gv][:, kc, jo:jo + 128], rhs=hT2[:, kc, t0:t0 + n],
                           start=(kc == 0), stop=(kc == 15))
                    if t0 == 813:
                        OP("act", "activation", [("pmm", pi)], [("U", gv)], out=U[gv][:, 813:1152], in_=pmm[pi][:, 0:339],
                           func=AF.Copy)
                        OP("act", "activation", [("pmm", pi)], [("Us", gv)], out=Us[gv][:, :], in_=pmm[pi][:, 339:343],
                           func=AF.Copy)
                    else:
                        OP("act", "activation", [("pmm", pi)], [("U", gv)], out=U[gv][:, t0:t0 + n], in_=pmm[pi][:, 0:n],
                           func=AF.Copy)
                    if t0 == 126:
                        OP("dve", "tensor_scalar", [("U", gv), "flag"], [("U", gv)], U[gv][:, 126:128], U[gv][:, 126:128],
                           flag[:, 0:1], None, ALU.mult)
                dst = cg if gv == 0 else cv
                dk = "cg" if gv == 0 else "cv"
                OP("dve", "tensor_scalar", [("U", gv), "cv_w2", "cv_b"], [dk], dst[:], U[gv][:, 128:1152],
                   cvec["w2"][:, jj:jj + 1], cvec["b"][:, jj:jj + 1], ALU.mult, ALU.add)
                OP("dve", "scalar_tensor_tensor", [("U", gv), "cv_w1", dk], [dk], out=dst[:], in0=U[gv][:, 127:1151],
                   scalar=cvec["w1"][:, jj:jj + 1], in1=dst[:], op0=ALU.mult, op1=ALU.add)
                OP("dve", "scalar_tensor_tensor", [("U", gv), "cv_w0", dk], [dk], out=dst[:], in0=U[gv][:, 126:1150],
                   scalar=cvec["w0"][:, jj:jj + 1], in1=dst[:], op0=ALU.mult, op1=ALU.add)
                OP("act", "activation", [("U", gv)], ["Ulast"], out=Ulast[:, jj, :], in_=U[gv][:, 1150:1152], func=AF.Copy)
                dsts = cgs if gv == 0 else cvs
                dks = "cgs" if gv == 0 else "cvs"
                OP("dve", "tensor_scalar", [("Us", gv), "cv_w2", "cv_b"], [dks], dsts[:], Us[gv][:, :],
                   cvec["w2"][:, jj:jj + 1], cvec["b"][:, jj:jj + 1], ALU.mult, ALU.add)
                OP("dve", "scalar_tensor_tensor", ["SF", "cv_w1", dks], [dks], out=dsts[:], in0=SF[:, jj, 4:8],
                   scalar=cvec["w1"][:, jj:jj + 1], in1=dsts[:], op0=ALU.mult, op1=ALU.add)
                OP("dve", "scalar_tensor_tensor", ["SF", "cv_w0", dks], [dks], out=dsts[:], in0=SF[:, jj, 0:4],
                   scalar=cvec["w0"][:, jj:jj + 1], in1=dsts[:], op0=ALU.mult, op1=ALU.add)
                OP("act", "activation", [("Us", gv)], ["Usamp"], out=Usamp[:, jj, :], in_=Us[gv][:, :], func=AF.Copy)
            OP("act", "activation", ["cg"], ["cg"], out=cg[:], in_=cg[:], func=AF.Silu)
            OP("dve", "tensor_tensor", ["cg", "cv"], [("aT", j)], out=aT[:, j, 0:1024], in0=cg[:], in1=cv[:], op=ALU.mult)
            OP("act", "activation", ["cgs"], ["cgs"], out=cgs[:], in_=cgs[:], func=AF.Silu)
            OP("dve", "tensor_tensor", ["cgs", "cvs"], [("aT", j)], out=aT[:, j, 1024:1028], in0=cgs[:], in1=cvs[:],
               op=ALU.mult)
        for r in range(2):
            DMA("sp", pffn_out[r, :].rearrange("(j p) -> p j", p=128), Ulast[:, :, r], ["Ulast"], [("pffn", r)],
                allow_slow_non_contiguous=True)
        DMA("sp", sffn_outT.rearrange("(j p) b -> p j b", p=128), Usamp[:], ["Usamp"], ["sffn"])
        DMA("pool", sffn_row0[:, :], sffn_in1[:, :], [], ["sffn0"])
        P.barrier()
        stE1.close()
        mark("E_up_done")
        wd_bf = [sbE("wd_bf%d" % i, [128, 44, 256], BF16) for i in range(2)]
        xres2 = [sbE("xres2_%d" % i, [128, 256]) for i in range(3)]
        xo = [sbE("xo%d" % i, [128, 256]) for i in range(3)]
        aT_all = [("aT", j) for j in range(44)] + ["aT_pad"]
        cnt = {"i": 0}
        for sl in range(8):
            c0 = sl * 256
            b = sl % 2
            for g in range(4):
                DMA("pool", wd_bf[b][:, g * 11:(g + 1) * 11, :],
                    w_down[g * 1408:(g + 1) * 1408, c0:c0 + 256].rearrange("(k p) c -> p k c", p=128),
                    [], [("wd_bf", b, g)])
            wk = [("wd_bf", b, g) for g in range(4)]
            for i in range(9):
                pi = cnt["i"] % 4
                xi = cnt["i"] % 3
                cnt["i"] += 1
                if cnt["i"] == 1:
                    for pf in range(2):
                        DMA("sp", xres2[pf][:], XM[(pf + 1) * 128:(pf + 2) * 128, 0:256], [("XM", pf + 1)], [("xres2", pf)])
                nxt = cnt["i"] + 1
                if nxt < 72:
                    ti = nxt % 9
                    cn = (nxt // 9) * 256
                    DMA("sp", xres2[nxt % 3][:], XM[(ti + 1) * 128:(ti + 2) * 128, cn:cn + 256], [("XM", ti + 1)],
                        [("xres2", nxt % 3)])
                mrows = 128 if i < 8 else 8
                for kc in range(44):
                    OP("pe", "matmul", wk + aT_all, [("pmm", pi)], pmm[pi][0:mrows, 0:256],
                       lhsT=aT[:, kc, i * 128:i * 128 + mrows], rhs=wd_bf[b][:, kc, :], start=(kc == 0), stop=(kc == 43))
                OP("dve", "tensor_tensor", [("pmm", pi), ("xres2", xi)], [("xo", xi)], out=xo[xi][:],
                   in0=pmm[pi][:, 0:256], in1=xres2[xi][:], op=ALU.add)
                DMA("sp", XO[i * 128:(i + 1) * 128, c0:c0 + 256], xo[xi][:], [("xo", xi)], [("XO", i)])
        P.barrier()
        stE.close()
        mark("E_down_done")
        stF = contextlib.ExitStack()
        def sbF(name, shape, dt=F32):
            return stF.enter_context(nc.sbuf_tensor(name, list(shape), dt))
        gfin = sbF("gfin", [128, D])
        DMA("pool", gfin[:], g_fin[0:1, :].partition_broadcast(128), [], ["gfin"])
        xf = [sbF("xf%d" % i, [128, D]) for i in range(2)]
        yf = [sbF("yf%d" % i, [128, D]) for i in range(2)]
        for i in range(9):
            b = i % 2
            DMA("sp", xf[b][:], XO[i * 128:(i + 1) * 128, :], [("XO", i)], [("xf", b)])
            OP("act", "activation", [("xf", b)], [("yf", b), "ss%d" % b], out=yf[b][:], in_=xf[b][:], func=AF.Square,
               accum_out=ss[b][:])
            OP("act", "activation", ["ss%d" % b, "epst"], ["rstd%d" % b], out=rstd[b][:], in_=ss[b][:], func=AF.Ln,
               scale=1.0 / D, bias=epst[:, 0:1])
            OP("act", "activation", ["rstd%d" % b], ["rstd%d" % b], out=rstd[b][:], in_=rstd[b][:], func=AF.Exp,
               scale=-0.5)
            OP("dve", "scalar_tensor_tensor", [("xf", b), "rstd%d" % b, "gfin"], [("yf", b)], out=yf[b][:], in0=xf[b][:],
               scalar=rstd[b][:, 0:1], in1=gfin[:], op0=ALU.mult, op1=ALU.mult)
            if i < 8:
                DMA("pool", y_out[i * 128:(i + 1) * 128, :], yf[b][:], [("yf", b)], [("y", i)])
            else:
                DMA("pool", ys_out[0:4, :], yf[b][0:4, :], [("yf", b)], [("y", i)])
        P.barrier()
        stF.close()
        mark("end")
        P.emit()
    return nc


_NC_CACHE = {}


def _get_nc():
    if "nc" not in _NC_CACHE:
        _NC_CACHE["nc"] = build()
    return _NC_CACHE["nc"]


def _count_mask():
    m = np.zeros((128, 9, 512), np.float32)
    s_idx = np.arange(128)[:, None]
    t_idx = np.arange(512)[None, :]
    for i in range(9):
        d0 = -384 + 128 * i if i < 8 else 1024
        d = d0 + t_idx - s_idx
        cnt = ((d >= 0) & (d <= 128)).astype(np.float32)
        cnt += ((d >= 0) & (d <= 512) & (d % 4 == 0))
        cnt += ((d >= 0) & (d <= 2048) & (d % 16 == 0))
        m[:, i, :] = cnt
    return m.astype(ml_dtypes.bfloat16)


def _tri_masks():
    m = np.zeros((64, 320), np.float32)
    a = np.arange(64)
    m[:, 0:64] = (a[:, None] < a[None, :])
    m[:, 64:128] = (a[:, None] <= a[None, :])
    m[:, 128:192] = 1.0
    m[:, 192:256] = (a[None, :] < a[:, None])
    m[:, 256:320] = 1.0
    return m.astype(ml_dtypes.bfloat16)


def kernel(**inp):
    f32 = np.float32
    x_prompt = np.asarray(inp["x_prompt"], f32)
    x_sample = np.asarray(inp["x_sample"], f32)
    ident = np.eye(128).astype(ml_dtypes.bfloat16)
    masks = _count_mask()
    A = lambda k: np.ascontiguousarray(inp[k][0], dtype=f32)
    shared = {
        "ident": ident, "masks": masks,
        "w_in": A("w_in"), "norm_mix_g": A("norm_mix_g"), "att_out_g": A("att_out_g").reshape(1, 1024),
        "rw_mu": A("rw_mu"), "trim": _tri_masks(),
        "rw_w0": A("rw_w0"), "rw_a0": A("rw_a0"), "rw_k_k": A("rw_k_k"), "rw_k_a": A("rw_k_a"), "rw_r_k": A("rw_r_k"),
        "rw_w_up": A("rw_w_up"), "rw_a_up": A("rw_a_up"), "rw_g_up": A("rw_g_up"),
        "rw_lnx_w": A("rw_lnx_w").reshape(1, 1024), "rw_lnx_b": A("rw_lnx_b").reshape(1, 1024),
        "w_o": A("w_o"), "norm_ffn_g": A("norm_ffn_g"), "ffn_w_up": A("ffn_w_up"), "ffn_conv_w": A("ffn_conv_w"),
        "ffn_conv_b": A("ffn_conv_b"), "ffn_w_down": A("ffn_w_down"),
        "norm_final_g": np.ascontiguousarray(inp["norm_final_g"], dtype=f32).reshape(1, D),
        "rw_lnx_w_c": A("rw_lnx_w"), "rw_lnx_b_c": A("rw_lnx_b"),
    }
    in_maps = []
    for core in range(8):
        b, half = core // 2, core % 2
        xin = np.zeros((NTOK, D), f32)
        if half == 1:
            xin[0:1024] = x_prompt[b, 0:1024]
        xin[1024:2048] = x_prompt[b, half * 1024:(half + 1) * 1024]
        xin[2048:2052] = x_sample[4 * core:4 * core + 4, 0]
        m = dict(shared)
        m["xin"] = xin
        m["flag"] = np.full((128, 1), float(half), f32)
        sf = np.asarray(inp["state_ffn_conv"][0, 4 * core:4 * core + 4], f32)
        m["sffnT"] = np.ascontiguousarray(sf.transpose(2, 1, 0))
        m["cache_k"] = np.ascontiguousarray(inp["cache_att_k"][0, 4 * core:4 * core + 4], dtype=f32).reshape(4, 2048, 1024)
        m["cache_v"] = np.ascontiguousarray(inp["cache_att_v"][0, 4 * core:4 * core + 4], dtype=f32).reshape(4, 2048, 1024)
        m["swkv_in"] = np.ascontiguousarray(inp["state_rwkv_wkv"][0, 4 * core:4 * core + 4], dtype=f32)
        m["sffn_in1"] = np.ascontiguousarray(sf[:, 1, :])
        m["sshiftT"] = np.ascontiguousarray(inp["state_rwkv_shift"][0, 4 * core:4 * core + 4, 0, :].T, dtype=f32)
        in_maps.append(m)
    nc = _get_nc()
    res = run_bass_kernel_spmd(nc, in_maps, core_ids=list(range(8)))
    R = res.results
    _NC_CACHE["R"] = R
    pk = np.zeros((1, 4, 2048, 16, 64), f32)
    pv = np.zeros((1, 4, 2048, 16, 64), f32)
    sk = np.zeros((1, 32, 1, 16, 64), f32)
    sv = np.zeros((1, 32, 1, 16, 64), f32)
    pshift = np.zeros((1, 4, 1, C_SH), f32)
    pwkv = np.zeros((1, 4, 16, 64, 64), f32)
    yp = np.zeros((4, 2048, D), f32)
    ysm = np.zeros((32, 1, D), f32)
    pffn = np.zeros((1, 4, 2, 2 * DFF), f32)
    sffn = np.zeros((1, 32, 2, 2 * DFF), f32)
    swkv = np.zeros((1, 32, 16, 64, 64), f32)
    sshift = np.zeros((1, 32, 1, C_SH), f32)
    for core in range(8):
        b, half = core // 2, core % 2
        pk[0, b, half * 1024:(half + 1) * 1024] = R[core]["k_out"].reshape(1024, 16, 64)
        pv[0, b, half * 1024:(half + 1) * 1024] = R[core]["v_out"].reshape(1024, 16, 64)
        sk[0, 4 * core:4 * core + 4, 0] = R[core]["sk_out"].reshape(4, 16, 64)
        sv[0, 4 * core:4 * core + 4, 0] = R[core]["sv_out"].reshape(4, 16, 64)
        sshift[0, 4 * core:4 * core + 4, 0] = R[core]["sshift_outT"].T
        yp[b, half * 1024:(half + 1) * 1024] = R[core]["y_out"]
        ysm[4 * core:4 * core + 4, 0] = R[core]["ys_out"]
        swkv[0, 4 * core:4 * core + 4] = R[core]["swkv_out"]
        sffn[0, 4 * core:4 * core + 4, 0] = R[core]["sffn_row0"]
        sffn[0, 4 * core:4 * core + 4, 1] = R[core]["sffn_outT"].T
        if half == 1:
            pshift[0, b, 0] = R[core]["pshift_out"][:, 0]
            pwkv[0, b] = R[core]["pwkv_out"]
            pffn[0, b] = R[core]["pffn_out"]
    z = lambda *s: np.zeros(s, f32)
    return (yp, ysm, pk, pv, pshift, pwkv, pffn, sk, sv, sshift, swkv, sffn)
```
